# Optimizing a Trainium2 kernel written in Bass

```python
import math
import jax
import jax.numpy as jnp
from jax import lax
import numpy as np

D_MODEL = 1024
BATCH = 8
SEQ = 4096
DEPTH = 2

CHUNK = 64
MEM_TOKENS = 256
HEAD_DIM = 64
D_MEM = D_MODEL // 4
N_MEM_HEADS = 4
MEM_HEAD_DIM = D_MEM // N_MEM_HEADS
D_TOK = D_MODEL - D_MEM
S5_GROUP = 16
S5_GROUPS = D_TOK // S5_GROUP
S5_STATE = 64
N_FOX_HEADS = D_TOK // HEAD_DIM
Q_BLOCK = 128
D_FF = ((8 * D_MODEL // 3 + 127) // 128) * 128
CONV_W = 3
N_A_LAYERS = DEPTH // 2
N_B_LAYERS = DEPTH - N_A_LAYERS
EPS = 1e-6

kernel_name = "hybrid_s5_fox_yoco_encoder"


def rmsnorm(x, g):
    xf = x.astype(jnp.float32)
    y = xf * lax.rsqrt(jnp.mean(xf * xf, axis=-1, keepdims=True) + EPS)
    return (y * g.astype(jnp.float32)).astype(x.dtype)


def s5_mixer(u, a_re, a_im, log_dt, b_re, b_im, c_re, c_im, d_skip):
    f32 = jnp.float32
    bsz, seq, _ = u.shape
    n_chunks = seq // CHUNK
    uf = u.astype(f32)
    u_c = uf.reshape(bsz, n_chunks, CHUNK, S5_GROUPS, S5_GROUP).transpose(1, 2, 0, 3, 4)
    dt = jnp.exp(log_dt.astype(f32))[:, None]
    lam_re = jnp.minimum(a_re.astype(f32), -1e-4)
    lam_im = a_im.astype(f32)
    mag = jnp.exp(lam_re * dt)
    ph = lam_im * dt
    ab_re, ab_im = mag * jnp.cos(ph), mag * jnp.sin(ph)
    den = lam_re * lam_re + lam_im * lam_im
    z_re = ((ab_re - 1.0) * lam_re + ab_im * lam_im) / den
    z_im = (ab_im * lam_re - (ab_re - 1.0) * lam_im) / den
    br, bi = b_re.astype(f32), b_im.astype(f32)
    bb_re = z_re[..., None] * br - z_im[..., None] * bi
    bb_im = z_re[..., None] * bi + z_im[..., None] * br
    steps = jnp.arange(1, CHUNK + 1, dtype=f32)[:, None, None]
    pw_mag = jnp.exp(lam_re * dt * steps)
    pw_ph = lam_im * dt * steps
    pw_re, pw_im = pw_mag * jnp.cos(pw_ph), pw_mag * jnp.sin(pw_ph)
    pw_re, pw_im = pw_re[:, None], pw_im[:, None]
    a_el_re = jnp.broadcast_to(ab_re, (CHUNK, bsz, S5_GROUPS, S5_STATE))
    a_el_im = jnp.broadcast_to(ab_im, (CHUNK, bsz, S5_GROUPS, S5_STATE))
    cr, ci = c_re.astype(f32), c_im.astype(f32)

    def combine(e1, e2):
        a1r, a1i, b1r, b1i = e1
        a2r, a2i, b2r, b2i = e2
        return (a2r * a1r - a2i * a1i, a2r * a1i + a2i * a1r,
                a2r * b1r - a2i * b1i + b2r, a2r * b1i + a2i * b1r + b2i)

    def chunk_step(carry, uc):
        hp_re, hp_im = carry
        bu_re = jnp.einsum('lbgi,gpi->lbgp', uc, bb_re)
        bu_im = jnp.einsum('lbgi,gpi->lbgp', uc, bb_im)
        _, _, h_re, h_im = lax.associative_scan(combine, (a_el_re, a_el_im, bu_re, bu_im), axis=0)
        h_re = h_re + pw_re * hp_re[None] - pw_im * hp_im[None]
        h_im = h_im + pw_re * hp_im[None] + pw_im * hp_re[None]
        y = jnp.einsum('lbgp,gip->lbgi', h_re, cr) - jnp.einsum('lbgp,gip->lbgi', h_im, ci)
        return (h_re[-1], h_im[-1]), y

    h0 = jnp.zeros((bsz, S5_GROUPS, S5_STATE), f32)
    _, y = lax.scan(chunk_step, (h0, h0), u_c)
    y = y.transpose(2, 0, 1, 3, 4).reshape(bsz, seq, D_TOK) + d_skip.astype(f32) * uf
    return y.astype(u.dtype)


def memory_attention(q, mem_n, w_mem_kv):
    bsz, seq, _ = q.shape
    kv = mem_n @ w_mem_kv
    k, v = jnp.split(kv, 2, axis=-1)
    k = k.reshape(bsz, MEM_TOKENS, N_MEM_HEADS, MEM_HEAD_DIM)
    v = v.reshape(bsz, MEM_TOKENS, N_MEM_HEADS, MEM_HEAD_DIM)
    qh = q.reshape(bsz, seq, N_MEM_HEADS, MEM_HEAD_DIM)
    s = jnp.einsum('bshd,bmhd->bhsm', qh, k).astype(jnp.float32) * (MEM_HEAD_DIM ** -0.5)
    p = jax.nn.softmax(s, axis=-1)
    o = jnp.einsum('bhsm,bmhd->bshd', p.astype(v.dtype), v)
    return o.reshape(bsz, seq, D_MEM)


def forgetting_attention(q, k, v, fcum):
    bsz, seq, _, _ = q.shape
    scale = HEAD_DIM ** -0.5
    outs = []
    for i in range(seq // Q_BLOCK):
        q0, q1 = i * Q_BLOCK, (i + 1) * Q_BLOCK
        qs = q[:, q0:q1]
        s = jnp.einsum('bqhd,bkhd->bhqk', qs, k[:, :q1]).astype(jnp.float32) * scale
        s = s + fcum[:, :, q0:q1, None] - fcum[:, :, None, :q1]
        qpos = q0 + jnp.arange(Q_BLOCK)
        kpos = jnp.arange(q1)
        s = jnp.where(qpos[:, None] >= kpos[None, :], s, -jnp.inf)
        p = jax.nn.softmax(s, axis=-1)
        outs.append(jnp.einsum('bhqk,bkhd->bqhd', p.astype(v.dtype), v[:, :q1]))
    return jnp.concatenate(outs, axis=1).reshape(bsz, seq, D_TOK)


def conv_ffn(x, w_up, conv_w, conv_b, w_down):
    seq = x.shape[1]
    up = x @ w_up
    a, g = jnp.split(up, 2, axis=-1)
    gp = jnp.pad(g, ((0, 0), (CONV_W - 1, 0), (0, 0)))
    gc = gp[:, 0:seq] * conv_w[0] + gp[:, 1:seq + 1] * conv_w[1] + gp[:, 2:seq + 2] * conv_w[2] + conv_b
    return (jax.nn.silu(gc) * a) @ w_down


def setup_inputs(seed: int = 0) -> dict:
    key = jax.random.key(seed)
    ks = jax.random.split(key, 32)
    f32 = jnp.float32
    nrm = lambda k, shape, s: jax.random.normal(k, shape, f32) * s
    n_idx = jnp.arange(S5_STATE, dtype=f32)
    return {
        "x": nrm(ks[0], (BATCH, SEQ, D_MODEL), 1.0),
        "mem": nrm(ks[1], (BATCH, MEM_TOKENS, D_MODEL), 1.0),
        "g_mix": 1.0 + nrm(ks[2], (DEPTH, D_MODEL), 0.02),
        "w_in": nrm(ks[3], (DEPTH, D_MODEL, D_MODEL), D_MODEL ** -0.5),
        "w_out": nrm(ks[4], (DEPTH, D_MODEL, D_MODEL), D_MODEL ** -0.5),
        "g_mem": 1.0 + nrm(ks[5], (D_MODEL,), 0.02),
        "w_mem_kv": nrm(ks[6], (DEPTH, D_MODEL, 2 * D_MEM), D_MODEL ** -0.5),
        "s5_a_re": -0.5 + nrm(ks[7], (N_A_LAYERS, S5_GROUPS, S5_STATE), 0.01),
        "s5_a_im": math.pi * n_idx + nrm(ks[8], (N_A_LAYERS, S5_GROUPS, S5_STATE), 0.01),
        "s5_log_dt": jax.random.uniform(ks[9], (N_A_LAYERS, S5_GROUPS), f32, math.log(1e-3), math.log(1e-1)),
        "s5_b_re": nrm(ks[10], (N_A_LAYERS, S5_GROUPS, S5_STATE, S5_GROUP), (2 * S5_GROUP) ** -0.5),
        "s5_b_im": nrm(ks[11], (N_A_LAYERS, S5_GROUPS, S5_STATE, S5_GROUP), (2 * S5_GROUP) ** -0.5),
        "s5_c_re": nrm(ks[12], (N_A_LAYERS, S5_GROUPS, S5_GROUP, S5_STATE), S5_STATE ** -0.5),
        "s5_c_im": nrm(ks[13], (N_A_LAYERS, S5_GROUPS, S5_GROUP, S5_STATE), S5_STATE ** -0.5),
        "s5_d": nrm(ks[14], (N_A_LAYERS, D_TOK), 1.0),
        "w_glu": nrm(ks[15], (N_A_LAYERS, D_TOK, D_TOK), D_TOK ** -0.5),
        "g_kv": 1.0 + nrm(ks[16], (D_MODEL,), 0.02),
        "w_kv": nrm(ks[17], (D_MODEL, 2 * D_TOK), D_MODEL ** -0.5),
        "w_fgate": nrm(ks[18], (D_MODEL, N_FOX_HEADS), 0.5 * D_MODEL ** -0.5),
        "b_fgate": jax.random.uniform(ks[19], (N_FOX_HEADS,), f32, 1.0, 6.0),
        "g_ffn": 1.0 + nrm(ks[20], (DEPTH, D_MODEL), 0.02),
        "w_ffn_up": nrm(ks[21], (DEPTH, D_MODEL, 2 * D_FF), D_MODEL ** -0.5),
        "conv_w": nrm(ks[22], (DEPTH, CONV_W, D_FF), CONV_W ** -0.5),
        "conv_b": nrm(ks[23], (DEPTH, D_FF), 0.02),
        "w_ffn_down": nrm(ks[24], (DEPTH, D_FF, D_MODEL), D_FF ** -0.5),
        "g_final": 1.0 + nrm(ks[25], (D_MODEL,), 0.02),
    }


def reference(x, mem, g_mix, w_in, w_out, g_mem, w_mem_kv, s5_a_re, s5_a_im, s5_log_dt,
              s5_b_re, s5_b_im, s5_c_re, s5_c_im, s5_d, w_glu, g_kv, w_kv, w_fgate, b_fgate,
              g_ffn, w_ffn_up, conv_w, conv_b, w_ffn_down, g_final):
    bsz, seq, _ = x.shape
    h = x
    mem_n = rmsnorm(mem, g_mem)
    k_sh = v_sh = fcum_sh = None
    for l in range(DEPTH):
        hn = rmsnorm(h, g_mix[l])
        proj = hn @ w_in[l]
        tok_in, q_mem = proj[..., :D_TOK], proj[..., D_TOK:]
        if l < N_A_LAYERS:
            y = s5_mixer(tok_in, s5_a_re[l], s5_a_im[l], s5_log_dt[l], s5_b_re[l], s5_b_im[l],
                         s5_c_re[l], s5_c_im[l], s5_d[l])
            g = jax.nn.gelu(y)
            tok_out = g * jax.nn.sigmoid(g @ w_glu[l])
        else:
            q = tok_in.reshape(bsz, seq, N_FOX_HEADS, HEAD_DIM)
            tok_out = forgetting_attention(q, k_sh, v_sh, fcum_sh)
        mem_out = memory_attention(q_mem, mem_n, w_mem_kv[l])
        h = h + jnp.concatenate([tok_out, mem_out], axis=-1) @ w_out[l]
        h = h + conv_ffn(rmsnorm(h, g_ffn[l]), w_ffn_up[l], conv_w[l], conv_b[l], w_ffn_down[l])
        if l == N_A_LAYERS - 1:
            hs = rmsnorm(h, g_kv)
            kv = hs @ w_kv
            k_sh = kv[..., :D_TOK].reshape(bsz, seq, N_FOX_HEADS, HEAD_DIM)
            v_sh = kv[..., D_TOK:].reshape(bsz, seq, N_FOX_HEADS, HEAD_DIM)
            logf = jax.nn.log_sigmoid((hs @ w_fgate + b_fgate).astype(jnp.float32))
            fcum_sh = jnp.cumsum(logf, axis=1).transpose(0, 2, 1)
    return rmsnorm(h, g_final)
```

```python
import numpy as np
import ml_dtypes
from contextlib import ExitStack
import concourse.bass as bass
import concourse.mybir as mybir
from concourse.bass_utils import run_bass_kernel_spmd

F32 = mybir.dt.float32
BF16 = mybir.dt.bfloat16
AF = mybir.ActivationFunctionType
ALU = mybir.AluOpType

D = 1024
DFF = 2816
NJ = 22
MEM = 256
EPS = 1e-6
FFN_PIPE = True


class _Op:
    __slots__ = ("stream", "fn", "is_dma", "idx", "slot", "cnt", "waits", "clock")


class Sched:
    STREAMS = ("pe", "act", "dve", "pool", "sp")

    def __init__(self, nc, n_dma_sems=40):
        self.nc = nc
        self.ops = []
        self.n_dma = n_dma_sems
        self.dma_rr = 0
        self.dma_cnt = [0] * n_dma_sems
        self.last_writer = {}
        self.readers = {}
        self.know = {s: {} for s in self.STREAMS}
        self.count = {s: 0 for s in self.STREAMS}
        self.pending = {s: [] for s in self.STREAMS}

    def barrier(self):
        tgt = {}
        for s in ("pe", "act", "dve", "pool"):
            if self.count[s] > 0:
                tgt[s] = self.count[s]
        for i in range(self.n_dma):
            if self.dma_cnt[i] > 0:
                tgt[("d", i)] = self.dma_cnt[i]
        for s in self.STREAMS:
            know = self.know[s]
            for k, v in tgt.items():
                if k == s == "pe":
                    continue
                if know.get(k, 0) < v:
                    know[k] = v
                    self.pending[s].append((k, v))
        self.last_writer = {k: w for k, w in self.last_writer.items() if isinstance(k, tuple) and k and k[0] == "dram"}
        self.readers = {k: r for k, r in self.readers.items() if k in self.last_writer}

    def add(self, stream, fn, reads=(), writes=(), dma=False):
        op = _Op()
        op.stream = stream
        op.fn = fn
        op.is_dma = dma
        reads = list(reads)
        writes = list(writes)
        deps = []
        for k in reads + writes:
            w = self.last_writer.get(k)
            if w is not None:
                deps.append(w)
        for k in writes:
            deps.extend(self.readers.get(k, ()))
        know = self.know[stream]
        waits = self.pending[stream]
        self.pending[stream] = []

        def need(key, val, clock):
            if know.get(key, 0) >= val:
                return
            waits.append((key, val))
            if clock is not None:
                for kk, vv in clock.items():
                    if know.get(kk, 0) < vv:
                        know[kk] = vv
            know[key] = max(know.get(key, 0), val)

        if dma:
            slot = self.dma_rr
            self.dma_rr = (self.dma_rr + 1) % self.n_dma
            prev = self.dma_cnt[slot]
            if prev > 0:
                need(("d", slot), prev, None)
            self.dma_cnt[slot] = prev + 1
            op.slot = slot
            op.cnt = prev + 1
        for p in deps:
            if p.is_dma:
                need(("d", p.slot), p.cnt, p.clock)
            else:
                if p.stream == "pe" and stream == "pe" and not dma:
                    continue
                need(p.stream, p.idx, p.clock)
        op.waits = waits
        op.clock = dict(know)
        if dma:
            op.idx = None
            op.clock[("d", op.slot)] = op.cnt
        else:
            self.count[stream] += 1
            op.idx = self.count[stream]
            op.clock[stream] = op.idx
        for k in reads:
            self.readers.setdefault(k, []).append(op)
        for k in writes:
            self.last_writer[k] = op
            self.readers[k] = []
        self.ops.append(op)
        return op

    def emit(self, es, final_wait_keys=()):
        nc = self.nc
        sems = {}
        for s in ("pe", "act", "dve", "pool"):
            sems[s] = es.enter_context(nc.semaphore("c_" + s))
        for i in range(self.n_dma):
            sems[("d", i)] = es.enter_context(nc.semaphore("d_%d" % i))
        fin = list(self.pending["sp"])
        know = self.know["sp"]
        for k in final_wait_keys:
            w = self.last_writer.get(k)
            if w is None:
                continue
            key, val = (("d", w.slot), w.cnt) if w.is_dma else (w.stream, w.idx)
            if know.get(key, 0) < val:
                know[key] = val
                fin.append((key, val))
        block = es.enter_context(nc.Block())
        by = {s: [o for o in self.ops if o.stream == s] for s in self.STREAMS}

        def run(eng, stream, extra=()):
            for o in by[stream]:
                for key, val in o.waits:
                    eng.wait_ge(sems[key], val * (16 if isinstance(key, tuple) else 1))
                ins = o.fn(eng)
                if o.is_dma:
                    ins.then_inc(sems[("d", o.slot)], 16)
                else:
                    ins.then_inc(sems[stream], 1)
            for key, val in extra:
                eng.wait_ge(sems[key], val * (16 if isinstance(key, tuple) else 1))

        @block.tensor
        def _(e):
            run(e, "pe")

        @block.scalar
        def _(e):
            run(e, "act")

        @block.vector
        def _(e):
            run(e, "dve")

        @block.gpsimd
        def _(e):
            run(e, "pool")

        @block.sync
        def _(e):
            run(e, "sp", fin)


class Arena:
    def __init__(self, nc, es, words):
        self.t = es.enter_context(nc.sbuf_tensor("arena", [128, words], F32))
        self.off = 0
        self.words = words

    def mark(self):
        return self.off

    def release(self, m):
        self.off = m

    def alloc(self, shape, dtype=F32, parts=128):
        shape = [int(s) for s in shape]
        n = int(np.prod(shape))
        sz = 4 if dtype == F32 else 2
        words = (n * sz + 3) // 4
        words = (words + 7) // 8 * 8
        assert self.off + words <= self.words, "arena overflow %d + %d > %d" % (self.off, words, self.words)
        ap = self.t[0:parts, self.off:self.off + words]
        self.off += words
        if dtype != F32:
            ap = ap.bitcast(dtype)
        ap = ap[:, 0:n]
        if len(shape) > 1:
            names = " ".join("d%d" % i for i in range(len(shape)))
            kw = {"d%d" % i: s for i, s in enumerate(shape)}
            ap = ap.rearrange("p (%s) -> p %s" % (names, names), **kw)
        return ap


def host_consts():
    c = {}
    c["c_ident_bf"] = np.eye(128).astype(ml_dtypes.bfloat16)
    c["c_ident_f"] = np.eye(128).astype(np.float32)
    m = (np.arange(128)[None, :] >= np.arange(128)[:, None]).astype(np.float32)
    c["c_mask"] = m.astype(ml_dtypes.bfloat16)
    c["c_ones_bf"] = np.ones((128, 128), ml_dtypes.bfloat16)
    s127 = np.zeros((128, 128), np.float32)
    s127[127, :] = 1
    s63 = np.zeros((128, 128), np.float32)
    s63[63, :] = 1
    c["c_sel127"] = s127
    c["c_sel63"] = s63
    return c


class Builder:
    def __init__(self, S=4096, debug=False):
        self.S = S
        self.NT = S // 512
        self.Ns = S // 8
        self.NB = S // 128
        self.debug = debug
        self.nc = bass.Bass("TRN2", target_bir_lowering=False)
        self.es = ExitStack()
        self.uid = 0

    def din(self, name, shape, dt=F32):
        return self.nc.dram_tensor(name, list(shape), dt, kind="ExternalInput").ap()

    def dscr(self, name, shape, dt):
        kind = "ExternalOutput" if self.debug else "Internal"
        return self.nc.dram_tensor(name, list(shape), dt, kind=kind).ap()

    def op(self, stream, fn, reads=(), writes=()):
        return self.sc.add(stream, fn, reads, writes)

    def dma(self, stream, out, in_, reads=(), writes=(), slow=False):
        if slow:
            return self.sc.add(stream, lambda e: e.dma_start(out=out, in_=in_, allow_slow_non_contiguous=True), reads, writes, dma=True)
        return self.sc.add(stream, lambda e: e.dma_start(out=out, in_=in_), reads, writes, dma=True)

    def load_w(self, dst, src, key):
        cols = src.shape[-1]
        step = 2048
        c = 0
        while c < cols:
            n = min(step, cols - c)
            self.dma("pool", dst[:, c:c + n], src[:, c:c + n], writes=[key])
            c += n

    def new_phase(self, mark):
        self.sc.barrier()
        self.ar.release(mark)

    def build(self):
        nc, es = self.nc, self.es
        S, NT, Ns, NB = self.S, self.NT, self.Ns, self.NB
        with es:
            self.sc = Sched(nc)
            self.ar = Arena(nc, es, 53000)
            self.ps = [es.enter_context(nc.psum_tensor("ps%d" % i, [128, 512], F32)) for i in range(8)]
            I = {}
            I["x"] = self.din("x", [S, D])
            I["mem"] = self.din("mem", [MEM, D])
            for nm, shp in [("g_mix", [2, D]), ("w_in", [2, D, D]), ("w_out", [2, D, D]), ("g_mem", [D]),
                            ("w_mem_kv", [2, D, 512]), ("s5_a_re", [1, 48, 64]), ("s5_a_im", [1, 48, 64]),
                            ("s5_log_dt", [1, 48]), ("s5_b_re", [1, 48, 64, 16]), ("s5_b_im", [1, 48, 64, 16]),
                            ("s5_c_re", [1, 48, 16, 64]), ("s5_c_im", [1, 48, 16, 64]), ("s5_d", [1, 768]),
                            ("w_glu", [1, 768, 768]), ("g_kv", [D]), ("w_kv", [D, 1536]), ("w_fgate", [D, 12]),
                            ("b_fgate", [12]), ("g_ffn", [2, D]), ("w_ffn_up", [2, D, 2 * DFF]), ("conv_w", [2, 3, DFF]),
                            ("conv_b", [2, DFF]), ("w_ffn_down", [2, DFF, D]), ("g_final", [D])]:
                I[nm] = self.din(nm, shp)
            I["c_ident_bf"] = self.din("c_ident_bf", [128, 128], BF16)
            I["c_ident_f"] = self.din("c_ident_f", [128, 128], F32)
            I["c_mask"] = self.din("c_mask", [128, 128], BF16)
            I["c_ones_bf"] = self.din("c_ones_bf", [128, 128], BF16)
            I["c_sel127"] = self.din("c_sel127", [128, 128], F32)
            I["c_sel63"] = self.din("c_sel63", [128, 128], F32)
            self.I = I
            self.y = nc.dram_tensor("y", [S, D], F32, kind="ExternalOutput").ap()
            self.uT_d = self.dscr("uT_d", [8, 96, 8, Ns], BF16)
            self.gT_d = self.dscr("gT_d", [8, 96, S], BF16)
            self.memo_d = [self.dscr("memo_d%d" % l, [4, 64, S], BF16) for l in range(2)]
            self.hA = self.dscr("hA", [S, D], F32)
            self.hB = self.dscr("hB", [S, D], F32)
            self.kT_d = self.dscr("kT_d", [768, S], BF16)
            self.qT_d = self.dscr("qT_d", [768, S], BF16)
            self.v_d = self.dscr("v_d", [S, 768], BF16)
            self.oT_d = self.dscr("oT_d", [768, S], BF16)
            self.E_d = self.dscr("E_d", [8, 2, 128, 3 * Ns], F32)

            if self.debug:
                self.dbg_rc = self.dscr("dbg_rc", [NT, 512], F32)
                self.dbg_bc = self.dscr("dbg_bc", [NT, 65, 512], F32)
                self.dbg_qa1 = self.dscr("dbg_qa1", [1, S], BF16)
                self.dbg_ka1 = self.dscr("dbg_ka1", [1, S], BF16)
                self.dbg_v1 = self.dscr("dbg_v1", [128, NB, 2], BF16)
            self.setup_persistent()
            m0 = self.ar.mark()
            self.s5_prep()
            m1 = self.ar.mark()
            self.phase_front0()
            self.new_phase(m1)
            self.phase_s5()
            self.new_phase(m0)
            wup0 = self.ar.alloc([8, 2 * DFF], BF16)
            mA = self.ar.mark()
            self.sweep_glu_wout0(after_weights=lambda: self.load_wup(0, wup0))
            self.new_phase(mA)
            self.sweep_ffn(0, self.hA, self.hB, final=False, wup=wup0)
            self.new_phase(m0)
            self.cs = self.ar.alloc([S], F32)
            m2 = self.ar.mark()
            self.sweep_kv_front1()
            self.new_phase(m2)
            self.phase_fox()
            self.new_phase(m0)
            wup1 = self.ar.alloc([8, 2 * DFF], BF16)
            mB = self.ar.mark()
            self.sweep_wout1(after_weights=lambda: self.load_wup(1, wup1))
            self.new_phase(mB)
            self.sweep_ffn(1, self.hA, None, final=True, wup=wup1)
            self.sc.emit(es, final_wait_keys=[("dram", "y", t) for t in range(NT)])
        return nc

    def setup_persistent(self):
        ar, I = self.ar, self.I
        self.ident_bf = ar.alloc([128], BF16)
        self.ident_f = ar.alloc([128], F32)
        self.mask = ar.alloc([128], BF16)
        self.ones_bf = ar.alloc([128], BF16)
        self.sel127 = ar.alloc([128], F32)
        self.sel63 = ar.alloc([128], F32)
        for nm, dst in [("c_ident_bf", self.ident_bf), ("c_ident_f", self.ident_f), ("c_mask", self.mask),
                        ("c_ones_bf", self.ones_bf), ("c_sel127", self.sel127), ("c_sel63", self.sel63)]:
            self.dma("sp", dst, I[nm], writes=[nm])
        self.gcol = {}
        self.late = []
        for nm, src in [("gmix0", I["g_mix"][0]), ("gmem", I["g_mem"]), ("gmix1", I["g_mix"][1]), ("gffn0", I["g_ffn"][0]),
                        ("gffn1", I["g_ffn"][1]), ("gkv", I["g_kv"])]:
            t = ar.alloc([8], F32)
            f = lambda t=t, src=src, nm=nm: self.dma("sp", t, src.rearrange("(k p) -> p k", p=128), writes=[nm], slow=True)
            if nm in ("gmix0", "gmem"):
                f()
            else:
                self.late.append(f)
            self.gcol[nm] = t
        self.gF = ar.alloc([D], F32)
        self.late.append(lambda: self.dma("sp", self.gF, I["g_final"].partition_broadcast(128), writes=["gF"]))
        self.cw = []
        self.cb = []
        for l in range(2):
            t = ar.alloc([NJ, 3], F32)
            for kk in range(3):
                self.late.append(lambda t=t, l=l, kk=kk: self.dma("sp", t[:, :, kk], I["conv_w"][l, kk].rearrange("(j p) -> p j", p=128), reads=[("cw", l)], writes=[("cw", l)], slow=True))
            self.cw.append(t)
            t2 = ar.alloc([NJ], F32)
            self.late.append(lambda t2=t2, l=l: self.dma("sp", t2, I["conv_b"][l].rearrange("(j p) -> p j", p=128), writes=[("cb", l)], slow=True))
            self.cb.append(t2)
        self.nbf = ar.alloc([1], F32)
        self.late.append(lambda: self.dma("sp", self.nbf[0:12, :], I["b_fgate"].rearrange("(p o) -> p o", o=1), writes=["nbf"], slow=True))
        self.late.append(lambda: self.op("dve", lambda e: e.tensor_scalar(out=self.nbf[0:12, :], in0=self.nbf[0:12, :], scalar1=-1.0, scalar2=None, op0=ALU.mult),
                                         reads=["nbf"], writes=["nbf"]))
        self.zrow = ar.alloc([128], BF16)
        self.op("dve", lambda e: e.memset(self.zrow, 0.0), writes=["zrow"])
        self.KTm = ar.alloc([2, 2, MEM], BF16)
        self.Vm = ar.alloc([2, 2, 256], BF16)
        m = ar.mark()
        mt = ar.alloc([2, D], F32)
        mn = ar.alloc([2, D], BF16)
        mnT = ar.alloc([8, MEM], BF16)
        wmk = ar.alloc([2, 8, 512], BF16)
        ss = ar.alloc([2], F32)
        junk = ar.alloc([D], F32)
        for i in range(2):
            self.dma("sp", mt[:, i, :], I["mem"][128 * i:128 * i + 128, :], writes=[("mt", i)])
        for l in range(2):
            for k in range(8):
                self.load_w(wmk[:, l, k, :], I["w_mem_kv"][l, 128 * k:128 * k + 128, :], ("wmk", l, k))
        for i in range(2):
            self.rms_rows(mt[:, i, :], mn[:, i, :], ss[:, i:i + 1], junk, ("mt", i), ("mn", i), ("mss", i), "mjunk")
        for k in range(8):
            pb = self.ps[k % 2].bitcast(BF16)
            for i in range(2):
                self.op("pe", lambda e, k=k, i=i, pb=pb: e.transpose(out=pb[:, 128 * i:128 * i + 128], in_=mn[:, i, 128 * k:128 * k + 128], identity=self.ident_bf),
                        reads=[("mn", i), "c_ident_bf"], writes=[("ps", k % 2)])
            self.op("dve", lambda e, k=k, pb=pb: e.tensor_scalar(out=mnT[:, k, :], in0=pb[:, 0:256], scalar1=self.gcol["gmem"][:, k:k + 1], scalar2=None, op0=ALU.mult),
                    reads=[("ps", k % 2), "gmem"], writes=[("mnT", k)])
        for l in range(2):
            for c in range(2):
                b = 2 + (2 * l + c) % 2
                for k in range(8):
                    self.op("pe", lambda e, l=l, c=c, k=k, b=b: e.matmul(self.ps[b][:, 0:256], lhsT=wmk[:, l, k, 128 * c:128 * c + 128], rhs=mnT[:, k, :], start=(k == 0), stop=(k == 7)),
                            reads=[("wmk", l, k), ("mnT", k)], writes=[("ps", b)])
                self.op("act", lambda e, l=l, c=c, b=b: e.activation(out=self.KTm[:, l, c, :], in_=self.ps[b][:, 0:256], func=AF.Copy),
                        reads=[("ps", b)], writes=[("KTm", l)])
            for kb in range(2):
                b = 4 + kb
                for k in range(8):
                    self.op("pe", lambda e, l=l, kb=kb, k=k, b=b: e.matmul(self.ps[b][:, 0:256], lhsT=mnT[:, k, 128 * kb:128 * kb + 128], rhs=wmk[:, l, k, 256:512], start=(k == 0), stop=(k == 7)),
                            reads=[("wmk", l, k), ("mnT", k)], writes=[("ps", b)])
                self.op("act", lambda e, l=l, kb=kb, b=b: e.activation(out=self.Vm[:, l, kb, :], in_=self.ps[b][:, 0:256], func=AF.Copy),
                        reads=[("ps", b)], writes=[("Vm", l)])
        self.sc.barrier()
        ar.release(m)

    def rms_rows(self, src, dst_bf, ss, junk, ksrc, kdst, kss, kjunk):
        self.op("act", lambda e: e.activation(out=junk, in_=src, func=AF.Square, accum_out=ss), reads=[ksrc], writes=[kjunk, kss])
        self.op("act", lambda e: e.activation(out=ss, in_=ss, func=AF.Sqrt, scale=1.0 / D, bias=EPS), reads=[kss], writes=[kss])
        self.op("dve", lambda e: e.reciprocal(out=ss, in_=ss), reads=[kss], writes=[kss])
        self.op("act", lambda e: e.activation(out=dst_bf, in_=src, func=AF.Copy, scale=ss), reads=[ksrc, kss], writes=[kdst])

    def norm_T(self, ht, hkeys, outs, tag):
        w = self.nt_work
        for i in range(4):
            sl = i % 2
            self.rms_rows(ht[:, i, :], w["xn"][:, sl, :], w["ss"][:, i:i + 1], w["junk"], hkeys[i], ("xn", sl), ("nss", i), "njunk")
            bi = 6 + i % 2
            pb = self.ps[bi].bitcast(BF16)
            for k in range(8):
                self.op("pe", lambda e, k=k, sl=sl, pb=pb: e.transpose(out=pb[:, 128 * k:128 * k + 128], in_=w["xn"][:, sl, 128 * k:128 * k + 128], identity=self.ident_bf),
                        reads=[("xn", sl), "c_ident_bf"], writes=[("ps", bi)])
            for gi, (gname, dst, dkey) in enumerate(outs):
                self.op("dve", lambda e, i=i, pb=pb, gname=gname, dst=dst: e.tensor_tensor(
                    out=dst[:, :, 128 * i:128 * i + 128], in0=pb.rearrange("p (k t) -> p k t", k=8),
                    in1=self.gcol[gname].unsqueeze(2).to_broadcast([128, 8, 128]), op=ALU.mult),
                    reads=[("ps", bi), gname], writes=[(dkey, k) for k in range(8)])

    def alloc_norm_work(self):
        ar = self.ar
        self.nt_work = {"xn": ar.alloc([2, D], BF16), "ss": ar.alloc([4], F32), "junk": ar.alloc([D], BF16)}

    def mem_attention(self, l, qmT, qkey, mo, mokey, PTs, tag):
        def st(i):
            hh, kb = i // 2, i % 2
            c, base = hh // 2, 64 * (hh % 2)
            bs = 2 + i % 2
            self.op("pe", lambda e: e.matmul(self.ps[bs][:, :], lhsT=self.KTm[base:base + 64, l, c, 128 * kb:128 * kb + 128], rhs=qmT[base:base + 64, c, :], start=True, stop=True),
                    reads=[("KTm", l), qkey], writes=[("ps", bs)])

        def pv(i):
            hh, kb = i // 2, i % 2
            bs = 2 + i % 2
            bo, bd = (4, 5) if hh % 2 == 0 else (0, 1)
            pt = PTs[i % 2]
            self.op("act", lambda e: e.activation(out=pt, in_=self.ps[bs][:, :], func=AF.Exp), reads=[("ps", bs)], writes=[("PT", i % 2)])
            self.op("pe", lambda e: e.matmul(self.ps[bo][0:64, :], lhsT=self.Vm[:, l, kb, 64 * hh:64 * hh + 64], rhs=pt, start=(kb == 0), stop=(kb == 1)),
                    reads=[("Vm", l), ("PT", i % 2)], writes=[("ps", bo)])
            self.op("pe", lambda e: e.matmul(self.ps[bd][0:64, :], lhsT=self.ones_bf[:, 0:64], rhs=pt, start=(kb == 0), stop=(kb == 1)),
                    reads=["c_ones_bf", ("PT", i % 2)], writes=[("ps", bd)])

        def tail(hh):
            bo, bd = (4, 5) if hh % 2 == 0 else (0, 1)
            rec = self.ma_rec
            self.op("act", lambda e: e.activation(out=rec[0:64, :], in_=self.ps[bd][0:64, :], func=AF.Ln), reads=[("ps", bd)], writes=["marec"])
            self.op("act", lambda e: e.activation(out=rec[0:64, :], in_=rec[0:64, :], func=AF.Exp, scale=-1.0), reads=["marec"], writes=["marec"])
            self.op("dve", lambda e: e.tensor_tensor(out=mo[0:64, hh, :], in0=self.ps[bo][0:64, :], in1=rec[0:64, :], op=ALU.mult),
                    reads=[("ps", bo), "marec"], writes=[mokey])

        st(0)
        for i in range(8):
            if i + 1 < 8:
                st(i + 1)
            pv(i)
            if i % 2 == 0 and i >= 2:
                tail(i // 2 - 1)
        tail(3)

    def s5_prep(self):
        ar, I, sc = self.ar, self.I, self.sc
        nsteps = max(1, int(np.ceil(np.log2(self.Ns))))
        self.ks_steps = nsteps
        self.Win = ar.alloc([8, 8, 2, 128], BF16, parts=96)
        self.Kblk = ar.alloc([8, 8, 96], BF16, parts=96)
        self.Wout = ar.alloc([24, 8, 2, 32], BF16)
        self.r8 = ar.alloc([24], F32)
        self.EP = ar.alloc([2, nsteps, 24], F32)
        m = ar.mark()
        T = lambda shape=(24,): ar.alloc(list(shape), F32)
        are, aim, ldt = T(), T(), T()
        Bre, Bim = T((24, 16)), T((24, 16))
        Cre, Cim = T((24, 16)), T((24, 16))
        Yre, Yim = ar.alloc([8, 2, 64], F32, parts=48), ar.alloc([8, 2, 64], F32, parts=48)
        for gl in range(2):
            P = slice(64 * gl, 64 * gl + 64)
            self.dma("sp", are[P, :], I["s5_a_re"][0, gl::2, :].rearrange("g p -> p g"), writes=["are"], slow=True)
            self.dma("sp", aim[P, :], I["s5_a_im"][0, gl::2, :].rearrange("g p -> p g"), writes=["aim"], slow=True)
            self.dma("sp", ldt[P, :], I["s5_log_dt"][0, gl::2].partition_broadcast(64), writes=["ldt"], slow=True)
            self.dma("sp", Bre[P, :, :], I["s5_b_re"][0, gl::2].rearrange("g p i -> p g i"), writes=["Bre"], slow=True)
            self.dma("sp", Bim[P, :, :], I["s5_b_im"][0, gl::2].rearrange("g p i -> p g i"), writes=["Bim"], slow=True)
        for k in range(3):
            for (Y, nm, ky) in ((Yre, "s5_c_re", "Yre"), (Yim, "s5_c_im", "Yim")):
                src = I[nm][0].rearrange("(q k gl) i p -> k q i gl p", k=3, gl=2)[k]
                for q in range(8):
                    self.dma("sp", Y[16 * k:16 * k + 16, q, :, :], src[q], writes=[ky], slow=True)
        for (Y, Cx, ky, kc) in ((Yre, Cre, "Yre", "Cre"), (Yim, Cim, "Yim", "Cim")):
            for q in range(8):
                b = q % 2
                self.op("pe", lambda e, Y=Y, q=q, b=b: e.transpose(out=self.ps[b][:, 0:48], in_=Y[0:48, q, :, :].rearrange("p a b -> p (a b)"), identity=self.ident_f[0:48, 0:48]),
                        reads=[ky, "c_ident_f"], writes=[("ps", b)])
                self.op("act", lambda e, Cx=Cx, q=q, b=b: e.activation(out=Cx[:, 3 * q:3 * q + 3, :], in_=self.ps[b][:, 0:48].rearrange("p (k i) -> p k i", k=3), func=AF.Copy),
                        reads=[("ps", b)], writes=[kc])
        V = "dve"

        def tt(out, a, b, op, r, w):
            self.op(V, lambda e: e.tensor_tensor(out=out, in0=a, in1=b, op=op), reads=r, writes=w)

        def ts(out, a, s1, s2, op0, op1, r, w):
            if s2 is None:
                self.op(V, lambda e: e.tensor_scalar(out=out, in0=a, scalar1=s1, scalar2=None, op0=op0), reads=r, writes=w)
            else:
                self.op(V, lambda e: e.tensor_scalar(out=out, in0=a, scalar1=s1, scalar2=s2, op0=op0, op1=op1), reads=r, writes=w)

        dt, lre, mag, ph, sn, cs = T(), T(), T(), T(), T(), T()
        t1, t2, t3 = T(), T(), T()
        self.op("act", lambda e: e.activation(out=dt, in_=ldt, func=AF.Exp), reads=["ldt"], writes=["dt"])
        ts(lre, are, -1e-4, None, ALU.min, None, ["are"], ["lre"])
        tt(t1, lre, dt, ALU.mult, ["lre", "dt"], ["t1"])
        self.op("act", lambda e: e.activation(out=mag, in_=t1, func=AF.Exp), reads=["t1"], writes=["mag"])
        self.op("act", lambda e: e.activation(out=self.r8, in_=t1, func=AF.Exp, scale=8.0), reads=["t1"], writes=["r8"])
        tt(ph, aim, dt, ALU.mult, ["aim", "dt"], ["ph"])
        hpi = ar.alloc([1], F32)
        self.op(V, lambda e: e.memset(hpi, float(np.pi / 2)), writes=["hpi"])
        self.op("act", lambda e: e.activation(out=sn, in_=ph, func=AF.Sin, scale=1.0 / 16), reads=["ph"], writes=["sn"])
        self.op("act", lambda e: e.activation(out=cs, in_=ph, func=AF.Sin, scale=1.0 / 16, bias=hpi), reads=["ph", "hpi"], writes=["cs"])
        for it in range(4):
            tt(t1, cs, cs, ALU.mult, ["cs"], ["t1"])
            tt(t2, sn, sn, ALU.mult, ["sn"], ["t2"])
            tt(t3, cs, sn, ALU.mult, ["cs", "sn"], ["t3"])
            tt(cs, t1, t2, ALU.subtract, ["t1", "t2"], ["cs"])
            ts(sn, t3, 2.0, None, ALU.mult, None, ["t3"], ["sn"])
        PW = T((2, 9, 24))
        self.op(V, lambda e: e.memset(PW[:, 0, 0, :], 1.0), writes=["PW"])
        self.op(V, lambda e: e.memset(PW[:, 1, 0, :], 0.0), reads=["PW"], writes=["PW"])
        tt(PW[:, 0, 1, :], mag, cs, ALU.mult, ["mag", "cs", "PW"], ["PW"])
        tt(PW[:, 1, 1, :], mag, sn, ALU.mult, ["mag", "sn", "PW"], ["PW"])

        def cmul(or_, oi, ar_, ai_, br_, bi_, rk, wk):
            tt(t1, ar_, br_, ALU.mult, rk, ["t1"])
            tt(t2, ai_, bi_, ALU.mult, rk, ["t2"])
            tt(or_, t1, t2, ALU.subtract, ["t1", "t2"] + rk, wk)
            tt(t1, ar_, bi_, ALU.mult, rk, ["t1"])
            tt(t2, ai_, br_, ALU.mult, rk, ["t2"])
            tt(oi, t1, t2, ALU.add, ["t1", "t2"] + rk, wk)

        for k in range(2, 9):
            cmul(PW[:, 0, k, :], PW[:, 1, k, :], PW[:, 0, k - 1, :], PW[:, 1, k - 1, :], PW[:, 0, 1, :], PW[:, 1, 1, :], ["PW"], ["PW"])
        EP = self.EP
        rr = T()
        self.op(V, lambda e: e.reciprocal(out=rr, in_=self.r8), reads=["r8"], writes=["rr"])
        tt(EP[:, 0, 0, :], PW[:, 0, 8, :], rr, ALU.mult, ["PW", "rr", "EP"], ["EP"])
        tt(EP[:, 1, 0, :], PW[:, 1, 8, :], rr, ALU.mult, ["PW", "rr", "EP"], ["EP"])
        for k in range(1, nsteps):
            cmul(EP[:, 0, k, :], EP[:, 1, k, :], EP[:, 0, k - 1, :], EP[:, 1, k - 1, :], EP[:, 0, k - 1, :], EP[:, 1, k - 1, :], ["EP"], ["EP"])
        den, am1, zre, zim = T(), T(), T(), T()
        tt(t1, lre, lre, ALU.mult, ["lre"], ["t1"])
        tt(t2, aim, aim, ALU.mult, ["aim"], ["t2"])
        tt(den, t1, t2, ALU.add, ["t1", "t2"], ["den"])
        self.op(V, lambda e: e.reciprocal(out=den, in_=den), reads=["den"], writes=["den"])
        ts(am1, PW[:, 0, 1, :], -1.0, None, ALU.add, None, ["PW"], ["am1"])
        tt(t1, am1, lre, ALU.mult, ["am1", "lre"], ["t1"])
        tt(t2, PW[:, 1, 1, :], aim, ALU.mult, ["PW", "aim"], ["t2"])
        tt(t3, t1, t2, ALU.add, ["t1", "t2"], ["t3"])
        tt(zre, t3, den, ALU.mult, ["t3", "den"], ["zre"])
        tt(t1, PW[:, 1, 1, :], lre, ALU.mult, ["PW", "lre"], ["t1"])
        tt(t2, am1, aim, ALU.mult, ["am1", "aim"], ["t2"])
        tt(t3, t1, t2, ALU.subtract, ["t1", "t2"], ["t3"])
        tt(zim, t3, den, ALU.mult, ["t3", "den"], ["zim"])
        bbr, bbi, u1, u2 = T((24, 16)), T((24, 16)), T((24, 16)), T((24, 16))

        def bc(a):
            return a.unsqueeze(2).to_broadcast([128, 24, 16])

        def cmul3(or_, oi, pr, pi, br_, bi_, rk, wk):
            tt(u1, br_, bc(pr), ALU.mult, rk, ["u1"])
            tt(u2, bi_, bc(pi), ALU.mult, rk, ["u2"])
            tt(or_, u1, u2, ALU.subtract, ["u1", "u2"] + rk, wk)
            tt(u1, bi_, bc(pr), ALU.mult, rk, ["u1"])
            tt(u2, br_, bc(pi), ALU.mult, rk, ["u2"])
            tt(oi, u1, u2, ALU.add, ["u1", "u2"] + rk, wk)

        cmul3(bbr, bbi, zre, zim, Bre, Bim, ["zre", "zim", "Bre", "Bim"], ["bb"])
        W0 = T((24, 2, 32))
        self.op(V, lambda e: e.memset(W0, 0.0), writes=["W0"])
        self.op("pool", lambda e: e.memset(self.Wout, 0.0), writes=["Wout"])
        wr, wi = T((24, 16)), T((24, 16))
        for gl in range(2):
            P = slice(64 * gl, 64 * gl + 64)
            self.op(V, lambda e, P=P, gl=gl: e.tensor_copy(out=W0[P, :, 0, 16 * gl:16 * gl + 16], in_=Cre[P, :, :]), reads=["Cre", "W0"], writes=["W0"])
            self.op(V, lambda e, P=P, gl=gl: e.tensor_scalar(out=W0[P, :, 1, 16 * gl:16 * gl + 16], in0=Cim[P, :, :], scalar1=-1.0, scalar2=None, op0=ALU.mult), reads=["Cim", "W0"], writes=["W0"])
        for l in range(8):
            cmul3(wr, wi, PW[:, 0, l + 1, :], PW[:, 1, l + 1, :], Cre, Cim, ["PW", "Cre", "Cim"], ["wrwi"])
            for gl in range(2):
                P = slice(64 * gl, 64 * gl + 64)
                self.op(V, lambda e, P=P, gl=gl, l=l: e.tensor_copy(out=self.Wout[P, :, l, 0, 16 * gl:16 * gl + 16], in_=wr[P, :, :]), reads=["wrwi", "Wout"], writes=["Wout"])
                self.op(V, lambda e, P=P, gl=gl, l=l: e.tensor_scalar(out=self.Wout[P, :, l, 1, 16 * gl:16 * gl + 16], in0=wi[P, :, :], scalar1=-1.0, scalar2=None, op0=ALU.mult), reads=["wrwi", "Wout"], writes=["Wout"])
        Kst = ar.alloc([24, 8, 32], BF16, parts=32)
        xr, xi = T((24, 16)), T((24, 16))
        Xpad = T((4, 8, 2, 3, 32))
        for qh in range(2):
            self.op("pool", lambda e: e.memset(Xpad, 0.0), reads=["Xpad"], writes=["Xpad"])
            for j in range(8):
                cmul3(xr, xi, PW[:, 0, 7 - j, :], PW[:, 1, 7 - j, :], bbr, bbi, ["PW", "bb"], ["xrxi"])
                for gl in range(2):
                    P = slice(64 * gl, 64 * gl + 64)
                    for c, xx in ((0, xr), (1, xi)):
                        self.op(V, lambda e, P=P, gl=gl, j=j, c=c, xx=xx, qh=qh: e.tensor_copy(
                            out=Xpad[P, :, j, c, :, 16 * gl:16 * gl + 16],
                            in_=xx[P, 12 * qh:12 * qh + 12, :].rearrange("p (q k) i -> p q k i", k=3)),
                            reads=["xrxi", "Xpad"], writes=["Xpad"])
            n = 0
            for ql in range(4):
                q = 4 * qh + ql
                for j in range(8):
                    for c in range(2):
                        b = (n // 4) % 2
                        col = 128 * (n % 4)
                        self.op("pe", lambda e, ql=ql, j=j, c=c, b=b, col=col: e.transpose(
                            out=self.ps[b][0:96, col:col + 128], in_=Xpad[:, ql, j, c, :, :].rearrange("p k i -> p (k i)"), identity=self.ident_f),
                            reads=["Xpad", "c_ident_f"], writes=[("ps", b)])
                        if n % 4 == 3:
                            j0 = j - 1
                            self.op("act", lambda e, q=q, j0=j0, b=b: e.activation(
                                out=self.Win[:, q, j0:j0 + 2, :, :].rearrange("p a b s -> p (a b s)"), in_=self.ps[b][0:96, :], func=AF.Copy),
                                reads=[("ps", b)], writes=["Win"])
                        n += 1
            n = 0
            for ql in range(4):
                for k in range(3):
                    pair = 3 * (4 * qh + ql) + k
                    for tau in range(8):
                        b = 2 + (n // 16) % 2
                        col = 32 * (n % 16)
                        for c in range(2):
                            self.op("pe", lambda e, ql=ql, k=k, pair=pair, tau=tau, c=c, b=b, col=col: e.matmul(
                                self.ps[b][0:32, col:col + 32], lhsT=Xpad[:, ql, 7 - tau, c, k, :], rhs=W0[:, pair, c, :], start=(c == 0), stop=(c == 1)),
                                reads=["Xpad", "W0"], writes=[("ps", b)])
                        if n % 16 == 15:
                            self.op("act", lambda e, pair=pair, b=b: e.activation(
                                out=Kst[0:32, pair - 1:pair + 1, :, :].rearrange("p a t i -> p (a t i)"), in_=self.ps[b][0:32, :], func=AF.Copy),
                                reads=[("ps", b)], writes=["Kst"])
                        n += 1
        self.op("pool", lambda e: e.memset(self.Kblk, 0.0), writes=["Kblk"])
        for k in range(3):
            for q in range(8):
                self.dma("sp", self.Kblk[32 * k:32 * k + 32, q, :, 32 * k:32 * k + 32],
                         Kst[0:32, 3 * q + k, :, :], reads=["Kst", "Kblk"], writes=["Kblk"], slow=True)
        dT = ar.alloc([8], F32, parts=96)
        self.dma("sp", dT, I["s5_d"][0].rearrange("(q p) -> p q", p=96), writes=["dT"], slow=True)
        for q in range(8):
            self.op(V, lambda e, q=q: e.scalar_tensor_tensor(out=self.Kblk[:, q, 0, :], in0=self.ident_f[0:96, 0:96], scalar=dT[:, q:q + 1], in1=self.Kblk[:, q, 0, :], op0=ALU.mult, op1=ALU.add),
                    reads=["dT", "Kblk", "c_ident_f"], writes=["Kblk"])
        self.sc.barrier()
        ar.release(m)

    def phase_front0(self):
        ar, I, S, NT = self.ar, self.I, self.S, self.NT
        for f in self.late:
            f()
        self.late = []
        wsb = ar.alloc([8, D], BF16)
        for k in range(8):
            self.load_w(wsb[:, k, :], I["w_in"][0, 128 * k:128 * k + 128, :], ("w", k))
        xt = [ar.alloc([4, D], F32) for _ in range(2)]
        self.alloc_norm_work()
        xnTs = [ar.alloc([8, 512], BF16) for _ in range(2)]
        ust = [ar.alloc([8, 8, 64], BF16, parts=96) for _ in range(2)]
        qmT = ar.alloc([2, 512], BF16)
        PTs = [ar.alloc([512], BF16) for _ in range(2)]
        self.ma_rec = ar.alloc([512], F32)
        mo = [ar.alloc([4, 512], BF16, parts=64) for _ in range(2)]
        Ns = self.Ns
        gEc = ar.alloc([3, Ns], F32)
        gEs = ar.alloc([3, Ns], F32)
        gtg = [ar.alloc([3, max(1, Ns // 2)], F32) for _ in range(4)]
        per_tile = 8 // NT if NT <= 8 else 1

        def load(t):
            for i in range(4):
                self.dma("sp", xt[t % 2][:, i, :], I["x"][512 * t + 128 * i:512 * t + 128 * i + 128, :], writes=[("xt", t % 2, i)])

        load(0)
        self.norm_T(xt[0], [("xt", 0, i) for i in range(4)], [("gmix0", xnTs[0], ("xnT", 0))], "f0")
        for t in range(NT):
            if t + 1 < NT:
                load(t + 1)
            s = t % 2
            xnT = xnTs[s]
            xk = ("xnT", s)
            for m in range(8):
                b = m % 2
                for k in range(8):
                    self.op("pe", lambda e, m=m, k=k, b=b, xnT=xnT: e.matmul(self.ps[b][0:96, :], lhsT=wsb[:, k, 96 * m:96 * m + 96], rhs=xnT[:, k, :], start=(k == 0), stop=(k == 7)),
                            reads=[("w", k), (xk, k)], writes=[("ps", b)])
                self.op("act", lambda e, m=m, b=b, s=s: e.activation(out=ust[s][:, m, :, :], in_=self.ps[b][0:96, :].rearrange("p (s l) -> p l s", l=8), func=AF.Copy),
                        reads=[("ps", b)], writes=[("ust", s)])
            for m in range(8):
                self.dma("sp", self.uT_d[m, :, :, 64 * t:64 * t + 64], ust[s][:, m, :, :], reads=[("ust", s)], writes=[("dram", "uT")])
            if t + 1 < NT:
                self.norm_T(xt[1 - s], [("xt", 1 - s, i) for i in range(4)], [("gmix0", xnTs[1 - s], ("xnT", 1 - s))], "f0")
            for c in range(2):
                b = c % 2
                for k in range(8):
                    self.op("pe", lambda e, c=c, k=k, b=b, xnT=xnT: e.matmul(self.ps[b][:, :], lhsT=wsb[:, k, 768 + 128 * c:768 + 128 * c + 128], rhs=xnT[:, k, :], start=(k == 0), stop=(k == 7)),
                            reads=[("w", k), (xk, k)], writes=[("ps", b)])
                self.op("act", lambda e, c=c, b=b: e.activation(out=qmT[:, c, :], in_=self.ps[b][:, :], func=AF.Copy, scale=0.125),
                        reads=[("ps", b)], writes=["qmT"])
            self.mem_attention(0, qmT, "qmT", mo[s], ("mo", s), PTs, "m0")
            self.dma("sp", self.memo_d[0][:, :, 512 * t:512 * t + 512].rearrange("h p t -> p h t"), mo[s], reads=[("mo", s)], writes=[("dram", "memo0")])
            for q in range(t * per_tile, min(8, (t + 1) * per_tile)):
                self.gen_tables(q, gEc, gEs, gtg)

    def gen_tables(self, q, ec, es_, tg):
        Ns, EP = self.Ns, self.EP
        kc, ks = "gEc", "gEs"
        P_ = "pool"
        self.op(P_, lambda e: e.memset(ec[:, :, 0:1], 1.0), writes=[kc])
        self.op(P_, lambda e: e.memset(es_[:, :, 0:1], 0.0), writes=[ks])
        for k in range(self.ks_steps):
            n = 1 << k
            if n >= Ns:
                break
            cb = EP[:, 0, k, 3 * q:3 * q + 3].unsqueeze(2).to_broadcast([128, 3, n])
            sb = EP[:, 1, k, 3 * q:3 * q + 3].unsqueeze(2).to_broadcast([128, 3, n])
            lo = slice(0, n)
            hi = slice(n, 2 * n)
            t1, t2, t3, t4 = [t[:, :, 0:n] for t in tg]
            self.op(P_, lambda e, t1=t1, cb=cb, lo=lo: e.tensor_tensor(out=t1, in0=ec[:, :, lo], in1=cb, op=ALU.mult), reads=[kc, "EP"], writes=["tg0"])
            self.op(P_, lambda e, t2=t2, sb=sb, lo=lo: e.tensor_tensor(out=t2, in0=es_[:, :, lo], in1=sb, op=ALU.mult), reads=[ks, "EP"], writes=["tg1"])
            self.op(P_, lambda e, t3=t3, sb=sb, lo=lo: e.tensor_tensor(out=t3, in0=ec[:, :, lo], in1=sb, op=ALU.mult), reads=[kc, "EP"], writes=["tg2"])
            self.op(P_, lambda e, t4=t4, cb=cb, lo=lo: e.tensor_tensor(out=t4, in0=es_[:, :, lo], in1=cb, op=ALU.mult), reads=[ks, "EP"], writes=["tg3"])
            self.op(P_, lambda e, t1=t1, t2=t2, hi=hi: e.tensor_tensor(out=ec[:, :, hi], in0=t1, in1=t2, op=ALU.subtract), reads=["tg0", "tg1", kc], writes=[kc])
            self.op(P_, lambda e, t3=t3, t4=t4, hi=hi: e.tensor_tensor(out=es_[:, :, hi], in0=t3, in1=t4, op=ALU.add), reads=["tg2", "tg3", ks], writes=[ks])
        self.dma("sp", self.E_d[q, 0], ec.rearrange("p a b -> p (a b)"), reads=[kc], writes=[("dram", "E", q)])
        self.dma("sp", self.E_d[q, 1], es_.rearrange("p a b -> p (a b)"), reads=[ks], writes=[("dram", "E", q)])

    def phase_s5(self):
        ar, S, Ns = self.ar, self.S, self.Ns
        nsteps = self.ks_steps
        PAD = 1 << (nsteps - 1)
        ut = [ar.alloc([8, Ns], BF16, parts=96) for _ in range(3)]
        Hshs = [ar.alloc([3, 2, Ns], BF16) for _ in range(2)]
        gq = [ar.alloc([S], BF16, parts=96) for _ in range(2)]
        x2 = [ar.alloc([Ns], F32, parts=96) for _ in range(2)]
        sg = [ar.alloc([Ns], F32, parts=96) for _ in range(2)]
        Ec = [ar.alloc([3, Ns], F32) for _ in range(2)]
        Es = [ar.alloc([3, Ns], F32) for _ in range(2)]
        NW = 6
        wk = [[ar.alloc([Ns], F32) for _ in range(NW)] for _ in range(2)]
        g_pm = ar.alloc([8, Ns], BF16, parts=96)
        for hs in range(2):
            self.op("pool", lambda e, hs=hs: e.memset(Hshs[hs][:, :, :, 0:1], 0.0), writes=[("Hsh", hs)])

        def load_tables(q):
            sl = q % 2
            self.dma("sp", Ec[sl].rearrange("p a b -> p (a b)"), self.E_d[q, 0], reads=[("dram", "E", q)], writes=[("Ec", sl)])
            self.dma("sp", Es[sl].rearrange("p a b -> p (a b)"), self.E_d[q, 1], reads=[("dram", "E", q)], writes=[("Es", sl)])

        def load(q):
            self.dma("sp", ut[q % 3], self.uT_d[q], reads=[("dram", "uT")], writes=[("ut", q % 3)])

        D_ = "dve"

        def stage1(q):
            u = ut[q % 3]
            ukey = ("ut", q % 3)
            sl = q % 2
            Hsh = Hshs[q % 2]
            hkey = ("Hsh", q % 2)
            for k in range(3):
                pair = 3 * q + k
                R = slice(32 * k, 32 * k + 32)
                npair = pair
                W = wk[npair % 2]
                wkey = lambda i, npair=npair: ("wk", npair % 2, i)
                for c in range(2):
                    b = c
                    for j in range(8):
                        self.op("pe", lambda e, R=R, j=j, c=c, b=b: e.matmul(self.ps[b][:, 0:Ns], lhsT=self.Win[R, q, j, c, :], rhs=u[R, j, :], start=(j == 0), stop=(j == 7)),
                                reads=["Win", ukey], writes=[("ps", b)])
                ec, es_ = Ec[sl][:, k, :], Es[sl][:, k, :]
                kc, ks = ("Ec", sl), ("Es", sl)
                Sre, Sim = self.ps[0][:, 0:Ns], self.ps[1][:, 0:Ns]
                self.op(D_, lambda e, W=W, ec=ec: e.tensor_tensor(out=W[0], in0=Sre, in1=ec, op=ALU.mult), reads=[("ps", 0), kc], writes=[wkey(0)])
                self.op(D_, lambda e, W=W, es_=es_: e.tensor_tensor(out=W[1], in0=Sim, in1=es_, op=ALU.mult), reads=[("ps", 1), ks], writes=[wkey(1)])
                self.op(D_, lambda e, W=W, ec=ec: e.tensor_tensor(out=W[2], in0=Sim, in1=ec, op=ALU.mult), reads=[("ps", 1), kc], writes=[wkey(2)])
                self.op(D_, lambda e, W=W, es_=es_: e.tensor_tensor(out=W[3], in0=Sre, in1=es_, op=ALU.mult), reads=[("ps", 0), ks], writes=[wkey(3)])
                self.op(D_, lambda e, W=W: e.tensor_tensor(out=W[0], in0=W[0], in1=W[1], op=ALU.add), reads=[wkey(0), wkey(1)], writes=[wkey(0)])
                self.op(D_, lambda e, W=W: e.tensor_tensor(out=W[2], in0=W[2], in1=W[3], op=ALU.subtract), reads=[wkey(2), wkey(3)], writes=[wkey(2)])
                r8b = self.r8[:, pair:pair + 1].to_broadcast([128, Ns])
                self.op(D_, lambda e, W=W, r8b=r8b: e.tensor_tensor_scan(out=W[4], data0=r8b, data1=W[0], initial=0.0, op0=ALU.mult, op1=ALU.add), reads=[wkey(0), "r8"], writes=[wkey(4)])
                self.op(D_, lambda e, W=W, r8b=r8b: e.tensor_tensor_scan(out=W[5], data0=r8b, data1=W[2], initial=0.0, op0=ALU.mult, op1=ALU.add), reads=[wkey(2), "r8"], writes=[wkey(5)])
                n1 = Ns - 1
                self.op(D_, lambda e, W=W, ec=ec: e.tensor_tensor(out=W[0], in0=W[4], in1=ec, op=ALU.mult), reads=[wkey(4), kc], writes=[wkey(0)])
                self.op(D_, lambda e, W=W, es_=es_: e.tensor_tensor(out=W[1], in0=W[5], in1=es_, op=ALU.mult), reads=[wkey(5), ks], writes=[wkey(1)])
                self.op(D_, lambda e, W=W, ec=ec: e.tensor_tensor(out=W[2], in0=W[5], in1=ec, op=ALU.mult), reads=[wkey(5), kc], writes=[wkey(2)])
                self.op(D_, lambda e, W=W, es_=es_: e.tensor_tensor(out=W[3], in0=W[4], in1=es_, op=ALU.mult), reads=[wkey(4), ks], writes=[wkey(3)])
                self.op(D_, lambda e, W=W, k=k: e.tensor_tensor(out=Hsh[:, k, 0, 1:Ns], in0=W[0][:, 0:n1], in1=W[1][:, 0:n1], op=ALU.subtract), reads=[wkey(0), wkey(1), hkey], writes=[hkey])
                self.op(D_, lambda e, W=W, k=k: e.tensor_tensor(out=Hsh[:, k, 1, 1:Ns], in0=W[2][:, 0:n1], in1=W[3][:, 0:n1], op=ALU.add), reads=[wkey(2), wkey(3), hkey], writes=[hkey])

        def stage2(q):
            u = ut[q % 3]
            ukey = ("ut", q % 3)
            Hsh = Hshs[q % 2]
            hkey = ("Hsh", q % 2)
            g = gq[q % 2]
            gkey = ("gq", q % 2)

            def mm(l):
                b = 2 + l % 2
                first = True
                for tau in range(l + 1):
                    self.op("pe", lambda e, tau=tau, first=first: e.matmul(self.ps[b][0:96, 0:Ns], lhsT=self.Kblk[:, q, tau, :], rhs=u[:, l - tau, :], start=first, stop=False),
                            reads=["Kblk", ukey], writes=[("ps", b)])
                    first = False
                for k in range(3):
                    for c in range(2):
                        last = (k == 2 and c == 1)
                        self.op("pe", lambda e, k=k, c=c, last=last: e.matmul(self.ps[b][32 * k:32 * k + 32, 0:Ns], lhsT=self.Wout[:, 3 * q + k, l, c, :], rhs=Hsh[:, k, c, :], start=False, stop=last, skip_group_check=True),
                                reads=["Wout", hkey], writes=[("ps", b)])

            def gA(l):
                b = 2 + l % 2
                yp = self.ps[b][0:96, 0:Ns]
                a2 = x2[l % 2]
                k2 = ("x2", l % 2)
                self.op("act", lambda e: e.activation(out=a2, in_=yp, func=AF.Square), reads=[("ps", b)], writes=[k2])
                self.op("dve", lambda e: e.tensor_scalar(out=a2, in0=a2, scalar1=0.044715, scalar2=1.0, op0=ALU.mult, op1=ALU.add), reads=[k2], writes=[k2])
                self.op("dve", lambda e: e.tensor_tensor(out=a2, in0=a2, in1=yp, op=ALU.mult), reads=[k2, ("ps", b)], writes=[k2])

            def gB(l):
                b = 2 + l % 2
                yp = self.ps[b][0:96, 0:Ns]
                a2, a3 = x2[l % 2], sg[l % 2]
                k2, k3 = ("x2", l % 2), ("sg", l % 2)
                self.op("act", lambda e: e.activation(out=a3, in_=a2, func=AF.Sigmoid, scale=1.5957691216057308), reads=[k2], writes=[k3])
                self.op("dve", lambda e: e.tensor_tensor(out=g_pm[:, l, :], in0=a3, in1=yp, op=ALU.mult), reads=[k3, ("ps", b)], writes=["g_pm"])

            for l in range(8):
                mm(l)
                gA(l)
                if l >= 1:
                    gB(l - 1)
            gB(7)
            self.op("pool", lambda e: e.tensor_copy(out=g.rearrange("p (s l) -> p s l", l=8), in_=g_pm.rearrange("p l s -> p s l")), reads=["g_pm"], writes=[gkey])
            self.dma("sp", self.gT_d[q], g, reads=[gkey], writes=[("dram", "gT")])

        load(0)
        load_tables(0)
        load(1)
        load_tables(1)
        stage1(0)
        for q in range(8):
            if q + 1 < 8:
                stage1(q + 1)
            stage2(q)
            if q + 2 < 8:
                load(q + 2)
                load_tables(q + 2)

    def load_wup(self, l, wup):
        for k in range(8):
            self.load_w(wup[:, k, :], self.I["w_ffn_up"][l, 128 * k:128 * k + 128, :], ("wup", k))

    def sweep_glu_wout0(self, after_weights=None):
        ar, I, S, NT = self.ar, self.I, self.S, self.NT
        wg = ar.alloc([8, 768], BF16, parts=96)
        woT = ar.alloc([8, D], BF16, parts=96)
        woM = ar.alloc([4, D], BF16, parts=64)
        self.load_w(wg.rearrange("p q n -> p (q n)") if False else wg[:, 0, :], I["w_glu"][0, 0:96, :], ("wg", 0))
        for q in range(1, 8):
            self.load_w(wg[:, q, :], I["w_glu"][0, 96 * q:96 * q + 96, :], ("wg", q))
        for q in range(8):
            self.load_w(woT[:, q, :], I["w_out"][0, 96 * q:96 * q + 96, :], ("woT", q))
        for h in range(4):
            self.load_w(woM[:, h, :], I["w_out"][0, 768 + 64 * h:768 + 64 * h + 64, :], ("woM", h))
        if after_weights is not None:
            after_weights()
        gt = [ar.alloc([8, 512], BF16, parts=96) for _ in range(2)]
        mo = [ar.alloc([4, 512], BF16, parts=64) for _ in range(2)]
        xt = [ar.alloc([4, D], F32) for _ in range(2)]
        tokT = ar.alloc([8, 512], BF16, parts=96)
        sig = [ar.alloc([512], F32, parts=96) for _ in range(2)]

        def load(t):
            s = t % 2
            self.dma("sp", gt[s], self.gT_d[:, :, 512 * t:512 * t + 512].rearrange("q p t -> p q t"), reads=[("dram", "gT")], writes=[("gt", s)])
            self.dma("sp", mo[s], self.memo_d[0][:, :, 512 * t:512 * t + 512].rearrange("h p t -> p h t"), reads=[("dram", "memo0")], writes=[("mo", s)])
            for i in range(4):
                self.dma("sp", xt[s][:, i, :], I["x"][512 * t + 128 * i:512 * t + 128 * i + 128, :], reads=[("dram", "hA", t)], writes=[("xt", s, i)])

        load(0)
        for t in range(NT):
            if t + 1 < NT:
                load(t + 1)
            s = t % 2
            for m in range(8):
                b = m % 2
                for q in range(8):
                    self.op("pe", lambda e, m=m, q=q, b=b, s=s: e.matmul(self.ps[b][0:96, :], lhsT=wg[:, q, 96 * m:96 * m + 96], rhs=gt[s][:, q, :], start=(q == 0), stop=(q == 7)),
                            reads=[("wg", q), ("gt", s)], writes=[("ps", b)])
                sgb = sig[m % 2]
                self.op("act", lambda e, b=b, sgb=sgb: e.activation(out=sgb, in_=self.ps[b][0:96, :], func=AF.Sigmoid), reads=[("ps", b)], writes=[("sig", m % 2)])
                self.op("dve", lambda e, m=m, s=s, sgb=sgb: e.tensor_tensor(out=tokT[:, m, :], in0=gt[s][:, m, :], in1=sgb, op=ALU.mult),
                        reads=[("sig", m % 2), ("gt", s)], writes=[("tokT", m)])
            self.wout_residual(xt[s], [("xt", s, i) for i in range(4)],
                               [(tokT[:, q, :], woT[:, q, :], [("tokT", q), ("woT", q)]) for q in range(8)] +
                               [(mo[s][0:64, h, :], woM[0:64, h, :], [("mo", s), ("woM", h)]) for h in range(4)])
            for i in range(4):
                self.dma("sp", self.hA[512 * t + 128 * i:512 * t + 128 * i + 128, :], xt[s][:, i, :], reads=[("xt", s, i)], writes=[("dram", "hA", t)])

    def wout_residual(self, ht, hkeys, terms):
        n = len(terms)
        for sub in range(4):
            for half in range(2):
                b = 2 + (2 * sub + half) % 2
                for ti, (aT, w, keys) in enumerate(terms):
                    self.op("pe", lambda e, aT=aT, w=w, sub=sub, half=half, b=b, ti=ti: e.matmul(self.ps[b][:, :], lhsT=aT[:, 128 * sub:128 * sub + 128], rhs=w[:, 512 * half:512 * half + 512], start=(ti == 0), stop=(ti == n - 1)),
                            reads=keys, writes=[("ps", b)])
                self.op("dve", lambda e, sub=sub, half=half, b=b: e.tensor_tensor(out=ht[:, sub, 512 * half:512 * half + 512], in0=self.ps[b][:, :], in1=ht[:, sub, 512 * half:512 * half + 512], op=ALU.add),
                        reads=[("ps", b), hkeys[sub]], writes=[hkeys[sub]])

    def sweep_ffn(self, l, hin, hout, final, wup=None):
        ar, I, S, NT = self.ar, self.I, self.S, self.NT
        if wup is None:
            wup = ar.alloc([8, 2 * DFF], BF16)
            self.load_wup(l, wup)
        wdn = ar.alloc([NJ, D], BF16)
        for j in range(NJ):
            self.load_w(wdn[:, j, :], I["w_ffn_down"][l, 128 * j:128 * j + 128, :], ("wdn", j))
        ht = ar.alloc([4, D], F32)
        self.alloc_norm_work()
        xnT = ar.alloc([8, 512], BF16)
        act = ar.alloc([NJ, 512], BF16)
        Gb = [ar.alloc([514], F32) for _ in range(2)]
        T1 = [ar.alloc([512], F32) for _ in range(2)]
        carry = ar.alloc([NJ, 2], F32)
        self.op("pool", lambda e: e.memset(carry, 0.0), writes=["carry"])
        cw, cb = self.cw[l], self.cb[l]
        gname = "gffn%d" % l
        if final:
            yss = ar.alloc([4], F32)
        for t in range(NT):
            for i in range(4):
                self.dma("sp", ht[:, i, :], hin[512 * t + 128 * i:512 * t + 128 * i + 128, :], reads=[("dram", "h_in", l, t)] + ([("dram", "hA", t)] if True else []), writes=[("ht", i)])
            self.norm_T(ht, [("ht", i) for i in range(4)], [(gname, xnT, "xnT")], "ffn")
            def chain_a(j):
                bg = (j % 3) * 2 + 1
                G, Tt = Gb[j % 2], T1[j % 2]
                kG, kT = ("Gb", j % 2), ("T1", j % 2)
                self.op("pool", lambda e: e.tensor_copy(out=G[:, 0:2], in_=carry[:, j, :]), reads=["carry", kG], writes=[kG])
                self.op("act", lambda e: e.activation(out=G[:, 2:514], in_=self.ps[bg][:, :], func=AF.Copy), reads=[("ps", bg), kG], writes=[kG])
                self.op("pool", lambda e: e.tensor_copy(out=carry[:, j, :], in_=G[:, 512:514]), reads=[kG, "carry"], writes=["carry"])
                self.op("act", lambda e: e.activation(out=Tt, in_=G[:, 2:514], func=AF.Identity, scale=cw[:, j, 2:3], bias=cb[:, j:j + 1]),
                        reads=[kG, ("cw", l), ("cb", l)], writes=[kT])
                self.op("dve", lambda e: e.scalar_tensor_tensor(out=Tt, in0=G[:, 1:513], scalar=cw[:, j, 1:2], in1=Tt, op0=ALU.mult, op1=ALU.add),
                        reads=[kG, kT, ("cw", l)], writes=[kT])
                self.op("dve", lambda e: e.scalar_tensor_tensor(out=Tt, in0=G[:, 0:512], scalar=cw[:, j, 0:1], in1=Tt, op0=ALU.mult, op1=ALU.add),
                        reads=[kG, kT, ("cw", l)], writes=[kT])

            def chain_b(j):
                ba = (j % 3) * 2
                Tt = T1[j % 2]
                kT = ("T1", j % 2)
                self.op("act", lambda e: e.activation(out=Tt, in_=Tt, func=AF.Silu), reads=[kT], writes=[kT])
                self.op("dve", lambda e: e.tensor_tensor(out=act[:, j, :], in0=Tt, in1=self.ps[ba][:, :], op=ALU.mult),
                        reads=[kT, ("ps", ba)], writes=[("act", j)])

            for j in range(NJ):
                ba, bg = (j % 3) * 2, (j % 3) * 2 + 1
                for (bb, off) in ((ba, 0), (bg, DFF)):
                    for k in range(8):
                        self.op("pe", lambda e, j=j, k=k, bb=bb, off=off: e.matmul(self.ps[bb][:, :], lhsT=wup[:, k, off + 128 * j:off + 128 * j + 128], rhs=xnT[:, k, :], start=(k == 0), stop=(k == 7)),
                                reads=[("wup", k), ("xnT", k)], writes=[("ps", bb)])
                chain_a(j)
                if not FFN_PIPE:
                    chain_b(j)
                elif j >= 1:
                    chain_b(j - 1)
            if FFN_PIPE:
                chain_b(NJ - 1)
            for sub in range(4):
                for half in range(2):
                    b = 6 + (2 * sub + half) % 2
                    for j in range(NJ):
                        self.op("pe", lambda e, j=j, sub=sub, half=half, b=b: e.matmul(self.ps[b][:, :], lhsT=act[:, j, 128 * sub:128 * sub + 128], rhs=wdn[:, j, 512 * half:512 * half + 512], start=(j == 0), stop=(j == NJ - 1)),
                                reads=[("act", j), ("wdn", j)], writes=[("ps", b)])
                    self.op("dve", lambda e, sub=sub, half=half, b=b: e.tensor_tensor(out=ht[:, sub, 512 * half:512 * half + 512], in0=self.ps[b][:, :], in1=ht[:, sub, 512 * half:512 * half + 512], op=ALU.add),
                            reads=[("ps", b), ("ht", sub)], writes=[("ht", sub)])
            if not final:
                for i in range(4):
                    self.dma("sp", hout[512 * t + 128 * i:512 * t + 128 * i + 128, :], ht[:, i, :], reads=[("ht", i)], writes=[("dram", "hB", t)])
            else:
                w = self.nt_work
                for i in range(4):
                    ss = yss[:, i:i + 1]
                    self.op("act", lambda e, i=i, ss=ss: e.activation(out=w["junk"], in_=ht[:, i, :], func=AF.Square, accum_out=ss), reads=[("ht", i)], writes=["njunk", ("yss", i)])
                    self.op("act", lambda e, ss=ss: e.activation(out=ss, in_=ss, func=AF.Sqrt, scale=1.0 / D, bias=EPS), reads=[("yss", i)], writes=[("yss", i)])
                    self.op("dve", lambda e, ss=ss: e.reciprocal(out=ss, in_=ss), reads=[("yss", i)], writes=[("yss", i)])
                    self.op("act", lambda e, i=i, ss=ss: e.activation(out=ht[:, i, :], in_=ht[:, i, :], func=AF.Copy, scale=ss), reads=[("ht", i), ("yss", i)], writes=[("ht", i)])
                    self.op("dve", lambda e, i=i: e.tensor_tensor(out=ht[:, i, :], in0=ht[:, i, :], in1=self.gF, op=ALU.mult), reads=[("ht", i), "gF"], writes=[("ht", i)])
                    self.dma("sp", self.y[512 * t + 128 * i:512 * t + 128 * i + 128, :], ht[:, i, :], reads=[("ht", i)], writes=[("dram", "y", t)])

    def sweep_kv_front1(self):
        ar, I, S, NT = self.ar, self.I, self.S, self.NT
        wkv = ar.alloc([8, 1536], BF16)
        wfg = ar.alloc([8, 12], BF16)
        win = ar.alloc([8, D], BF16)
        for k in range(8):
            self.load_w(wkv[:, k, :], I["w_kv"][128 * k:128 * k + 128, :], ("wkv", k))
            self.load_w(win[:, k, :], I["w_in"][1, 128 * k:128 * k + 128, :], ("win", k))
        self.dma("pool", wfg, I["w_fgate"].rearrange("(k p) h -> p k h", p=128), writes=["wfg"])
        ht = [ar.alloc([4, D], F32) for _ in range(2)]
        self.alloc_norm_work()
        hsTs = [ar.alloc([8, 512], BF16) for _ in range(2)]
        hnTs = [ar.alloc([8, 512], BF16) for _ in range(2)]
        kst = [ar.alloc([6, 512], BF16) for _ in range(2)]
        qst = [ar.alloc([6, 512], BF16) for _ in range(2)]
        vst = [ar.alloc([4, 768], BF16) for _ in range(2)]
        qmT = ar.alloc([2, 512], BF16)
        PTs = [ar.alloc([512], BF16) for _ in range(2)]
        self.ma_rec = ar.alloc([512], F32)
        mo = [ar.alloc([4, 512], BF16, parts=64) for _ in range(2)]
        ex = ar.alloc([512], F32)

        def load(t):
            for i in range(4):
                self.dma("sp", ht[t % 2][:, i, :], self.hB[512 * t + 128 * i:512 * t + 128 * i + 128, :], reads=[("dram", "hB", t)], writes=[("ht", t % 2, i)])

        load(0)
        self.norm_T(ht[0], [("ht", 0, i) for i in range(4)], [("gkv", hsTs[0], ("hsT", 0)), ("gmix1", hnTs[0], ("hnT", 0))], "kv")
        for t in range(NT):
            if t + 1 < NT:
                load(t + 1)
            s = t % 2
            hsT, hnT = hsTs[s], hnTs[s]
            ks_, kn_ = ("hsT", s), ("hnT", s)
            for m in range(6):
                b = m % 2
                for k in range(8):
                    self.op("pe", lambda e, m=m, k=k, b=b, hsT=hsT: e.matmul(self.ps[b][:, :], lhsT=wkv[:, k, 128 * m:128 * m + 128], rhs=hsT[:, k, :], start=(k == 0), stop=(k == 7)),
                            reads=[("wkv", k), (ks_, k)], writes=[("ps", b)])
                self.op("act", lambda e, m=m, b=b, s=s: e.activation(out=kst[s][:, m, :], in_=self.ps[b][:, :], func=AF.Copy), reads=[("ps", b)], writes=[("kst", s)])
            self.dma("sp", self.kT_d[:, 512 * t:512 * t + 512].rearrange("(m p) t -> p m t", p=128), kst[s], reads=[("kst", s)], writes=[("dram", "kT")])
            for sub in range(4):
                for half in range(2):
                    b = 2 + (2 * sub + half) % 2
                    for k in range(8):
                        self.op("pe", lambda e, sub=sub, half=half, k=k, b=b, hsT=hsT: e.matmul(self.ps[b][:, 0:384], lhsT=hsT[:, k, 128 * sub:128 * sub + 128], rhs=wkv[:, k, 768 + 384 * half:768 + 384 * half + 384], start=(k == 0), stop=(k == 7)),
                                reads=[("wkv", k), (ks_, k)], writes=[("ps", b)])
                    self.op("dve", lambda e, sub=sub, half=half, b=b, s=s: e.tensor_copy(out=vst[s][:, sub, 384 * half:384 * half + 384], in_=self.ps[b][:, 0:384]), reads=[("ps", b)], writes=[("vst", s)])
            self.dma("sp", self.v_d[512 * t:512 * t + 512, :].rearrange("(i p) d -> p i d", p=128), vst[s], reads=[("vst", s)], writes=[("dram", "v")])
            if t + 1 < NT:
                self.norm_T(ht[1 - s], [("ht", 1 - s, i) for i in range(4)], [("gkv", hsTs[1 - s], ("hsT", 1 - s)), ("gmix1", hnTs[1 - s], ("hnT", 1 - s))], "kv")
            b = 4
            for k in range(8):
                self.op("pe", lambda e, k=k, b=b, hsT=hsT: e.matmul(self.ps[b][0:12, :], lhsT=wfg[:, k, :], rhs=hsT[:, k, :], start=(k == 0), stop=(k == 7)),
                        reads=["wfg", (ks_, k)], writes=[("ps", b)])
            self.op("act", lambda e, b=b: e.activation(out=ex[0:12, :], in_=self.ps[b][0:12, :], func=AF.Exp, scale=-1.0, bias=self.nbf[0:12, :]), reads=[("ps", b), "nbf"], writes=["ex"])
            self.op("act", lambda e, t=t: e.activation(out=self.cs[0:12, 512 * t:512 * t + 512], in_=ex[0:12, :], func=AF.Ln, bias=1.0), reads=["ex"], writes=["cs"])
            for m in range(6):
                b = m % 2
                for k in range(8):
                    self.op("pe", lambda e, m=m, k=k, b=b, hnT=hnT: e.matmul(self.ps[b][:, :], lhsT=win[:, k, 128 * m:128 * m + 128], rhs=hnT[:, k, :], start=(k == 0), stop=(k == 7)),
                            reads=[("win", k), (kn_, k)], writes=[("ps", b)])
                self.op("act", lambda e, m=m, b=b, s=s: e.activation(out=qst[s][:, m, :], in_=self.ps[b][:, :], func=AF.Copy, scale=0.125), reads=[("ps", b)], writes=[("qst", s)])
            self.dma("sp", self.qT_d[:, 512 * t:512 * t + 512].rearrange("(m p) t -> p m t", p=128), qst[s], reads=[("qst", s)], writes=[("dram", "qT")])
            for c in range(2):
                b = c % 2
                for k in range(8):
                    self.op("pe", lambda e, c=c, k=k, b=b, hnT=hnT: e.matmul(self.ps[b][:, :], lhsT=win[:, k, 768 + 128 * c:768 + 128 * c + 128], rhs=hnT[:, k, :], start=(k == 0), stop=(k == 7)),
                            reads=[("win", k), (kn_, k)], writes=[("ps", b)])
                self.op("act", lambda e, c=c, b=b: e.activation(out=qmT[:, c, :], in_=self.ps[b][:, :], func=AF.Copy, scale=0.125), reads=[("ps", b)], writes=["qmT"])
            self.mem_attention(1, qmT, "qmT", mo[s], ("mo", s), PTs, "m1")
            self.dma("sp", self.memo_d[1][:, :, 512 * t:512 * t + 512].rearrange("h p t -> p h t"), mo[s], reads=[("mo", s)], writes=[("dram", "memo1")])

    def phase_fox(self):
        ar, S, NT, NB = self.ar, self.S, self.NT, self.NB
        cs = self.cs
        ones12 = ar.alloc([S], F32)
        self.op("pool", lambda e: e.memset(ones12[0:12, :], 1.0), writes=["ones12"])
        cum = ar.alloc([S], F32)
        self.op("dve", lambda e: e.tensor_tensor_scan(out=cum[0:12, :], data0=ones12[0:12, :], data1=cs[0:12, :], initial=0.0, op0=ALU.mult, op1=ALU.add),
                reads=["cs", "ones12"], writes=["cum"])
        csKP = ar.alloc([NB, 12], F32)
        CL = ar.alloc([NB, 12], F32)
        CM = ar.alloc([NB, 12], F32)
        DL = ar.alloc([NB, 12], F32)
        for j in range(NB):
            self.op("pe", lambda e, j=j: e.transpose(out=self.ps[0][:, 12 * j:12 * j + 12], in_=cum[0:12, 128 * j:128 * j + 128], identity=self.ident_f[0:12, 0:12]),
                    reads=["cum", "c_ident_f"], writes=[("ps", 0)])
        self.op("act", lambda e: e.activation(out=csKP.rearrange("p j h -> p (j h)"), in_=self.ps[0][:, 0:12 * NB], func=AF.Copy), reads=[("ps", 0)], writes=["csKP"])
        for (sel, dst, kd, b) in ((self.sel127, CL, "CL", 1), (self.sel63, CM, "CM", 2)):
            self.op("pe", lambda e, sel=sel, b=b: e.matmul(self.ps[b][:, 0:12 * NB], lhsT=sel, rhs=csKP.rearrange("p j h -> p (j h)"), start=True, stop=True),
                    reads=["csKP", "c_sel127", "c_sel63"], writes=[("ps", b)])
            self.op("act", lambda e, dst=dst, b=b: e.activation(out=dst.rearrange("p j h -> p (j h)"), in_=self.ps[b][:, 0:12 * NB], func=AF.Copy), reads=[("ps", b)], writes=[kd])
        for Ig in range(NT):
            self.op("dve", lambda e, Ig=Ig: e.tensor_tensor(out=DL[:, 4 * Ig:4 * Ig + 4, :], in0=CL[:, 4 * Ig + 1:4 * Ig + 2, :].to_broadcast([128, 4, 12]), in1=CM[:, 4 * Ig:4 * Ig + 4, :], op=ALU.subtract),
                    reads=["CL", "CM"], writes=["DL"])
        qa = [ar.alloc([S], BF16) for _ in range(2)]
        ka = [ar.alloc([S], BF16) for _ in range(2)]
        Vh = [ar.alloc([NB, 128], BF16) for _ in range(2)]
        oh = [ar.alloc([S], BF16) for _ in range(2)]
        NPT = 4
        PT = [ar.alloc([512], BF16) for _ in range(NPT)]
        BIAS = [ar.alloc([NB], F32) for _ in range(2)]
        rcs = [ar.alloc([512], F32, parts=1) for _ in range(2)]
        bcs = [ar.alloc([512], F32) for _ in range(2)]
        rch = [ar.alloc([512], BF16, parts=1) for _ in range(2)]
        rcl = [ar.alloc([512], BF16, parts=1) for _ in range(2)]
        onesb = ar.alloc([128], BF16, parts=1)
        self.op("pool", lambda e: e.memset(onesb, 1.0), writes=["onesb"])
        for s in range(2):
            self.op("pool", lambda e, s=s: e.memset(ka[s][64:65, :], 1.0), writes=[("ka1", s)])
            self.op("pool", lambda e, s=s: e.memset(Vh[s][:, :, 0:64], 0.0), writes=[("Vh1", s)])
            self.op("pool", lambda e, s=s: e.memset(Vh[s][:, :, 0:1], 1.0), reads=[("Vh1", s)], writes=[("Vh1", s)])

        def load(h):
            s = h % 2
            self.dma("sp", qa[s][0:64, :], self.qT_d[64 * h:64 * h + 64, :], reads=[("dram", "qT")], writes=[("qa", s)])
            self.dma("sp", ka[s][0:64, :], self.kT_d[64 * h:64 * h + 64, :], reads=[("dram", "kT")], writes=[("ka", s)])
            self.dma("sp", Vh[s][:, :, 64:128], self.v_d[:, 64 * h:64 * h + 64].rearrange("(j p) d -> p j d", p=128), reads=[("dram", "v")], writes=[("Vh", s)])
            self.op("dve", lambda e, s=s, h=h: e.tensor_copy(out=qa[s][64:65, :].rearrange("p (j t) -> p j t", t=128), in_=DL[64:65, :, h:h + 1].to_broadcast([1, NB, 128])),
                    reads=["DL", ("qa1", s)], writes=[("qa1", s)])

        its = []
        for h in range(12):
            for Ig in range(NT):
                for j in range(4 * Ig + 4):
                    its.append((h, Ig, j, 4 * Ig + 4))
        NSB = 3

        def group_pre(h, Ig):
            s = h % 2
            g = h * NT + Ig
            bi = BIAS[g % 2]
            nj = 4 * Ig + 4
            self.op("dve", lambda e, bi=bi, nj=nj, Ig=Ig, h=h: e.tensor_scalar(out=bi[:, 0:nj], in0=csKP[:, 0:nj, h], scalar1=CL[:, 4 * Ig + 1, h:h + 1], scalar2=None, op0=ALU.subtract),
                    reads=["csKP", "CL"], writes=[("BIAS", g % 2)])

        def emit_st(idx):
            h, Ig, j, nj = its[idx]
            s = h % 2
            if j == 0:
                group_pre(h, Ig)
            c0 = max(0, 128 * (j - 4 * Ig))
            bs = idx % NSB
            self.op("pe", lambda e, s=s, j=j, Ig=Ig, c0=c0, bs=bs: e.matmul(self.ps[bs][:, c0:512], lhsT=ka[s][0:65, 128 * j:128 * j + 128], rhs=qa[s][0:65, 512 * Ig + c0:512 * Ig + 512], start=True, stop=True),
                    reads=[("ka", s), ("ka1", s), ("qa", s), ("qa1", s)], writes=[("ps", bs)])

        def emit_exp_pv(idx):
            h, Ig, j, nj = its[idx]
            s = h % 2
            g = h * NT + Ig
            a = j - 4 * Ig
            c0 = max(0, 128 * a)
            bs = idx % NSB
            pt = PT[idx % NPT]
            kpt = ("PTf", idx % NPT)
            bi = BIAS[g % 2]
            po = self.ps[4 + g % 2]
            ko = ("ps", 4 + g % 2)
            self.op("act", lambda e, pt=pt, bs=bs, c0=c0, bi=bi, j=j: e.activation(out=pt[:, c0:512], in_=self.ps[bs][:, c0:512], func=AF.Exp, bias=bi[:, j:j + 1]),
                    reads=[("ps", bs), ("BIAS", g % 2)], writes=[kpt])
            if a >= 0:
                self.op("pool", lambda e, pt=pt, c0=c0: e.tensor_tensor(out=pt[:, c0:c0 + 128], in0=pt[:, c0:c0 + 128], in1=self.mask, op=ALU.mult),
                        reads=[kpt, "c_mask"], writes=[kpt])
            self.op("pe", lambda e, s=s, j=j, pt=pt, c0=c0, po=po, nj=nj: e.matmul(po[:, c0:512], lhsT=Vh[s][:, j, :], rhs=pt[:, c0:512], start=(j == 0), stop=(j == nj - 1), skip_group_check=True),
                    reads=[("Vh", s), ("Vh1", s), kpt], writes=[ko])

        def tail_a(g):
            po = self.ps[4 + g % 2]
            ko = ("ps", 4 + g % 2)
            rc, rh, rl = rcs[g % 2], rch[g % 2], rcl[g % 2]
            self.op("dve", lambda e: e.reciprocal(out=rc[0:1, :], in_=po[0:1, :]), reads=[ko], writes=[("rcs", g % 2)])
            self.op("dve", lambda e: e.tensor_copy(out=rh[0:1, :], in_=rc[0:1, :]), reads=[("rcs", g % 2)], writes=[("rch", g % 2)])
            self.op("dve", lambda e: e.tensor_tensor(out=rl[0:1, :], in0=rc[0:1, :], in1=rh[0:1, :], op=ALU.subtract), reads=[("rcs", g % 2), ("rch", g % 2)], writes=[("rcl", g % 2)])

        def tail(g):
            h, Ig = g // NT, g % NT
            s = h % 2
            po = self.ps[4 + g % 2]
            ko = ("ps", 4 + g % 2)
            rc, bc = rcs[g % 2], bcs[g % 2]
            rh, rl = rch[g % 2], rcl[g % 2]
            self.op("pe", lambda e: e.matmul(self.ps[6][:, :], lhsT=onesb[0:1, 0:128], rhs=rh[0:1, :], start=True, stop=False),
                    reads=["onesb", ("rch", g % 2)], writes=[("ps", 6)])
            self.op("pe", lambda e: e.matmul(self.ps[6][:, :], lhsT=onesb[0:1, 0:128], rhs=rl[0:1, :], start=False, stop=True),
                    reads=["onesb", ("rcl", g % 2)], writes=[("ps", 6)])
            self.op("act", lambda e, bc=bc: e.activation(out=bc[64:128, :], in_=self.ps[6][64:128, :], func=AF.Copy), reads=[("ps", 6)], writes=[("bcs", g % 2)])
            self.op("dve", lambda e, bc=bc, po=po, s=s, Ig=Ig: e.tensor_tensor(out=oh[s][64:128, 512 * Ig:512 * Ig + 512], in0=po[64:128, :], in1=bc[64:128, :], op=ALU.mult),
                    reads=[ko, ("bcs", g % 2)], writes=[("oh", s)])
            if Ig == NT - 1:
                self.dma("sp", self.oT_d[64 * h:64 * h + 64, :], oh[s][64:128, :], reads=[("oh", s)], writes=[("dram", "oT")])
            if self.debug and h == 2:
                self.dma("sp", self.dbg_rc[Ig:Ig + 1, :], rc[0:1, :], reads=[("rcs", g % 2)], writes=[("dram", "dbg")])
                if Ig == NT - 1:
                    self.dma("sp", self.dbg_qa1, qa[s][64:65, :], reads=[("qa1", s)], writes=[("dram", "dbg3")])
                    self.dma("sp", self.dbg_ka1, ka[s][64:65, :], reads=[("ka1", s)], writes=[("dram", "dbg4")])
                    self.dma("sp", self.dbg_v1, Vh[s][:, :, 0:2], reads=[("Vh1", s), ("Vh", s)], writes=[("dram", "dbg5")])

        load(0)
        LOOK = 2
        p_st = 0
        n_it = len(its)
        pending_tail = None
        for idx in range(n_it):
            while p_st < n_it and p_st <= idx + LOOK:
                emit_st(p_st)
                p_st += 1
            h, Ig, j, nj = its[idx]
            g = h * NT + Ig
            if j == 0 and Ig == 0 and h + 1 < 12:
                load(h + 1)
            emit_exp_pv(idx)
            if pending_tail is not None and j == min(6, nj - 1):
                tail(pending_tail)
                pending_tail = None
            if j == nj - 1:
                if pending_tail is not None:
                    tail(pending_tail)
                tail_a(g)
                pending_tail = g
        if pending_tail is not None:
            tail(pending_tail)

    def sweep_wout1(self, after_weights=None):
        ar, I, S, NT = self.ar, self.I, self.S, self.NT
        woT = ar.alloc([6, D], BF16)
        woM = ar.alloc([4, D], BF16, parts=64)
        for q in range(6):
            self.load_w(woT[:, q, :], I["w_out"][1, 128 * q:128 * q + 128, :], ("woT", q))
        for h in range(4):
            self.load_w(woM[:, h, :], I["w_out"][1, 768 + 64 * h:768 + 64 * h + 64, :], ("woM", h))
        if after_weights is not None:
            after_weights()
        ot = [ar.alloc([6, 512], BF16) for _ in range(2)]
        mo = [ar.alloc([4, 512], BF16, parts=64) for _ in range(2)]
        ht = [ar.alloc([4, D], F32) for _ in range(2)]

        def load(t):
            s = t % 2
            self.dma("sp", ot[s], self.oT_d[:, 512 * t:512 * t + 512].rearrange("(q p) t -> p q t", p=128), reads=[("dram", "oT")], writes=[("ot", s)])
            self.dma("sp", mo[s], self.memo_d[1][:, :, 512 * t:512 * t + 512].rearrange("h p t -> p h t"), reads=[("dram", "memo1")], writes=[("mo", s)])
            for i in range(4):
                self.dma("sp", ht[s][:, i, :], self.hB[512 * t + 128 * i:512 * t + 128 * i + 128, :], reads=[("dram", "hB", t), ("dram", "hA", t)], writes=[("ht", s, i)])

        load(0)
        for t in range(NT):
            if t + 1 < NT:
                load(t + 1)
            s = t % 2
            self.wout_residual(ht[s], [("ht", s, i) for i in range(4)],
                               [(ot[s][:, q, :], woT[:, q, :], [("ot", s), ("woT", q)]) for q in range(6)] +
                               [(mo[s][0:64, h, :], woM[0:64, h, :], [("mo", s), ("woM", h)]) for h in range(4)])
            for i in range(4):
                self.dma("sp", self.hA[512 * t + 128 * i:512 * t + 128 * i + 128, :], ht[s][:, i, :], reads=[("ht", s, i)], writes=[("dram", "hA", t)])


_CACHE = {}


def _get_nc(S, debug=False):
    key = (S, debug)
    if key not in _CACHE:
        _CACHE[key] = Builder(S, debug).build()
    return _CACHE[key]


def kernel(**inputs):
    x = np.asarray(inputs["x"], np.float32)
    mem = np.asarray(inputs["mem"], np.float32)
    B, S, _ = x.shape
    nc = _get_nc(S)
    consts = host_consts()
    shared = {k: np.ascontiguousarray(np.asarray(v, np.float32)) for k, v in inputs.items() if k not in ("x", "mem")}
    in_maps = []
    for b in range(B):
        m = dict(shared)
        m.update(consts)
        m["x"] = np.ascontiguousarray(x[b])
        m["mem"] = np.ascontiguousarray(mem[b])
        in_maps.append(m)
    res = run_bass_kernel_spmd(nc, in_maps, core_ids=list(range(B)))
    return np.stack([np.asarray(r["y"], np.float32) for r in res.results], axis=0)
```

```python
import numpy as np
import ml_dtypes
from contextlib import ExitStack
import concourse.bass as bass
import concourse.mybir as mybir
from concourse.bass_utils import run_bass_kernel_spmd

F32 = mybir.dt.float32
BF16 = mybir.dt.bfloat16
AF = mybir.ActivationFunctionType
ALU = mybir.AluOpType

D = 1024
DFF = 2816
NJ = 22
MEM = 256
EPS = 1e-6
FFN_PIPE = True


class _Op:
    __slots__ = ("stream", "fn", "is_dma", "idx", "slot", "cnt", "waits", "clock")


class Sched:
    STREAMS = ("pe", "act", "dve", "pool", "sp")

    def __init__(self, nc, n_dma_sems=40):
        self.nc = nc
        self.ops = []
        self.n_dma = n_dma_sems
        self.dma_rr = 0
        self.dma_cnt = [0] * n_dma_sems
        self.last_writer = {}
        self.readers = {}
        self.know = {s: {} for s in self.STREAMS}
        self.count = {s: 0 for s in self.STREAMS}
        self.pending = {s: [] for s in self.STREAMS}

    def barrier(self):
        tgt = {}
        for s in ("pe", "act", "dve", "pool"):
            if self.count[s] > 0:
                tgt[s] = self.count[s]
        for i in range(self.n_dma):
            if self.dma_cnt[i] > 0:
                tgt[("d", i)] = self.dma_cnt[i]
        for s in self.STREAMS:
            know = self.know[s]
            for k, v in tgt.items():
                if k == s == "pe":
                    continue
                if know.get(k, 0) < v:
                    know[k] = v
                    self.pending[s].append((k, v))
        self.last_writer = {k: w for k, w in self.last_writer.items() if isinstance(k, tuple) and k and k[0] == "dram"}
        self.readers = {k: r for k, r in self.readers.items() if k in self.last_writer}

    def add(self, stream, fn, reads=(), writes=(), dma=False):
        op = _Op()
        op.stream = stream
        op.fn = fn
        op.is_dma = dma
        reads = list(reads)
        writes = list(writes)
        deps = []
        for k in reads + writes:
            w = self.last_writer.get(k)
            if w is not None:
                deps.append(w)
        for k in writes:
            deps.extend(self.readers.get(k, ()))
        know = self.know[stream]
        waits = self.pending[stream]
        self.pending[stream] = []

        def need(key, val, clock):
            if know.get(key, 0) >= val:
                return
            waits.append((key, val))
            if clock is not None:
                for kk, vv in clock.items():
                    if know.get(kk, 0) < vv:
                        know[kk] = vv
            know[key] = max(know.get(key, 0), val)

        if dma:
            slot = self.dma_rr
            self.dma_rr = (self.dma_rr + 1) % self.n_dma
            prev = self.dma_cnt[slot]
            if prev > 0:
                need(("d", slot), prev, None)
            self.dma_cnt[slot] = prev + 1
            op.slot = slot
            op.cnt = prev + 1
        for p in deps:
            if p.is_dma:
                need(("d", p.slot), p.cnt, p.clock)
            else:
                if p.stream == "pe" and stream == "pe" and not dma:
                    continue
                need(p.stream, p.idx, p.clock)
        op.waits = waits
        op.clock = dict(know)
        if dma:
            op.idx = None
            op.clock[("d", op.slot)] = op.cnt
        else:
            self.count[stream] += 1
            op.idx = self.count[stream]
            op.clock[stream] = op.idx
        for k in reads:
            self.readers.setdefault(k, []).append(op)
        for k in writes:
            self.last_writer[k] = op
            self.readers[k] = []
        self.ops.append(op)
        return op

    def emit(self, es, final_wait_keys=()):
        nc = self.nc
        sems = {}
        for s in ("pe", "act", "dve", "pool"):
            sems[s] = es.enter_context(nc.semaphore("c_" + s))
        for i in range(self.n_dma):
            sems[("d", i)] = es.enter_context(nc.semaphore("d_%d" % i))
        fin = list(self.pending["sp"])
        know = self.know["sp"]
        for k in final_wait_keys:
            w = self.last_writer.get(k)
            if w is None:
                continue
            key, val = (("d", w.slot), w.cnt) if w.is_dma else (w.stream, w.idx)
            if know.get(key, 0) < val:
                know[key] = val
                fin.append((key, val))
        block = es.enter_context(nc.Block())
        by = {s: [o for o in self.ops if o.stream == s] for s in self.STREAMS}

        def run(eng, stream, extra=()):
            for o in by[stream]:
                for key, val in o.waits:
                    eng.wait_ge(sems[key], val * (16 if isinstance(key, tuple) else 1))
                ins = o.fn(eng)
                if o.is_dma:
                    ins.then_inc(sems[("d", o.slot)], 16)
                else:
                    ins.then_inc(sems[stream], 1)
            for key, val in extra:
                eng.wait_ge(sems[key], val * (16 if isinstance(key, tuple) else 1))

        @block.tensor
        def _(e):
            run(e, "pe")

        @block.scalar
        def _(e):
            run(e, "act")

        @block.vector
        def _(e):
            run(e, "dve")

        @block.gpsimd
        def _(e):
            run(e, "pool")

        @block.sync
        def _(e):
            run(e, "sp", fin)


class Arena:
    def __init__(self, nc, es, words):
        self.t = es.enter_context(nc.sbuf_tensor("arena", [128, words], F32))
        self.off = 0
        self.words = words

    def mark(self):
        return self.off

    def release(self, m):
        self.off = m

    def alloc(self, shape, dtype=F32, parts=128):
        shape = [int(s) for s in shape]
        n = int(np.prod(shape))
        sz = 4 if dtype == F32 else 2
        words = (n * sz + 3) // 4
        words = (words + 7) // 8 * 8
        assert self.off + words <= self.words, "arena overflow %d + %d > %d" % (self.off, words, self.words)
        ap = self.t[0:parts, self.off:self.off + words]
        self.off += words
        if dtype != F32:
            ap = ap.bitcast(dtype)
        ap = ap[:, 0:n]
        if len(shape) > 1:
            names = " ".join("d%d" % i for i in range(len(shape)))
            kw = {"d%d" % i: s for i, s in enumerate(shape)}
            ap = ap.rearrange("p (%s) -> p %s" % (names, names), **kw)
        return ap


def host_consts():
    c = {}
    c["c_ident_bf"] = np.eye(128).astype(ml_dtypes.bfloat16)
    c["c_ident_f"] = np.eye(128).astype(np.float32)
    m = (np.arange(128)[None, :] >= np.arange(128)[:, None]).astype(np.float32)
    c["c_mask"] = m.astype(ml_dtypes.bfloat16)
    c["c_ones_bf"] = np.ones((128, 128), ml_dtypes.bfloat16)
    s127 = np.zeros((128, 128), np.float32)
    s127[127, :] = 1
    s63 = np.zeros((128, 128), np.float32)
    s63[63, :] = 1
    c["c_sel127"] = s127
    c["c_sel63"] = s63
    return c


class Builder:
    def __init__(self, S=4096, debug=False):
        self.S = S
        self.NT = S // 512
        self.Ns = S // 8
        self.NB = S // 128
        self.debug = debug
        self.nc = bass.Bass("TRN2", target_bir_lowering=False)
        self.es = ExitStack()
        self.uid = 0

    def din(self, name, shape, dt=F32):
        return self.nc.dram_tensor(name, list(shape), dt, kind="ExternalInput").ap()

    def dscr(self, name, shape, dt):
        kind = "ExternalOutput" if self.debug else "Internal"
        return self.nc.dram_tensor(name, list(shape), dt, kind=kind).ap()

    def op(self, stream, fn, reads=(), writes=()):
        return self.sc.add(stream, fn, reads, writes)

    def dma(self, stream, out, in_, reads=(), writes=(), slow=False):
        if slow:
            return self.sc.add(stream, lambda e: e.dma_start(out=out, in_=in_, allow_slow_non_contiguous=True), reads, writes, dma=True)
        return self.sc.add(stream, lambda e: e.dma_start(out=out, in_=in_), reads, writes, dma=True)

    def load_w(self, dst, src, key):
        cols = src.shape[-1]
        step = 2048
        c = 0
        while c < cols:
            n = min(step, cols - c)
            self.dma("pool", dst[:, c:c + n], src[:, c:c + n], writes=[key])
            c += n

    def new_phase(self, mark):
        self.sc.barrier()
        self.ar.release(mark)

    def build(self):
        nc, es = self.nc, self.es
        S, NT, Ns, NB = self.S, self.NT, self.Ns, self.NB
        with es:
            self.sc = Sched(nc)
            self.ar = Arena(nc, es, 53000)
            self.ps = [es.enter_context(nc.psum_tensor("ps%d" % i, [128, 512], F32)) for i in range(8)]
            I = {}
            I["x"] = self.din("x", [S, D])
            I["mem"] = self.din("mem", [MEM, D])
            for nm, shp in [("g_mix", [2, D]), ("w_in", [2, D, D]), ("w_out", [2, D, D]), ("g_mem", [D]),
                            ("w_mem_kv", [2, D, 512]), ("s5_a_re", [1, 48, 64]), ("s5_a_im", [1, 48, 64]),
                            ("s5_log_dt", [1, 48]), ("s5_b_re", [1, 48, 64, 16]), ("s5_b_im", [1, 48, 64, 16]),
                            ("s5_c_re", [1, 48, 16, 64]), ("s5_c_im", [1, 48, 16, 64]), ("s5_d", [1, 768]),
                            ("w_glu", [1, 768, 768]), ("g_kv", [D]), ("w_kv", [D, 1536]), ("w_fgate", [D, 12]),
                            ("b_fgate", [12]), ("g_ffn", [2, D]), ("w_ffn_up", [2, D, 2 * DFF]), ("conv_w", [2, 3, DFF]),
                            ("conv_b", [2, DFF]), ("w_ffn_down", [2, DFF, D]), ("g_final", [D])]:
                I[nm] = self.din(nm, shp)
            I["c_ident_bf"] = self.din("c_ident_bf", [128, 128], BF16)
            I["c_ident_f"] = self.din("c_ident_f", [128, 128], F32)
            I["c_mask"] = self.din("c_mask", [128, 128], BF16)
            I["c_ones_bf"] = self.din("c_ones_bf", [128, 128], BF16)
            I["c_sel127"] = self.din("c_sel127", [128, 128], F32)
            I["c_sel63"] = self.din("c_sel63", [128, 128], F32)
            self.I = I
            self.y = nc.dram_tensor("y", [S, D], F32, kind="ExternalOutput").ap()
            self.uT_d = self.dscr("uT_d", [8, 96, 8, Ns], BF16)
            self.gT_d = self.dscr("gT_d", [8, 96, S], BF16)
            self.memo_d = [self.dscr("memo_d%d" % l, [4, 64, S], BF16) for l in range(2)]
            self.hA = self.dscr("hA", [S, D], F32)
            self.hB = self.dscr("hB", [S, D], F32)
            self.kT_d = self.dscr("kT_d", [768, S], BF16)
            self.qT_d = self.dscr("qT_d", [768, S], BF16)
            self.v_d = self.dscr("v_d", [S, 768], BF16)
            self.oT_d = self.dscr("oT_d", [768, S], BF16)
            self.E_d = self.dscr("E_d", [8, 2, 128, 3 * Ns], F32)

            if self.debug:
                self.dbg_rc = self.dscr("dbg_rc", [NT, 512], F32)
                self.dbg_bc = self.dscr("dbg_bc", [NT, 65, 512], F32)
                self.dbg_qa1 = self.dscr("dbg_qa1", [1, S], BF16)
                self.dbg_ka1 = self.dscr("dbg_ka1", [1, S], BF16)
                self.dbg_v1 = self.dscr("dbg_v1", [128, NB, 2], BF16)
            self.setup_persistent()
            m0 = self.ar.mark()
            self.s5_prep()
            m1 = self.ar.mark()
            self.phase_front0()
            self.new_phase(m1)
            self.phase_s5()
            self.new_phase(m0)
            wup0 = self.ar.alloc([8, 2 * DFF], BF16)
            mA = self.ar.mark()
            self.sweep_glu_wout0(after_weights=lambda: self.load_wup(0, wup0))
            self.new_phase(mA)
            self.sweep_ffn(0, self.hA, self.hB, final=False, wup=wup0)
            self.new_phase(m0)
            self.cs = self.ar.alloc([S], F32)
            m2 = self.ar.mark()
            self.sweep_kv_front1()
            self.new_phase(m2)
            self.phase_fox()
            self.new_phase(m0)
            wup1 = self.ar.alloc([8, 2 * DFF], BF16)
            mB = self.ar.mark()
            self.sweep_wout1(after_weights=lambda: self.load_wup(1, wup1))
            self.new_phase(mB)
            self.sweep_ffn(1, self.hA, None, final=True, wup=wup1)
            self.sc.emit(es, final_wait_keys=[("dram", "y", t) for t in range(NT)])
        return nc

    def setup_persistent(self):
        ar, I = self.ar, self.I
        self.ident_bf = ar.alloc([128], BF16)
        self.ident_f = ar.alloc([128], F32)
        self.mask = ar.alloc([128], BF16)
        self.ones_bf = ar.alloc([128], BF16)
        self.sel127 = ar.alloc([128], F32)
        self.sel63 = ar.alloc([128], F32)
        for nm, dst in [("c_ident_bf", self.ident_bf), ("c_ident_f", self.ident_f), ("c_mask", self.mask),
                        ("c_ones_bf", self.ones_bf), ("c_sel127", self.sel127), ("c_sel63", self.sel63)]:
            self.dma("sp", dst, I[nm], writes=[nm])
        self.gcol = {}
        self.late = []
        for nm, src in [("gmix0", I["g_mix"][0]), ("gmem", I["g_mem"]), ("gmix1", I["g_mix"][1]), ("gffn0", I["g_ffn"][0]),
                        ("gffn1", I["g_ffn"][1]), ("gkv", I["g_kv"])]:
            t = ar.alloc([8], F32)
            f = lambda t=t, src=src, nm=nm: self.dma("sp", t, src.rearrange("(k p) -> p k", p=128), writes=[nm], slow=True)
            if nm in ("gmix0", "gmem"):
                f()
            else:
                self.late.append(f)
            self.gcol[nm] = t
        self.gF = ar.alloc([D], F32)
        self.late.append(lambda: self.dma("sp", self.gF, I["g_final"].partition_broadcast(128), writes=["gF"]))
        self.cw = []
        self.cb = []
        for l in range(2):
            t = ar.alloc([NJ, 3], F32)
            for kk in range(3):
                self.late.append(lambda t=t, l=l, kk=kk: self.dma("sp", t[:, :, kk], I["conv_w"][l, kk].rearrange("(j p) -> p j", p=128), reads=[("cw", l)], writes=[("cw", l)], slow=True))
            self.cw.append(t)
            t2 = ar.alloc([NJ], F32)
            self.late.append(lambda t2=t2, l=l: self.dma("sp", t2, I["conv_b"][l].rearrange("(j p) -> p j", p=128), writes=[("cb", l)], slow=True))
            self.cb.append(t2)
        self.nbf = ar.alloc([1], F32)
        self.late.append(lambda: self.dma("sp", self.nbf[0:12, :], I["b_fgate"].rearrange("(p o) -> p o", o=1), writes=["nbf"], slow=True))
        self.late.append(lambda: self.op("dve", lambda e: e.tensor_scalar(out=self.nbf[0:12, :], in0=self.nbf[0:12, :], scalar1=-1.0, scalar2=None, op0=ALU.mult),
                                         reads=["nbf"], writes=["nbf"]))
        self.zrow = ar.alloc([128], BF16)
        self.op("dve", lambda e: e.memset(self.zrow, 0.0), writes=["zrow"])
        self.KTm = ar.alloc([2, 2, MEM], BF16)
        self.Vm = ar.alloc([2, 2, 256], BF16)
        m = ar.mark()
        mt = ar.alloc([2, D], F32)
        mn = ar.alloc([2, D], BF16)
        mnT = ar.alloc([8, MEM], BF16)
        wmk = ar.alloc([2, 8, 512], BF16)
        ss = ar.alloc([2], F32)
        junk = ar.alloc([D], F32)
        for i in range(2):
            self.dma("sp", mt[:, i, :], I["mem"][128 * i:128 * i + 128, :], writes=[("mt", i)])
        for l in range(2):
            for k in range(8):
                self.load_w(wmk[:, l, k, :], I["w_mem_kv"][l, 128 * k:128 * k + 128, :], ("wmk", l, k))
        for i in range(2):
            self.rms_rows(mt[:, i, :], mn[:, i, :], ss[:, i:i + 1], junk, ("mt", i), ("mn", i), ("mss", i), "mjunk")
        for k in range(8):
            pb = self.ps[k % 2].bitcast(BF16)
            for i in range(2):
                self.op("pe", lambda e, k=k, i=i, pb=pb: e.transpose(out=pb[:, 128 * i:128 * i + 128], in_=mn[:, i, 128 * k:128 * k + 128], identity=self.ident_bf),
                        reads=[("mn", i), "c_ident_bf"], writes=[("ps", k % 2)])
            self.op("dve", lambda e, k=k, pb=pb: e.tensor_scalar(out=mnT[:, k, :], in0=pb[:, 0:256], scalar1=self.gcol["gmem"][:, k:k + 1], scalar2=None, op0=ALU.mult),
                    reads=[("ps", k % 2), "gmem"], writes=[("mnT", k)])
        for l in range(2):
            for c in range(2):
                b = 2 + (2 * l + c) % 2
                for k in range(8):
                    self.op("pe", lambda e, l=l, c=c, k=k, b=b: e.matmul(self.ps[b][:, 0:256], lhsT=wmk[:, l, k, 128 * c:128 * c + 128], rhs=mnT[:, k, :], start=(k == 0), stop=(k == 7)),
                            reads=[("wmk", l, k), ("mnT", k)], writes=[("ps", b)])
                self.op("act", lambda e, l=l, c=c, b=b: e.activation(out=self.KTm[:, l, c, :], in_=self.ps[b][:, 0:256], func=AF.Copy),
                        reads=[("ps", b)], writes=[("KTm", l)])
            for kb in range(2):
                b = 4 + kb
                for k in range(8):
                    self.op("pe", lambda e, l=l, kb=kb, k=k, b=b: e.matmul(self.ps[b][:, 0:256], lhsT=mnT[:, k, 128 * kb:128 * kb + 128], rhs=wmk[:, l, k, 256:512], start=(k == 0), stop=(k == 7)),
                            reads=[("wmk", l, k), ("mnT", k)], writes=[("ps", b)])
                self.op("act", lambda e, l=l, kb=kb, b=b: e.activation(out=self.Vm[:, l, kb, :], in_=self.ps[b][:, 0:256], func=AF.Copy),
                        reads=[("ps", b)], writes=[("Vm", l)])
        self.sc.barrier()
        ar.release(m)

    def rms_rows(self, src, dst_bf, ss, junk, ksrc, kdst, kss, kjunk):
        self.op("act", lambda e: e.activation(out=junk, in_=src, func=AF.Square, accum_out=ss), reads=[ksrc], writes=[kjunk, kss])
        self.op("act", lambda e: e.activation(out=ss, in_=ss, func=AF.Sqrt, scale=1.0 / D, bias=EPS), reads=[kss], writes=[kss])
        self.op("dve", lambda e: e.reciprocal(out=ss, in_=ss), reads=[kss], writes=[kss])
        self.op("act", lambda e: e.activation(out=dst_bf, in_=src, func=AF.Copy, scale=ss), reads=[ksrc, kss], writes=[kdst])

    def norm_T(self, ht, hkeys, outs, tag):
        w = self.nt_work
        for i in range(4):
            sl = i % 2
            self.rms_rows(ht[:, i, :], w["xn"][:, sl, :], w["ss"][:, i:i + 1], w["junk"], hkeys[i], ("xn", sl), ("nss", i), "njunk")
            bi = 6 + i % 2
            pb = self.ps[bi].bitcast(BF16)
            for k in range(8):
                self.op("pe", lambda e, k=k, sl=sl, pb=pb: e.transpose(out=pb[:, 128 * k:128 * k + 128], in_=w["xn"][:, sl, 128 * k:128 * k + 128], identity=self.ident_bf),
                        reads=[("xn", sl), "c_ident_bf"], writes=[("ps", bi)])
            for gi, (gname, dst, dkey) in enumerate(outs):
                self.op("dve", lambda e, i=i, pb=pb, gname=gname, dst=dst: e.tensor_tensor(
                    out=dst[:, :, 128 * i:128 * i + 128], in0=pb.rearrange("p (k t) -> p k t", k=8),
                    in1=self.gcol[gname].unsqueeze(2).to_broadcast([128, 8, 128]), op=ALU.mult),
                    reads=[("ps", bi), gname], writes=[(dkey, k) for k in range(8)])

    def alloc_norm_work(self):
        ar = self.ar
        self.nt_work = {"xn": ar.alloc([2, D], BF16), "ss": ar.alloc([4], F32), "junk": ar.alloc([D], BF16)}

    def mem_attention(self, l, qmT, qkey, mo, mokey, PTs, tag):
        def st(i):
            hh, kb = i // 2, i % 2
            c, base = hh // 2, 64 * (hh % 2)
            bs = 2 + i % 2
            self.op("pe", lambda e: e.matmul(self.ps[bs][:, :], lhsT=self.KTm[base:base + 64, l, c, 128 * kb:128 * kb + 128], rhs=qmT[base:base + 64, c, :], start=True, stop=True),
                    reads=[("KTm", l), qkey], writes=[("ps", bs)])

        def pv(i):
            hh, kb = i // 2, i % 2
            bs = 2 + i % 2
            bo, bd = (4, 5) if hh % 2 == 0 else (0, 1)
            pt = PTs[i % 2]
            self.op("act", lambda e: e.activation(out=pt, in_=self.ps[bs][:, :], func=AF.Exp), reads=[("ps", bs)], writes=[("PT", i % 2)])
            self.op("pe", lambda e: e.matmul(self.ps[bo][0:64, :], lhsT=self.Vm[:, l, kb, 64 * hh:64 * hh + 64], rhs=pt, start=(kb == 0), stop=(kb == 1)),
                    reads=[("Vm", l), ("PT", i % 2)], writes=[("ps", bo)])
            self.op("pe", lambda e: e.matmul(self.ps[bd][0:64, :], lhsT=self.ones_bf[:, 0:64], rhs=pt, start=(kb == 0), stop=(kb == 1)),
                    reads=["c_ones_bf", ("PT", i % 2)], writes=[("ps", bd)])

        def tail(hh):
            bo, bd = (4, 5) if hh % 2 == 0 else (0, 1)
            rec = self.ma_rec
            self.op("act", lambda e: e.activation(out=rec[0:64, :], in_=self.ps[bd][0:64, :], func=AF.Ln), reads=[("ps", bd)], writes=["marec"])
            self.op("act", lambda e: e.activation(out=rec[0:64, :], in_=rec[0:64, :], func=AF.Exp, scale=-1.0), reads=["marec"], writes=["marec"])
            self.op("dve", lambda e: e.tensor_tensor(out=mo[0:64, hh, :], in0=self.ps[bo][0:64, :], in1=rec[0:64, :], op=ALU.mult),
                    reads=[("ps", bo), "marec"], writes=[mokey])

        st(0)
        for i in range(8):
            if i + 1 < 8:
                st(i + 1)
            pv(i)
            if i % 2 == 0 and i >= 2:
                tail(i // 2 - 1)
        tail(3)

    def s5_prep(self):
        ar, I, sc = self.ar, self.I, self.sc
        nsteps = max(1, int(np.ceil(np.log2(self.Ns))))
        self.ks_steps = nsteps
        self.Win = ar.alloc([8, 8, 2, 128], BF16, parts=96)
        self.Kblk = ar.alloc([8, 8, 96], BF16, parts=96)
        self.Wout = ar.alloc([24, 8, 2, 32], BF16)
        self.r8 = ar.alloc([24], F32)
        self.EP = ar.alloc([2, nsteps, 24], F32)
        m = ar.mark()
        T = lambda shape=(24,): ar.alloc(list(shape), F32)
        are, aim, ldt = T(), T(), T()
        Bre, Bim = T((24, 16)), T((24, 16))
        Cre, Cim = T((24, 16)), T((24, 16))
        Yre, Yim = ar.alloc([8, 2, 64], F32, parts=48), ar.alloc([8, 2, 64], F32, parts=48)
        for gl in range(2):
            P = slice(64 * gl, 64 * gl + 64)
            self.dma("sp", are[P, :], I["s5_a_re"][0, gl::2, :].rearrange("g p -> p g"), writes=["are"], slow=True)
            self.dma("sp", aim[P, :], I["s5_a_im"][0, gl::2, :].rearrange("g p -> p g"), writes=["aim"], slow=True)
            self.dma("sp", ldt[P, :], I["s5_log_dt"][0, gl::2].partition_broadcast(64), writes=["ldt"], slow=True)
            self.dma("sp", Bre[P, :, :], I["s5_b_re"][0, gl::2].rearrange("g p i -> p g i"), writes=["Bre"], slow=True)
            self.dma("sp", Bim[P, :, :], I["s5_b_im"][0, gl::2].rearrange("g p i -> p g i"), writes=["Bim"], slow=True)
        for k in range(3):
            for (Y, nm, ky) in ((Yre, "s5_c_re", "Yre"), (Yim, "s5_c_im", "Yim")):
                src = I[nm][0].rearrange("(q k gl) i p -> k q i gl p", k=3, gl=2)[k]
                for q in range(8):
                    self.dma("sp", Y[16 * k:16 * k + 16, q, :, :], src[q], writes=[ky], slow=True)
        for (Y, Cx, ky, kc) in ((Yre, Cre, "Yre", "Cre"), (Yim, Cim, "Yim", "Cim")):
            for q in range(8):
                b = q % 2
                self.op("pe", lambda e, Y=Y, q=q, b=b: e.transpose(out=self.ps[b][:, 0:48], in_=Y[0:48, q, :, :].rearrange("p a b -> p (a b)"), identity=self.ident_f[0:48, 0:48]),
                        reads=[ky, "c_ident_f"], writes=[("ps", b)])
                self.op("act", lambda e, Cx=Cx, q=q, b=b: e.activation(out=Cx[:, 3 * q:3 * q + 3, :], in_=self.ps[b][:, 0:48].rearrange("p (k i) -> p k i", k=3), func=AF.Copy),
                        reads=[("ps", b)], writes=[kc])
        V = "dve"

        def tt(out, a, b, op, r, w):
            self.op(V, lambda e: e.tensor_tensor(out=out, in0=a, in1=b, op=op), reads=r, writes=w)

        def ts(out, a, s1, s2, op0, op1, r, w):
            if s2 is None:
                self.op(V, lambda e: e.tensor_scalar(out=out, in0=a, scalar1=s1, scalar2=None, op0=op0), reads=r, writes=w)
            else:
                self.op(V, lambda e: e.tensor_scalar(out=out, in0=a, scalar1=s1, scalar2=s2, op0=op0, op1=op1), reads=r, writes=w)

        dt, lre, mag, ph, sn, cs = T(), T(), T(), T(), T(), T()
        t1, t2, t3 = T(), T(), T()
        self.op("act", lambda e: e.activation(out=dt, in_=ldt, func=AF.Exp), reads=["ldt"], writes=["dt"])
        ts(lre, are, -1e-4, None, ALU.min, None, ["are"], ["lre"])
        tt(t1, lre, dt, ALU.mult, ["lre", "dt"], ["t1"])
        self.op("act", lambda e: e.activation(out=mag, in_=t1, func=AF.Exp), reads=["t1"], writes=["mag"])
        self.op("act", lambda e: e.activation(out=self.r8, in_=t1, func=AF.Exp, scale=8.0), reads=["t1"], writes=["r8"])
        tt(ph, aim, dt, ALU.mult, ["aim", "dt"], ["ph"])
        hpi = ar.alloc([1], F32)
        self.op(V, lambda e: e.memset(hpi, float(np.pi / 2)), writes=["hpi"])
        self.op("act", lambda e: e.activation(out=sn, in_=ph, func=AF.Sin, scale=1.0 / 16), reads=["ph"], writes=["sn"])
        self.op("act", lambda e: e.activation(out=cs, in_=ph, func=AF.Sin, scale=1.0 / 16, bias=hpi), reads=["ph", "hpi"], writes=["cs"])
        for it in range(4):
            tt(t1, cs, cs, ALU.mult, ["cs"], ["t1"])
            tt(t2, sn, sn, ALU.mult, ["sn"], ["t2"])
            tt(t3, cs, sn, ALU.mult, ["cs", "sn"], ["t3"])
            tt(cs, t1, t2, ALU.subtract, ["t1", "t2"], ["cs"])
            ts(sn, t3, 2.0, None, ALU.mult, None, ["t3"], ["sn"])
        PW = T((2, 9, 24))
        self.op(V, lambda e: e.memset(PW[:, 0, 0, :], 1.0), writes=["PW"])
        self.op(V, lambda e: e.memset(PW[:, 1, 0, :], 0.0), reads=["PW"], writes=["PW"])
        tt(PW[:, 0, 1, :], mag, cs, ALU.mult, ["mag", "cs", "PW"], ["PW"])
        tt(PW[:, 1, 1, :], mag, sn, ALU.mult, ["mag", "sn", "PW"], ["PW"])

        def cmul(or_, oi, ar_, ai_, br_, bi_, rk, wk):
            tt(t1, ar_, br_, ALU.mult, rk, ["t1"])
            tt(t2, ai_, bi_, ALU.mult, rk, ["t2"])
            tt(or_, t1, t2, ALU.subtract, ["t1", "t2"] + rk, wk)
            tt(t1, ar_, bi_, ALU.mult, rk, ["t1"])
            tt(t2, ai_, br_, ALU.mult, rk, ["t2"])
            tt(oi, t1, t2, ALU.add, ["t1", "t2"] + rk, wk)

        for k in range(2, 9):
            cmul(PW[:, 0, k, :], PW[:, 1, k, :], PW[:, 0, k - 1, :], PW[:, 1, k - 1, :], PW[:, 0, 1, :], PW[:, 1, 1, :], ["PW"], ["PW"])
        EP = self.EP
        rr = T()
        self.op(V, lambda e: e.reciprocal(out=rr, in_=self.r8), reads=["r8"], writes=["rr"])
        tt(EP[:, 0, 0, :], PW[:, 0, 8, :], rr, ALU.mult, ["PW", "rr", "EP"], ["EP"])
        tt(EP[:, 1, 0, :], PW[:, 1, 8, :], rr, ALU.mult, ["PW", "rr", "EP"], ["EP"])
        for k in range(1, nsteps):
            cmul(EP[:, 0, k, :], EP[:, 1, k, :], EP[:, 0, k - 1, :], EP[:, 1, k - 1, :], EP[:, 0, k - 1, :], EP[:, 1, k - 1, :], ["EP"], ["EP"])
        den, am1, zre, zim = T(), T(), T(), T()
        tt(t1, lre, lre, ALU.mult, ["lre"], ["t1"])
        tt(t2, aim, aim, ALU.mult, ["aim"], ["t2"])
        tt(den, t1, t2, ALU.add, ["t1", "t2"], ["den"])
        self.op(V, lambda e: e.reciprocal(out=den, in_=den), reads=["den"], writes=["den"])
        ts(am1, PW[:, 0, 1, :], -1.0, None, ALU.add, None, ["PW"], ["am1"])
        tt(t1, am1, lre, ALU.mult, ["am1", "lre"], ["t1"])
        tt(t2, PW[:, 1, 1, :], aim, ALU.mult, ["PW", "aim"], ["t2"])
        tt(t3, t1, t2, ALU.add, ["t1", "t2"], ["t3"])
        tt(zre, t3, den, ALU.mult, ["t3", "den"], ["zre"])
        tt(t1, PW[:, 1, 1, :], lre, ALU.mult, ["PW", "lre"], ["t1"])
        tt(t2, am1, aim, ALU.mult, ["am1", "aim"], ["t2"])
        tt(t3, t1, t2, ALU.subtract, ["t1", "t2"], ["t3"])
        tt(zim, t3, den, ALU.mult, ["t3", "den"], ["zim"])
        bbr, bbi, u1, u2 = T((24, 16)), T((24, 16)), T((24, 16)), T((24, 16))

        def bc(a):
            return a.unsqueeze(2).to_broadcast([128, 24, 16])

        def cmul3(or_, oi, pr, pi, br_, bi_, rk, wk):
            tt(u1, br_, bc(pr), ALU.mult, rk, ["u1"])
            tt(u2, bi_, bc(pi), ALU.mult, rk, ["u2"])
            tt(or_, u1, u2, ALU.subtract, ["u1", "u2"] + rk, wk)
            tt(u1, bi_, bc(pr), ALU.mult, rk, ["u1"])
            tt(u2, br_, bc(pi), ALU.mult, rk, ["u2"])
            tt(oi, u1, u2, ALU.add, ["u1", "u2"] + rk, wk)

        cmul3(bbr, bbi, zre, zim, Bre, Bim, ["zre", "zim", "Bre", "Bim"], ["bb"])
        W0 = T((24, 2, 32))
        self.op(V, lambda e: e.memset(W0, 0.0), writes=["W0"])
        self.op("pool", lambda e: e.memset(self.Wout, 0.0), writes=["Wout"])
        wr, wi = T((24, 16)), T((24, 16))
        for gl in range(2):
            P = slice(64 * gl, 64 * gl + 64)
            self.op(V, lambda e, P=P, gl=gl: e.tensor_copy(out=W0[P, :, 0, 16 * gl:16 * gl + 16], in_=Cre[P, :, :]), reads=["Cre", "W0"], writes=["W0"])
            self.op(V, lambda e, P=P, gl=gl: e.tensor_scalar(out=W0[P, :, 1, 16 * gl:16 * gl + 16], in0=Cim[P, :, :], scalar1=-1.0, scalar2=None, op0=ALU.mult), reads=["Cim", "W0"], writes=["W0"])
        for l in range(8):
            cmul3(wr, wi, PW[:, 0, l + 1, :], PW[:, 1, l + 1, :], Cre, Cim, ["PW", "Cre", "Cim"], ["wrwi"])
            for gl in range(2):
                P = slice(64 * gl, 64 * gl + 64)
                self.op(V, lambda e, P=P, gl=gl, l=l: e.tensor_copy(out=self.Wout[P, :, l, 0, 16 * gl:16 * gl + 16], in_=wr[P, :, :]), reads=["wrwi", "Wout"], writes=["Wout"])
                self.op(V, lambda e, P=P, gl=gl, l=l: e.tensor_scalar(out=self.Wout[P, :, l, 1, 16 * gl:16 * gl + 16], in0=wi[P, :, :], scalar1=-1.0, scalar2=None, op0=ALU.mult), reads=["wrwi", "Wout"], writes=["Wout"])
        Kst = ar.alloc([24, 8, 32], BF16, parts=32)
        xr, xi = T((24, 16)), T((24, 16))
        Xpad = T((4, 8, 2, 3, 32))
        for qh in range(2):
            self.op("pool", lambda e: e.memset(Xpad, 0.0), reads=["Xpad"], writes=["Xpad"])
            for j in range(8):
                cmul3(xr, xi, PW[:, 0, 7 - j, :], PW[:, 1, 7 - j, :], bbr, bbi, ["PW", "bb"], ["xrxi"])
                for gl in range(2):
                    P = slice(64 * gl, 64 * gl + 64)
                    for c, xx in ((0, xr), (1, xi)):
                        self.op(V, lambda e, P=P, gl=gl, j=j, c=c, xx=xx, qh=qh: e.tensor_copy(
                            out=Xpad[P, :, j, c, :, 16 * gl:16 * gl + 16],
                            in_=xx[P, 12 * qh:12 * qh + 12, :].rearrange("p (q k) i -> p q k i", k=3)),
                            reads=["xrxi", "Xpad"], writes=["Xpad"])
            n = 0
            for ql in range(4):
                q = 4 * qh + ql
                for j in range(8):
                    for c in range(2):
                        b = (n // 4) % 2
                        col = 128 * (n % 4)
                        self.op("pe", lambda e, ql=ql, j=j, c=c, b=b, col=col: e.transpose(
                            out=self.ps[b][0:96, col:col + 128], in_=Xpad[:, ql, j, c, :, :].rearrange("p k i -> p (k i)"), identity=self.ident_f),
                            reads=["Xpad", "c_ident_f"], writes=[("ps", b)])
                        if n % 4 == 3:
                            j0 = j - 1
                            self.op("act", lambda e, q=q, j0=j0, b=b: e.activation(
                                out=self.Win[:, q, j0:j0 + 2, :, :].rearrange("p a b s -> p (a b s)"), in_=self.ps[b][0:96, :], func=AF.Copy),
                                reads=[("ps", b)], writes=["Win"])
                        n += 1
            n = 0
            for ql in range(4):
                for k in range(3):
                    pair = 3 * (4 * qh + ql) + k
                    for tau in range(8):
                        b = 2 + (n // 16) % 2
                        col = 32 * (n % 16)
                        for c in range(2):
                            self.op("pe", lambda e, ql=ql, k=k, pair=pair, tau=tau, c=c, b=b, col=col: e.matmul(
                                self.ps[b][0:32, col:col + 32], lhsT=Xpad[:, ql, 7 - tau, c, k, :], rhs=W0[:, pair, c, :], start=(c == 0), stop=(c == 1)),
                                reads=["Xpad", "W0"], writes=[("ps", b)])
                        if n % 16 == 15:
                            self.op("act", lambda e, pair=pair, b=b: e.activation(
                                out=Kst[0:32, pair - 1:pair + 1, :, :].rearrange("p a t i -> p (a t i)"), in_=self.ps[b][0:32, :], func=AF.Copy),
                                reads=[("ps", b)], writes=["Kst"])
                        n += 1
        self.op("pool", lambda e: e.memset(self.Kblk, 0.0), writes=["Kblk"])
        for k in range(3):
            for q in range(8):
                self.dma("sp", self.Kblk[32 * k:32 * k + 32, q, :, 32 * k:32 * k + 32],
                         Kst[0:32, 3 * q + k, :, :], reads=["Kst", "Kblk"], writes=["Kblk"], slow=True)
        dT = ar.alloc([8], F32, parts=96)
        self.dma("sp", dT, I["s5_d"][0].rearrange("(q p) -> p q", p=96), writes=["dT"], slow=True)
        for q in range(8):
            self.op(V, lambda e, q=q: e.scalar_tensor_tensor(out=self.Kblk[:, q, 0, :], in0=self.ident_f[0:96, 0:96], scalar=dT[:, q:q + 1], in1=self.Kblk[:, q, 0, :], op0=ALU.mult, op1=ALU.add),
                    reads=["dT", "Kblk", "c_ident_f"], writes=["Kblk"])
        self.sc.barrier()
        ar.release(m)

    def phase_front0(self):
        ar, I, S, NT = self.ar, self.I, self.S, self.NT
        for f in self.late:
            f()
        self.late = []
        wsb = ar.alloc([8, D], BF16)
        for k in range(8):
            self.load_w(wsb[:, k, :], I["w_in"][0, 128 * k:128 * k + 128, :], ("w", k))
        xt = [ar.alloc([4, D], F32) for _ in range(2)]
        self.alloc_norm_work()
        xnTs = [ar.alloc([8, 512], BF16) for _ in range(2)]
        ust = [ar.alloc([8, 8, 64], BF16, parts=96) for _ in range(2)]
        qmT = ar.alloc([2, 512], BF16)
        PTs = [ar.alloc([512], BF16) for _ in range(2)]
        self.ma_rec = ar.alloc([512], F32)
        mo = [ar.alloc([4, 512], BF16, parts=64) for _ in range(2)]
        Ns = self.Ns
        gEc = ar.alloc([3, Ns], F32)
        gEs = ar.alloc([3, Ns], F32)
        gtg = [ar.alloc([3, max(1, Ns // 2)], F32) for _ in range(4)]
        per_tile = 8 // NT if NT <= 8 else 1

        def load(t):
            for i in range(4):
                self.dma("sp", xt[t % 2][:, i, :], I["x"][512 * t + 128 * i:512 * t + 128 * i + 128, :], writes=[("xt", t % 2, i)])

        load(0)
        self.norm_T(xt[0], [("xt", 0, i) for i in range(4)], [("gmix0", xnTs[0], ("xnT", 0))], "f0")
        for t in range(NT):
            if t + 1 < NT:
                load(t + 1)
            s = t % 2
            xnT = xnTs[s]
            xk = ("xnT", s)
            for m in range(8):
                b = m % 2
                for k in range(8):
                    self.op("pe", lambda e, m=m, k=k, b=b, xnT=xnT: e.matmul(self.ps[b][0:96, :], lhsT=wsb[:, k, 96 * m:96 * m + 96], rhs=xnT[:, k, :], start=(k == 0), stop=(k == 7)),
                            reads=[("w", k), (xk, k)], writes=[("ps", b)])
                self.op("act", lambda e, m=m, b=b, s=s: e.activation(out=ust[s][:, m, :, :], in_=self.ps[b][0:96, :].rearrange("p (s l) -> p l s", l=8), func=AF.Copy),
                        reads=[("ps", b)], writes=[("ust", s)])
            for m in range(8):
                self.dma("sp", self.uT_d[m, :, :, 64 * t:64 * t + 64], ust[s][:, m, :, :], reads=[("ust", s)], writes=[("dram", "uT")])
            if t + 1 < NT:
                self.norm_T(xt[1 - s], [("xt", 1 - s, i) for i in range(4)], [("gmix0", xnTs[1 - s], ("xnT", 1 - s))], "f0")
            for c in range(2):
                b = c % 2
                for k in range(8):
                    self.op("pe", lambda e, c=c, k=k, b=b, xnT=xnT: e.matmul(self.ps[b][:, :], lhsT=wsb[:, k, 768 + 128 * c:768 + 128 * c + 128], rhs=xnT[:, k, :], start=(k == 0), stop=(k == 7)),
                            reads=[("w", k), (xk, k)], writes=[("ps", b)])
                self.op("act", lambda e, c=c, b=b: e.activation(out=qmT[:, c, :], in_=self.ps[b][:, :], func=AF.Copy, scale=0.125),
                        reads=[("ps", b)], writes=["qmT"])
            self.mem_attention(0, qmT, "qmT", mo[s], ("mo", s), PTs, "m0")
            self.dma("sp", self.memo_d[0][:, :, 512 * t:512 * t + 512].rearrange("h p t -> p h t"), mo[s], reads=[("mo", s)], writes=[("dram", "memo0")])
            for q in range(t * per_tile, min(8, (t + 1) * per_tile)):
                self.gen_tables(q, gEc, gEs, gtg)

    def gen_tables(self, q, ec, es_, tg):
        Ns, EP = self.Ns, self.EP
        kc, ks = "gEc", "gEs"
        P_ = "pool"
        self.op(P_, lambda e: e.memset(ec[:, :, 0:1], 1.0), writes=[kc])
        self.op(P_, lambda e: e.memset(es_[:, :, 0:1], 0.0), writes=[ks])
        for k in range(self.ks_steps):
            n = 1 << k
            if n >= Ns:
                break
            cb = EP[:, 0, k, 3 * q:3 * q + 3].unsqueeze(2).to_broadcast([128, 3, n])
            sb = EP[:, 1, k, 3 * q:3 * q + 3].unsqueeze(2).to_broadcast([128, 3, n])
            lo = slice(0, n)
            hi = slice(n, 2 * n)
            t1, t2, t3, t4 = [t[:, :, 0:n] for t in tg]
            self.op(P_, lambda e, t1=t1, cb=cb, lo=lo: e.tensor_tensor(out=t1, in0=ec[:, :, lo], in1=cb, op=ALU.mult), reads=[kc, "EP"], writes=["tg0"])
            self.op(P_, lambda e, t2=t2, sb=sb, lo=lo: e.tensor_tensor(out=t2, in0=es_[:, :, lo], in1=sb, op=ALU.mult), reads=[ks, "EP"], writes=["tg1"])
            self.op(P_, lambda e, t3=t3, sb=sb, lo=lo: e.tensor_tensor(out=t3, in0=ec[:, :, lo], in1=sb, op=ALU.mult), reads=[kc, "EP"], writes=["tg2"])
            self.op(P_, lambda e, t4=t4, cb=cb, lo=lo: e.tensor_tensor(out=t4, in0=es_[:, :, lo], in1=cb, op=ALU.mult), reads=[ks, "EP"], writes=["tg3"])
            self.op(P_, lambda e, t1=t1, t2=t2, hi=hi: e.tensor_tensor(out=ec[:, :, hi], in0=t1, in1=t2, op=ALU.subtract), reads=["tg0", "tg1", kc], writes=[kc])
            self.op(P_, lambda e, t3=t3, t4=t4, hi=hi: e.tensor_tensor(out=es_[:, :, hi], in0=t3, in1=t4, op=ALU.add), reads=["tg2", "tg3", ks], writes=[ks])
        self.dma("sp", self.E_d[q, 0], ec.rearrange("p a b -> p (a b)"), reads=[kc], writes=[("dram", "E", q)])
        self.dma("sp", self.E_d[q, 1], es_.rearrange("p a b -> p (a b)"), reads=[ks], writes=[("dram", "E", q)])

    def phase_s5(self):
        ar, S, Ns = self.ar, self.S, self.Ns
        nsteps = self.ks_steps
        PAD = 1 << (nsteps - 1)
        ut = [ar.alloc([8, Ns], BF16, parts=96) for _ in range(3)]
        Hshs = [ar.alloc([3, 2, Ns], BF16) for _ in range(2)]
        gq = [ar.alloc([S], BF16, parts=96) for _ in range(2)]
        x2 = [ar.alloc([Ns], F32, parts=96) for _ in range(2)]
        sg = [ar.alloc([Ns], F32, parts=96) for _ in range(2)]
        Ec = [ar.alloc([3, Ns], F32) for _ in range(2)]
        Es = [ar.alloc([3, Ns], F32) for _ in range(2)]
        NW = 6
        wk = [[ar.alloc([Ns], F32) for _ in range(NW)] for _ in range(2)]
        g_pm = ar.alloc([8, Ns], BF16, parts=96)
        for hs in range(2):
            self.op("pool", lambda e, hs=hs: e.memset(Hshs[hs][:, :, :, 0:1], 0.0), writes=[("Hsh", hs)])

        def load_tables(q):
            sl = q % 2
            self.dma("sp", Ec[sl].rearrange("p a b -> p (a b)"), self.E_d[q, 0], reads=[("dram", "E", q)], writes=[("Ec", sl)])
            self.dma("sp", Es[sl].rearrange("p a b -> p (a b)"), self.E_d[q, 1], reads=[("dram", "E", q)], writes=[("Es", sl)])

        def load(q):
            self.dma("sp", ut[q % 3], self.uT_d[q], reads=[("dram", "uT")], writes=[("ut", q % 3)])

        D_ = "dve"

        def stage1(q):
            u = ut[q % 3]
            ukey = ("ut", q % 3)
            sl = q % 2
            Hsh = Hshs[q % 2]
            hkey = ("Hsh", q % 2)
            for k in range(3):
                pair = 3 * q + k
                R = slice(32 * k, 32 * k + 32)
                npair = pair
                W = wk[npair % 2]
                wkey = lambda i, npair=npair: ("wk", npair % 2, i)
                for c in range(2):
                    b = c
                    for j in range(8):
                        self.op("pe", lambda e, R=R, j=j, c=c, b=b: e.matmul(self.ps[b][:, 0:Ns], lhsT=self.Win[R, q, j, c, :], rhs=u[R, j, :], start=(j == 0), stop=(j == 7)),
                                reads=["Win", ukey], writes=[("ps", b)])
                ec, es_ = Ec[sl][:, k, :], Es[sl][:, k, :]
                kc, ks = ("Ec", sl), ("Es", sl)
                Sre, Sim = self.ps[0][:, 0:Ns], self.ps[1][:, 0:Ns]
                self.op(D_, lambda e, W=W, ec=ec: e.tensor_tensor(out=W[0], in0=Sre, in1=ec, op=ALU.mult), reads=[("ps", 0), kc], writes=[wkey(0)])
                self.op(D_, lambda e, W=W, es_=es_: e.tensor_tensor(out=W[1], in0=Sim, in1=es_, op=ALU.mult), reads=[("ps", 1), ks], writes=[wkey(1)])
                self.op(D_, lambda e, W=W, ec=ec: e.tensor_tensor(out=W[2], in0=Sim, in1=ec, op=ALU.mult), reads=[("ps", 1), kc], writes=[wkey(2)])
                self.op(D_, lambda e, W=W, es_=es_: e.tensor_tensor(out=W[3], in0=Sre, in1=es_, op=ALU.mult), reads=[("ps", 0), ks], writes=[wkey(3)])
                yield
                self.op(D_, lambda e, W=W: e.tensor_tensor(out=W[0], in0=W[0], in1=W[1], op=ALU.add), reads=[wkey(0), wkey(1)], writes=[wkey(0)])
                self.op(D_, lambda e, W=W: e.tensor_tensor(out=W[2], in0=W[2], in1=W[3], op=ALU.subtract), reads=[wkey(2), wkey(3)], writes=[wkey(2)])
                r8b = self.r8[:, pair:pair + 1].to_broadcast([128, Ns])
                self.op(D_, lambda e, W=W, r8b=r8b: e.tensor_tensor_scan(out=W[4], data0=r8b, data1=W[0], initial=0.0, op0=ALU.mult, op1=ALU.add), reads=[wkey(0), "r8"], writes=[wkey(4)])
                self.op(D_, lambda e, W=W, r8b=r8b: e.tensor_tensor_scan(out=W[5], data0=r8b, data1=W[2], initial=0.0, op0=ALU.mult, op1=ALU.add), reads=[wkey(2), "r8"], writes=[wkey(5)])
                yield
                n1 = Ns - 1
                self.op(D_, lambda e, W=W, ec=ec: e.tensor_tensor(out=W[0], in0=W[4], in1=ec, op=ALU.mult), reads=[wkey(4), kc], writes=[wkey(0)])
                self.op(D_, lambda e, W=W, es_=es_: e.tensor_tensor(out=W[1], in0=W[5], in1=es_, op=ALU.mult), reads=[wkey(5), ks], writes=[wkey(1)])
                self.op(D_, lambda e, W=W, ec=ec: e.tensor_tensor(out=W[2], in0=W[5], in1=ec, op=ALU.mult), reads=[wkey(5), kc], writes=[wkey(2)])
                self.op(D_, lambda e, W=W, es_=es_: e.tensor_tensor(out=W[3], in0=W[4], in1=es_, op=ALU.mult), reads=[wkey(4), ks], writes=[wkey(3)])
                self.op(D_, lambda e, W=W, k=k: e.tensor_tensor(out=Hsh[:, k, 0, 1:Ns], in0=W[0][:, 0:n1], in1=W[1][:, 0:n1], op=ALU.subtract), reads=[wkey(0), wkey(1), hkey], writes=[hkey])
                self.op(D_, lambda e, W=W, k=k: e.tensor_tensor(out=Hsh[:, k, 1, 1:Ns], in0=W[2][:, 0:n1], in1=W[3][:, 0:n1], op=ALU.add), reads=[wkey(2), wkey(3), hkey], writes=[hkey])
                yield

        def stage2(q, gen):
            u = ut[q % 3]
            ukey = ("ut", q % 3)
            Hsh = Hshs[q % 2]
            hkey = ("Hsh", q % 2)
            g = gq[q % 2]
            gkey = ("gq", q % 2)

            def mm(l):
                b = 2 + l % 2
                first = True
                for tau in range(l + 1):
                    self.op("pe", lambda e, tau=tau, first=first: e.matmul(self.ps[b][0:96, 0:Ns], lhsT=self.Kblk[:, q, tau, :], rhs=u[:, l - tau, :], start=first, stop=False),
                            reads=["Kblk", ukey], writes=[("ps", b)])
                    first = False
                for k in range(3):
                    for c in range(2):
                        last = (k == 2 and c == 1)
                        self.op("pe", lambda e, k=k, c=c, last=last: e.matmul(self.ps[b][32 * k:32 * k + 32, 0:Ns], lhsT=self.Wout[:, 3 * q + k, l, c, :], rhs=Hsh[:, k, c, :], start=False, stop=last, skip_group_check=True),
                                reads=["Wout", hkey], writes=[("ps", b)])

            def gA(l):
                b = 2 + l % 2
                yp = self.ps[b][0:96, 0:Ns]
                a2 = x2[l % 2]
                k2 = ("x2", l % 2)
                self.op("act", lambda e: e.activation(out=a2, in_=yp, func=AF.Square), reads=[("ps", b)], writes=[k2])
                self.op("dve", lambda e: e.tensor_scalar(out=a2, in0=a2, scalar1=0.044715, scalar2=1.0, op0=ALU.mult, op1=ALU.add), reads=[k2], writes=[k2])
                self.op("dve", lambda e: e.tensor_tensor(out=a2, in0=a2, in1=yp, op=ALU.mult), reads=[k2, ("ps", b)], writes=[k2])

            def gB(l):
                b = 2 + l % 2
                yp = self.ps[b][0:96, 0:Ns]
                a2, a3 = x2[l % 2], sg[l % 2]
                k2, k3 = ("x2", l % 2), ("sg", l % 2)
                self.op("act", lambda e: e.activation(out=a3, in_=a2, func=AF.Sigmoid, scale=1.5957691216057308), reads=[k2], writes=[k3])
                self.op("dve", lambda e: e.tensor_tensor(out=g_pm[:, l, :], in0=a3, in1=yp, op=ALU.mult), reads=[k3, ("ps", b)], writes=["g_pm"])

            for l in range(8):
                mm(l)
                gA(l)
                if l >= 1:
                    gB(l - 1)
                if gen is not None:
                    next(gen, None)
            gB(7)
            if gen is not None:
                for _ in gen:
                    pass
            self.op("pool", lambda e: e.tensor_copy(out=g.rearrange("p (s l) -> p s l", l=8), in_=g_pm.rearrange("p l s -> p s l")), reads=["g_pm"], writes=[gkey])
            self.dma("sp", self.gT_d[q], g, reads=[gkey], writes=[("dram", "gT")])

        load(0)
        load_tables(0)
        load(1)
        load_tables(1)
        for _ in stage1(0):
            pass
        for q in range(8):
            stage2(q, stage1(q + 1) if q + 1 < 8 else None)
            if q + 2 < 8:
                load(q + 2)
                load_tables(q + 2)

    def load_wup(self, l, wup):
        for k in range(8):
            self.load_w(wup[:, k, :], self.I["w_ffn_up"][l, 128 * k:128 * k + 128, :], ("wup", k))

    def sweep_glu_wout0(self, after_weights=None):
        ar, I, S, NT = self.ar, self.I, self.S, self.NT
        wg = ar.alloc([8, 768], BF16, parts=96)
        woT = ar.alloc([8, D], BF16, parts=96)
        woM = ar.alloc([4, D], BF16, parts=64)
        self.load_w(wg.rearrange("p q n -> p (q n)") if False else wg[:, 0, :], I["w_glu"][0, 0:96, :], ("wg", 0))
        for q in range(1, 8):
            self.load_w(wg[:, q, :], I["w_glu"][0, 96 * q:96 * q + 96, :], ("wg", q))
        for q in range(8):
            self.load_w(woT[:, q, :], I["w_out"][0, 96 * q:96 * q + 96, :], ("woT", q))
        for h in range(4):
            self.load_w(woM[:, h, :], I["w_out"][0, 768 + 64 * h:768 + 64 * h + 64, :], ("woM", h))
        if after_weights is not None:
            after_weights()
        gt = [ar.alloc([8, 512], BF16, parts=96) for _ in range(2)]
        mo = [ar.alloc([4, 512], BF16, parts=64) for _ in range(2)]
        xt = [ar.alloc([4, D], F32) for _ in range(2)]
        tokT = ar.alloc([8, 512], BF16, parts=96)
        sig = [ar.alloc([512], F32, parts=96) for _ in range(2)]

        def load(t):
            s = t % 2
            self.dma("sp", gt[s], self.gT_d[:, :, 512 * t:512 * t + 512].rearrange("q p t -> p q t"), reads=[("dram", "gT")], writes=[("gt", s)])
            self.dma("sp", mo[s], self.memo_d[0][:, :, 512 * t:512 * t + 512].rearrange("h p t -> p h t"), reads=[("dram", "memo0")], writes=[("mo", s)])
            for i in range(4):
                self.dma("sp", xt[s][:, i, :], I["x"][512 * t + 128 * i:512 * t + 128 * i + 128, :], reads=[("dram", "hA", t)], writes=[("xt", s, i)])

        load(0)
        for t in range(NT):
            if t + 1 < NT:
                load(t + 1)
            s = t % 2
            for m in range(8):
                b = m % 2
                for q in range(8):
                    self.op("pe", lambda e, m=m, q=q, b=b, s=s: e.matmul(self.ps[b][0:96, :], lhsT=wg[:, q, 96 * m:96 * m + 96], rhs=gt[s][:, q, :], start=(q == 0), stop=(q == 7)),
                            reads=[("wg", q), ("gt", s)], writes=[("ps", b)])
                sgb = sig[m % 2]
                self.op("act", lambda e, b=b, sgb=sgb: e.activation(out=sgb, in_=self.ps[b][0:96, :], func=AF.Sigmoid), reads=[("ps", b)], writes=[("sig", m % 2)])
                self.op("dve", lambda e, m=m, s=s, sgb=sgb: e.tensor_tensor(out=tokT[:, m, :], in0=gt[s][:, m, :], in1=sgb, op=ALU.mult),
                        reads=[("sig", m % 2), ("gt", s)], writes=[("tokT", m)])
            self.wout_residual(xt[s], [("xt", s, i) for i in range(4)],
                               [(tokT[:, q, :], woT[:, q, :], [("tokT", q), ("woT", q)]) for q in range(8)] +
                               [(mo[s][0:64, h, :], woM[0:64, h, :], [("mo", s), ("woM", h)]) for h in range(4)])
            for i in range(4):
                self.dma("sp", self.hA[512 * t + 128 * i:512 * t + 128 * i + 128, :], xt[s][:, i, :], reads=[("xt", s, i)], writes=[("dram", "hA", t)])

    def wout_residual(self, ht, hkeys, terms):
        n = len(terms)
        for sub in range(4):
            for half in range(2):
                b = 2 + (2 * sub + half) % 2
                for ti, (aT, w, keys) in enumerate(terms):
                    self.op("pe", lambda e, aT=aT, w=w, sub=sub, half=half, b=b, ti=ti: e.matmul(self.ps[b][:, :], lhsT=aT[:, 128 * sub:128 * sub + 128], rhs=w[:, 512 * half:512 * half + 512], start=(ti == 0), stop=(ti == n - 1)),
                            reads=keys, writes=[("ps", b)])
                self.op("dve", lambda e, sub=sub, half=half, b=b: e.tensor_tensor(out=ht[:, sub, 512 * half:512 * half + 512], in0=self.ps[b][:, :], in1=ht[:, sub, 512 * half:512 * half + 512], op=ALU.add),
                        reads=[("ps", b), hkeys[sub]], writes=[hkeys[sub]])

    def sweep_ffn(self, l, hin, hout, final, wup=None):
        ar, I, S, NT = self.ar, self.I, self.S, self.NT
        if wup is None:
            wup = ar.alloc([8, 2 * DFF], BF16)
            self.load_wup(l, wup)
        wdn = ar.alloc([NJ, D], BF16)
        for j in range(NJ):
            self.load_w(wdn[:, j, :], I["w_ffn_down"][l, 128 * j:128 * j + 128, :], ("wdn", j))
        ht = ar.alloc([4, D], F32)
        self.alloc_norm_work()
        xnT = ar.alloc([8, 512], BF16)
        act = ar.alloc([NJ, 512], BF16)
        Gb = [ar.alloc([514], F32) for _ in range(2)]
        T1 = [ar.alloc([512], F32) for _ in range(2)]
        carry = ar.alloc([NJ, 2], F32)
        self.op("pool", lambda e: e.memset(carry, 0.0), writes=["carry"])
        cw, cb = self.cw[l], self.cb[l]
        gname = "gffn%d" % l
        if final:
            yss = ar.alloc([4], F32)
        for t in range(NT):
            for i in range(4):
                self.dma("sp", ht[:, i, :], hin[512 * t + 128 * i:512 * t + 128 * i + 128, :], reads=[("dram", "h_in", l, t)] + ([("dram", "hA", t)] if True else []), writes=[("ht", i)])
            self.norm_T(ht, [("ht", i) for i in range(4)], [(gname, xnT, "xnT")], "ffn")
            def chain_a(j):
                bg = (j % 3) * 2 + 1
                G, Tt = Gb[j % 2], T1[j % 2]
                kG, kT = ("Gb", j % 2), ("T1", j % 2)
                self.op("pool", lambda e: e.tensor_copy(out=G[:, 0:2], in_=carry[:, j, :]), reads=["carry", kG], writes=[kG])
                self.op("act", lambda e: e.activation(out=G[:, 2:514], in_=self.ps[bg][:, :], func=AF.Copy), reads=[("ps", bg), kG], writes=[kG])
                self.op("pool", lambda e: e.tensor_copy(out=carry[:, j, :], in_=G[:, 512:514]), reads=[kG, "carry"], writes=["carry"])
                self.op("act", lambda e: e.activation(out=Tt, in_=G[:, 2:514], func=AF.Identity, scale=cw[:, j, 2:3], bias=cb[:, j:j + 1]),
                        reads=[kG, ("cw", l), ("cb", l)], writes=[kT])
                self.op("dve", lambda e: e.scalar_tensor_tensor(out=Tt, in0=G[:, 1:513], scalar=cw[:, j, 1:2], in1=Tt, op0=ALU.mult, op1=ALU.add),
                        reads=[kG, kT, ("cw", l)], writes=[kT])
                self.op("dve", lambda e: e.scalar_tensor_tensor(out=Tt, in0=G[:, 0:512], scalar=cw[:, j, 0:1], in1=Tt, op0=ALU.mult, op1=ALU.add),
                        reads=[kG, kT, ("cw", l)], writes=[kT])

            def chain_b(j):
                ba = (j % 3) * 2
                Tt = T1[j % 2]
                kT = ("T1", j % 2)
                self.op("act", lambda e: e.activation(out=Tt, in_=Tt, func=AF.Silu), reads=[kT], writes=[kT])
                self.op("dve", lambda e: e.tensor_tensor(out=act[:, j, :], in0=Tt, in1=self.ps[ba][:, :], op=ALU.mult),
                        reads=[kT, ("ps", ba)], writes=[("act", j)])

            for j in range(NJ):
                ba, bg = (j % 3) * 2, (j % 3) * 2 + 1
                for (bb, off) in ((ba, 0), (bg, DFF)):
                    for k in range(8):
                        self.op("pe", lambda e, j=j, k=k, bb=bb, off=off: e.matmul(self.ps[bb][:, :], lhsT=wup[:, k, off + 128 * j:off + 128 * j + 128], rhs=xnT[:, k, :], start=(k == 0), stop=(k == 7)),
                                reads=[("wup", k), ("xnT", k)], writes=[("ps", bb)])
                chain_a(j)
                if not FFN_PIPE:
                    chain_b(j)
                elif j >= 1:
                    chain_b(j - 1)
            if FFN_PIPE:
                chain_b(NJ - 1)
            for sub in range(4):
                for half in range(2):
                    b = 6 + (2 * sub + half) % 2
                    for j in range(NJ):
                        self.op("pe", lambda e, j=j, sub=sub, half=half, b=b: e.matmul(self.ps[b][:, :], lhsT=act[:, j, 128 * sub:128 * sub + 128], rhs=wdn[:, j, 512 * half:512 * half + 512], start=(j == 0), stop=(j == NJ - 1)),
                                reads=[("act", j), ("wdn", j)], writes=[("ps", b)])
                    self.op("dve", lambda e, sub=sub, half=half, b=b: e.tensor_tensor(out=ht[:, sub, 512 * half:512 * half + 512], in0=self.ps[b][:, :], in1=ht[:, sub, 512 * half:512 * half + 512], op=ALU.add),
                            reads=[("ps", b), ("ht", sub)], writes=[("ht", sub)])
            if not final:
                for i in range(4):
                    self.dma("sp", hout[512 * t + 128 * i:512 * t + 128 * i + 128, :], ht[:, i, :], reads=[("ht", i)], writes=[("dram", "hB", t)])
            else:
                w = self.nt_work
                for i in range(4):
                    ss = yss[:, i:i + 1]
                    self.op("act", lambda e, i=i, ss=ss: e.activation(out=w["junk"], in_=ht[:, i, :], func=AF.Square, accum_out=ss), reads=[("ht", i)], writes=["njunk", ("yss", i)])
                    self.op("act", lambda e, ss=ss: e.activation(out=ss, in_=ss, func=AF.Sqrt, scale=1.0 / D, bias=EPS), reads=[("yss", i)], writes=[("yss", i)])
                    self.op("dve", lambda e, ss=ss: e.reciprocal(out=ss, in_=ss), reads=[("yss", i)], writes=[("yss", i)])
                    self.op("act", lambda e, i=i, ss=ss: e.activation(out=ht[:, i, :], in_=ht[:, i, :], func=AF.Copy, scale=ss), reads=[("ht", i), ("yss", i)], writes=[("ht", i)])
                    self.op("dve", lambda e, i=i: e.tensor_tensor(out=ht[:, i, :], in0=ht[:, i, :], in1=self.gF, op=ALU.mult), reads=[("ht", i), "gF"], writes=[("ht", i)])
                    self.dma("sp", self.y[512 * t + 128 * i:512 * t + 128 * i + 128, :], ht[:, i, :], reads=[("ht", i)], writes=[("dram", "y", t)])

    def sweep_kv_front1(self):
        ar, I, S, NT = self.ar, self.I, self.S, self.NT
        wkv = ar.alloc([8, 1536], BF16)
        wfg = ar.alloc([8, 12], BF16)
        win = ar.alloc([8, D], BF16)
        for k in range(8):
            self.load_w(wkv[:, k, :], I["w_kv"][128 * k:128 * k + 128, :], ("wkv", k))
            self.load_w(win[:, k, :], I["w_in"][1, 128 * k:128 * k + 128, :], ("win", k))
        self.dma("pool", wfg, I["w_fgate"].rearrange("(k p) h -> p k h", p=128), writes=["wfg"])
        ht = [ar.alloc([4, D], F32) for _ in range(2)]
        self.alloc_norm_work()
        hsTs = [ar.alloc([8, 512], BF16) for _ in range(2)]
        hnTs = [ar.alloc([8, 512], BF16) for _ in range(2)]
        kst = [ar.alloc([6, 512], BF16) for _ in range(2)]
        qst = [ar.alloc([6, 512], BF16) for _ in range(2)]
        vst = [ar.alloc([4, 768], BF16) for _ in range(2)]
        qmT = ar.alloc([2, 512], BF16)
        PTs = [ar.alloc([512], BF16) for _ in range(2)]
        self.ma_rec = ar.alloc([512], F32)
        mo = [ar.alloc([4, 512], BF16, parts=64) for _ in range(2)]
        ex = ar.alloc([512], F32)

        def load(t):
            for i in range(4):
                self.dma("sp", ht[t % 2][:, i, :], self.hB[512 * t + 128 * i:512 * t + 128 * i + 128, :], reads=[("dram", "hB", t)], writes=[("ht", t % 2, i)])

        load(0)
        self.norm_T(ht[0], [("ht", 0, i) for i in range(4)], [("gkv", hsTs[0], ("hsT", 0)), ("gmix1", hnTs[0], ("hnT", 0))], "kv")
        for t in range(NT):
            if t + 1 < NT:
                load(t + 1)
            s = t % 2
            hsT, hnT = hsTs[s], hnTs[s]
            ks_, kn_ = ("hsT", s), ("hnT", s)
            for m in range(6):
                b = m % 2
                for k in range(8):
                    self.op("pe", lambda e, m=m, k=k, b=b, hsT=hsT: e.matmul(self.ps[b][:, :], lhsT=wkv[:, k, 128 * m:128 * m + 128], rhs=hsT[:, k, :], start=(k == 0), stop=(k == 7)),
                            reads=[("wkv", k), (ks_, k)], writes=[("ps", b)])
                self.op("act", lambda e, m=m, b=b, s=s: e.activation(out=kst[s][:, m, :], in_=self.ps[b][:, :], func=AF.Copy), reads=[("ps", b)], writes=[("kst", s)])
            self.dma("sp", self.kT_d[:, 512 * t:512 * t + 512].rearrange("(m p) t -> p m t", p=128), kst[s], reads=[("kst", s)], writes=[("dram", "kT")])
            for sub in range(4):
                for half in range(2):
                    b = 2 + (2 * sub + half) % 2
                    for k in range(8):
                        self.op("pe", lambda e, sub=sub, half=half, k=k, b=b, hsT=hsT: e.matmul(self.ps[b][:, 0:384], lhsT=hsT[:, k, 128 * sub:128 * sub + 128], rhs=wkv[:, k, 768 + 384 * half:768 + 384 * half + 384], start=(k == 0), stop=(k == 7)),
                                reads=[("wkv", k), (ks_, k)], writes=[("ps", b)])
                    self.op("dve", lambda e, sub=sub, half=half, b=b, s=s: e.tensor_copy(out=vst[s][:, sub, 384 * half:384 * half + 384], in_=self.ps[b][:, 0:384]), reads=[("ps", b)], writes=[("vst", s)])
            self.dma("sp", self.v_d[512 * t:512 * t + 512, :].rearrange("(i p) d -> p i d", p=128), vst[s], reads=[("vst", s)], writes=[("dram", "v")])
            if t + 1 < NT:
                self.norm_T(ht[1 - s], [("ht", 1 - s, i) for i in range(4)], [("gkv", hsTs[1 - s], ("hsT", 1 - s)), ("gmix1", hnTs[1 - s], ("hnT", 1 - s))], "kv")
            b = 4
            for k in range(8):
                self.op("pe", lambda e, k=k, b=b, hsT=hsT: e.matmul(self.ps[b][0:12, :], lhsT=wfg[:, k, :], rhs=hsT[:, k, :], start=(k == 0), stop=(k == 7)),
                        reads=["wfg", (ks_, k)], writes=[("ps", b)])
            self.op("act", lambda e, b=b: e.activation(out=ex[0:12, :], in_=self.ps[b][0:12, :], func=AF.Exp, scale=-1.0, bias=self.nbf[0:12, :]), reads=[("ps", b), "nbf"], writes=["ex"])
            self.op("act", lambda e, t=t: e.activation(out=self.cs[0:12, 512 * t:512 * t + 512], in_=ex[0:12, :], func=AF.Ln, bias=1.0), reads=["ex"], writes=["cs"])
            for m in range(6):
                b = m % 2
                for k in range(8):
                    self.op("pe", lambda e, m=m, k=k, b=b, hnT=hnT: e.matmul(self.ps[b][:, :], lhsT=win[:, k, 128 * m:128 * m + 128], rhs=hnT[:, k, :], start=(k == 0), stop=(k == 7)),
                            reads=[("win", k), (kn_, k)], writes=[("ps", b)])
                self.op("act", lambda e, m=m, b=b, s=s: e.activation(out=qst[s][:, m, :], in_=self.ps[b][:, :], func=AF.Copy, scale=0.125), reads=[("ps", b)], writes=[("qst", s)])
            self.dma("sp", self.qT_d[:, 512 * t:512 * t + 512].rearrange("(m p) t -> p m t", p=128), qst[s], reads=[("qst", s)], writes=[("dram", "qT")])
            for c in range(2):
                b = c % 2
                for k in range(8):
                    self.op("pe", lambda e, c=c, k=k, b=b, hnT=hnT: e.matmul(self.ps[b][:, :], lhsT=win[:, k, 768 + 128 * c:768 + 128 * c + 128], rhs=hnT[:, k, :], start=(k == 0), stop=(k == 7)),
                            reads=[("win", k), (kn_, k)], writes=[("ps", b)])
                self.op("act", lambda e, c=c, b=b: e.activation(out=qmT[:, c, :], in_=self.ps[b][:, :], func=AF.Copy, scale=0.125), reads=[("ps", b)], writes=["qmT"])
            self.mem_attention(1, qmT, "qmT", mo[s], ("mo", s), PTs, "m1")
            self.dma("sp", self.memo_d[1][:, :, 512 * t:512 * t + 512].rearrange("h p t -> p h t"), mo[s], reads=[("mo", s)], writes=[("dram", "memo1")])

    def phase_fox(self):
        ar, S, NT, NB = self.ar, self.S, self.NT, self.NB
        cs = self.cs
        ones12 = ar.alloc([S], F32)
        self.op("pool", lambda e: e.memset(ones12[0:12, :], 1.0), writes=["ones12"])
        cum = ar.alloc([S], F32)
        self.op("dve", lambda e: e.tensor_tensor_scan(out=cum[0:12, :], data0=ones12[0:12, :], data1=cs[0:12, :], initial=0.0, op0=ALU.mult, op1=ALU.add),
                reads=["cs", "ones12"], writes=["cum"])
        csKP = ar.alloc([NB, 12], F32)
        CL = ar.alloc([NB, 12], F32)
        CM = ar.alloc([NB, 12], F32)
        DL = ar.alloc([NB, 12], F32)
        for j in range(NB):
            self.op("pe", lambda e, j=j: e.transpose(out=self.ps[0][:, 12 * j:12 * j + 12], in_=cum[0:12, 128 * j:128 * j + 128], identity=self.ident_f[0:12, 0:12]),
                    reads=["cum", "c_ident_f"], writes=[("ps", 0)])
        self.op("act", lambda e: e.activation(out=csKP.rearrange("p j h -> p (j h)"), in_=self.ps[0][:, 0:12 * NB], func=AF.Copy), reads=[("ps", 0)], writes=["csKP"])
        for (sel, dst, kd, b) in ((self.sel127, CL, "CL", 1), (self.sel63, CM, "CM", 2)):
            self.op("pe", lambda e, sel=sel, b=b: e.matmul(self.ps[b][:, 0:12 * NB], lhsT=sel, rhs=csKP.rearrange("p j h -> p (j h)"), start=True, stop=True),
                    reads=["csKP", "c_sel127", "c_sel63"], writes=[("ps", b)])
            self.op("act", lambda e, dst=dst, b=b: e.activation(out=dst.rearrange("p j h -> p (j h)"), in_=self.ps[b][:, 0:12 * NB], func=AF.Copy), reads=[("ps", b)], writes=[kd])
        for Ig in range(NT):
            self.op("dve", lambda e, Ig=Ig: e.tensor_tensor(out=DL[:, 4 * Ig:4 * Ig + 4, :], in0=CL[:, 4 * Ig + 1:4 * Ig + 2, :].to_broadcast([128, 4, 12]), in1=CM[:, 4 * Ig:4 * Ig + 4, :], op=ALU.subtract),
                    reads=["CL", "CM"], writes=["DL"])
        qa = [ar.alloc([S], BF16) for _ in range(2)]
        ka = [ar.alloc([S], BF16) for _ in range(2)]
        Vh = [ar.alloc([NB, 128], BF16) for _ in range(2)]
        oh = [ar.alloc([S], BF16) for _ in range(2)]
        NPT = 4
        PT = [ar.alloc([512], BF16) for _ in range(NPT)]
        BIAS = [ar.alloc([NB], F32) for _ in range(2)]
        rcs = [ar.alloc([512], F32, parts=1) for _ in range(2)]
        bcs = [ar.alloc([512], F32) for _ in range(2)]
        rch = [ar.alloc([512], BF16, parts=1) for _ in range(2)]
        rcl = [ar.alloc([512], BF16, parts=1) for _ in range(2)]
        onesb = ar.alloc([128], BF16, parts=1)
        self.op("pool", lambda e: e.memset(onesb, 1.0), writes=["onesb"])
        for s in range(2):
            self.op("pool", lambda e, s=s: e.memset(ka[s][64:65, :], 1.0), writes=[("ka1", s)])
            self.op("pool", lambda e, s=s: e.memset(Vh[s][:, :, 0:64], 0.0), writes=[("Vh1", s)])
            self.op("pool", lambda e, s=s: e.memset(Vh[s][:, :, 0:1], 1.0), reads=[("Vh1", s)], writes=[("Vh1", s)])

        def load(h):
            s = h % 2
            self.dma("sp", qa[s][0:64, :], self.qT_d[64 * h:64 * h + 64, :], reads=[("dram", "qT")], writes=[("qa", s)])
            self.dma("sp", ka[s][0:64, :], self.kT_d[64 * h:64 * h + 64, :], reads=[("dram", "kT")], writes=[("ka", s)])
            self.dma("sp", Vh[s][:, :, 64:128], self.v_d[:, 64 * h:64 * h + 64].rearrange("(j p) d -> p j d", p=128), reads=[("dram", "v")], writes=[("Vh", s)])
            self.op("dve", lambda e, s=s, h=h: e.tensor_copy(out=qa[s][64:65, :].rearrange("p (j t) -> p j t", t=128), in_=DL[64:65, :, h:h + 1].to_broadcast([1, NB, 128])),
                    reads=["DL", ("qa1", s)], writes=[("qa1", s)])

        its = []
        for h in range(12):
            for Ig in range(NT):
                for j in range(4 * Ig + 4):
                    its.append((h, Ig, j, 4 * Ig + 4))
        NSB = 3

        def group_pre(h, Ig):
            s = h % 2
            g = h * NT + Ig
            bi = BIAS[g % 2]
            nj = 4 * Ig + 4
            self.op("dve", lambda e, bi=bi, nj=nj, Ig=Ig, h=h: e.tensor_scalar(out=bi[:, 0:nj], in0=csKP[:, 0:nj, h], scalar1=CL[:, 4 * Ig + 1, h:h + 1], scalar2=None, op0=ALU.subtract),
                    reads=["csKP", "CL"], writes=[("BIAS", g % 2)])

        def emit_st(idx):
            h, Ig, j, nj = its[idx]
            s = h % 2
            if j == 0:
                group_pre(h, Ig)
            c0 = max(0, 128 * (j - 4 * Ig))
            bs = idx % NSB
            self.op("pe", lambda e, s=s, j=j, Ig=Ig, c0=c0, bs=bs: e.matmul(self.ps[bs][:, c0:512], lhsT=ka[s][0:65, 128 * j:128 * j + 128], rhs=qa[s][0:65, 512 * Ig + c0:512 * Ig + 512], start=True, stop=True),
                    reads=[("ka", s), ("ka1", s), ("qa", s), ("qa1", s)], writes=[("ps", bs)])

        def emit_exp_pv(idx):
            h, Ig, j, nj = its[idx]
            s = h % 2
            g = h * NT + Ig
            a = j - 4 * Ig
            c0 = max(0, 128 * a)
            bs = idx % NSB
            pt = PT[idx % NPT]
            kpt = ("PTf", idx % NPT)
            bi = BIAS[g % 2]
            po = self.ps[4 + g % 2]
            ko = ("ps", 4 + g % 2)
            self.op("act", lambda e, pt=pt, bs=bs, c0=c0, bi=bi, j=j: e.activation(out=pt[:, c0:512], in_=self.ps[bs][:, c0:512], func=AF.Exp, bias=bi[:, j:j + 1]),
                    reads=[("ps", bs), ("BIAS", g % 2)], writes=[kpt])
            if a >= 0:
                self.op("pool", lambda e, pt=pt, c0=c0: e.tensor_tensor(out=pt[:, c0:c0 + 128], in0=pt[:, c0:c0 + 128], in1=self.mask, op=ALU.mult),
                        reads=[kpt, "c_mask"], writes=[kpt])
            self.op("pe", lambda e, s=s, j=j, pt=pt, c0=c0, po=po, nj=nj: e.matmul(po[:, c0:512], lhsT=Vh[s][:, j, :], rhs=pt[:, c0:512], start=(j == 0), stop=(j == nj - 1), skip_group_check=True),
                    reads=[("Vh", s), ("Vh1", s), kpt], writes=[ko])

        def tail_a(g):
            po = self.ps[4 + g % 2]
            ko = ("ps", 4 + g % 2)
            rc, rh, rl = rcs[g % 2], rch[g % 2], rcl[g % 2]
            self.op("dve", lambda e: e.reciprocal(out=rc[0:1, :], in_=po[0:1, :]), reads=[ko], writes=[("rcs", g % 2)])
            self.op("dve", lambda e: e.tensor_copy(out=rh[0:1, :], in_=rc[0:1, :]), reads=[("rcs", g % 2)], writes=[("rch", g % 2)])
            self.op("dve", lambda e: e.tensor_tensor(out=rl[0:1, :], in0=rc[0:1, :], in1=rh[0:1, :], op=ALU.subtract), reads=[("rcs", g % 2), ("rch", g % 2)], writes=[("rcl", g % 2)])

        def tail(g):
            h, Ig = g // NT, g % NT
            s = h % 2
            po = self.ps[4 + g % 2]
            ko = ("ps", 4 + g % 2)
            rc, bc = rcs[g % 2], bcs[g % 2]
            rh, rl = rch[g % 2], rcl[g % 2]
            self.op("pe", lambda e: e.matmul(self.ps[6][:, :], lhsT=onesb[0:1, 0:128], rhs=rh[0:1, :], start=True, stop=False),
                    reads=["onesb", ("rch", g % 2)], writes=[("ps", 6)])
            self.op("pe", lambda e: e.matmul(self.ps[6][:, :], lhsT=onesb[0:1, 0:128], rhs=rl[0:1, :], start=False, stop=True),
                    reads=["onesb", ("rcl", g % 2)], writes=[("ps", 6)])
            self.op("act", lambda e, bc=bc: e.activation(out=bc[64:128, :], in_=self.ps[6][64:128, :], func=AF.Copy), reads=[("ps", 6)], writes=[("bcs", g % 2)])
            self.op("dve", lambda e, bc=bc, po=po, s=s, Ig=Ig: e.tensor_tensor(out=oh[s][64:128, 512 * Ig:512 * Ig + 512], in0=po[64:128, :], in1=bc[64:128, :], op=ALU.mult),
                    reads=[ko, ("bcs", g % 2)], writes=[("oh", s)])
            if Ig == NT - 1:
                self.dma("sp", self.oT_d[64 * h:64 * h + 64, :], oh[s][64:128, :], reads=[("oh", s)], writes=[("dram", "oT")])
            if self.debug and h == 2:
                self.dma("sp", self.dbg_rc[Ig:Ig + 1, :], rc[0:1, :], reads=[("rcs", g % 2)], writes=[("dram", "dbg")])
                if Ig == NT - 1:
                    self.dma("sp", self.dbg_qa1, qa[s][64:65, :], reads=[("qa1", s)], writes=[("dram", "dbg3")])
                    self.dma("sp", self.dbg_ka1, ka[s][64:65, :], reads=[("ka1", s)], writes=[("dram", "dbg4")])
                    self.dma("sp", self.dbg_v1, Vh[s][:, :, 0:2], reads=[("Vh1", s), ("Vh", s)], writes=[("dram", "dbg5")])

        load(0)
        LOOK = 2
        p_st = 0
        n_it = len(its)
        pending_tail = None
        for idx in range(n_it):
            while p_st < n_it and p_st <= idx + LOOK:
                emit_st(p_st)
                p_st += 1
            h, Ig, j, nj = its[idx]
            g = h * NT + Ig
            if j == 0 and Ig == 0 and h + 1 < 12:
                load(h + 1)
            emit_exp_pv(idx)
            if pending_tail is not None and j == min(6, nj - 1):
                tail(pending_tail)
                pending_tail = None
            if j == nj - 1:
                if pending_tail is not None:
                    tail(pending_tail)
                tail_a(g)
                pending_tail = g
        if pending_tail is not None:
            tail(pending_tail)

    def sweep_wout1(self, after_weights=None):
        ar, I, S, NT = self.ar, self.I, self.S, self.NT
        woT = ar.alloc([6, D], BF16)
        woM = ar.alloc([4, D], BF16, parts=64)
        for q in range(6):
            self.load_w(woT[:, q, :], I["w_out"][1, 128 * q:128 * q + 128, :], ("woT", q))
        for h in range(4):
            self.load_w(woM[:, h, :], I["w_out"][1, 768 + 64 * h:768 + 64 * h + 64, :], ("woM", h))
        if after_weights is not None:
            after_weights()
        ot = [ar.alloc([6, 512], BF16) for _ in range(2)]
        mo = [ar.alloc([4, 512], BF16, parts=64) for _ in range(2)]
        ht = [ar.alloc([4, D], F32) for _ in range(2)]

        def load(t):
            s = t % 2
            self.dma("sp", ot[s], self.oT_d[:, 512 * t:512 * t + 512].rearrange("(q p) t -> p q t", p=128), reads=[("dram", "oT")], writes=[("ot", s)])
            self.dma("sp", mo[s], self.memo_d[1][:, :, 512 * t:512 * t + 512].rearrange("h p t -> p h t"), reads=[("dram", "memo1")], writes=[("mo", s)])
            for i in range(4):
                self.dma("sp", ht[s][:, i, :], self.hB[512 * t + 128 * i:512 * t + 128 * i + 128, :], reads=[("dram", "hB", t), ("dram", "hA", t)], writes=[("ht", s, i)])

        load(0)
        for t in range(NT):
            if t + 1 < NT:
                load(t + 1)
            s = t % 2
            self.wout_residual(ht[s], [("ht", s, i) for i in range(4)],
                               [(ot[s][:, q, :], woT[:, q, :], [("ot", s), ("woT", q)]) for q in range(6)] +
                               [(mo[s][0:64, h, :], woM[0:64, h, :], [("mo", s), ("woM", h)]) for h in range(4)])
            for i in range(4):
                self.dma("sp", self.hA[512 * t + 128 * i:512 * t + 128 * i + 128, :], ht[s][:, i, :], reads=[("ht", s, i)], writes=[("dram", "hA", t)])


_CACHE = {}


def _get_nc(S, debug=False):
    key = (S, debug)
    if key not in _CACHE:
        _CACHE[key] = Builder(S, debug).build()
    return _CACHE[key]


def kernel(**inputs):
    x = np.asarray(inputs["x"], np.float32)
    mem = np.asarray(inputs["mem"], np.float32)
    B, S, _ = x.shape
    nc = _get_nc(S)
    consts = host_consts()
    shared = {k: np.ascontiguousarray(np.asarray(v, np.float32)) for k, v in inputs.items() if k not in ("x", "mem")}
    in_maps = []
    for b in range(B):
        m = dict(shared)
        m.update(consts)
        m["x"] = np.ascontiguousarray(x[b])
        m["mem"] = np.ascontiguousarray(mem[b])
        in_maps.append(m)
    res = run_bass_kernel_spmd(nc, in_maps, core_ids=list(range(B)))
    return np.stack([np.asarray(r["y"], np.float32) for r in res.results], axis=0)
```

```python
import numpy as np
import ml_dtypes
from contextlib import ExitStack
import concourse.bass as bass
import concourse.mybir as mybir
from concourse.bass_utils import run_bass_kernel_spmd

F32 = mybir.dt.float32
BF16 = mybir.dt.bfloat16
AF = mybir.ActivationFunctionType
ALU = mybir.AluOpType

D = 1024
DFF = 2816
NJ = 22
MEM = 256
EPS = 1e-6
FFN_PIPE = True


class _Op:
    __slots__ = ("stream", "fn", "is_dma", "idx", "slot", "cnt", "waits", "clock")


class Sched:
    STREAMS = ("pe", "act", "dve", "pool", "sp")

    def __init__(self, nc, n_dma_sems=40, n_sw=12):
        self.nc = nc
        self.ops = []
        self.n_dma = n_dma_sems
        self.n_sw = n_sw
        self.rr_sw = 0
        self.rr_hw = 0
        self.dma_cnt = [0] * n_dma_sems
        self.last_writer = {}
        self.readers = {}
        self.know = {s: {} for s in self.STREAMS}
        self.count = {s: 0 for s in self.STREAMS}
        self.pending = {s: [] for s in self.STREAMS}

    def barrier(self):
        tgt = {}
        for s in ("pe", "act", "dve", "pool"):
            if self.count[s] > 0:
                tgt[s] = self.count[s]
        for i in range(self.n_dma):
            if self.dma_cnt[i] > 0:
                tgt[("d", i)] = self.dma_cnt[i]
        for s in self.STREAMS:
            know = self.know[s]
            for k, v in tgt.items():
                if k == s == "pe":
                    continue
                if know.get(k, 0) < v:
                    know[k] = v
                    self.pending[s].append((k, v))
        self.last_writer = {k: w for k, w in self.last_writer.items() if isinstance(k, tuple) and k and k[0] == "dram"}
        self.readers = {k: r for k, r in self.readers.items() if k in self.last_writer}

    def add(self, stream, fn, reads=(), writes=(), dma=False):
        op = _Op()
        op.stream = stream
        op.fn = fn
        op.is_dma = dma
        reads = list(reads)
        writes = list(writes)
        deps = []
        for k in reads + writes:
            w = self.last_writer.get(k)
            if w is not None:
                deps.append(w)
        for k in writes:
            deps.extend(self.readers.get(k, ()))
        know = self.know[stream]
        waits = self.pending[stream]
        self.pending[stream] = []

        def need(key, val, clock):
            if know.get(key, 0) >= val:
                return
            waits.append((key, val))
            if clock is not None:
                for kk, vv in clock.items():
                    if know.get(kk, 0) < vv:
                        know[kk] = vv
            know[key] = max(know.get(key, 0), val)

        if dma:
            if stream == "pool":
                slot = self.rr_sw
                self.rr_sw = (self.rr_sw + 1) % self.n_sw
            else:
                slot = self.n_sw + self.rr_hw
                self.rr_hw = (self.rr_hw + 1) % (self.n_dma - self.n_sw)
            prev = self.dma_cnt[slot]
            if prev > 0:
                need(("d", slot), prev, None)
            self.dma_cnt[slot] = prev + 1
            op.slot = slot
            op.cnt = prev + 1
        for p in deps:
            if p.is_dma:
                need(("d", p.slot), p.cnt, p.clock)
            else:
                if p.stream == "pe" and stream == "pe" and not dma:
                    continue
                need(p.stream, p.idx, p.clock)
        op.waits = waits
        op.clock = dict(know)
        if dma:
            op.idx = None
            op.clock[("d", op.slot)] = op.cnt
        else:
            self.count[stream] += 1
            op.idx = self.count[stream]
            op.clock[stream] = op.idx
        for k in reads:
            self.readers.setdefault(k, []).append(op)
        for k in writes:
            self.last_writer[k] = op
            self.readers[k] = []
        self.ops.append(op)
        return op

    def emit(self, es, final_wait_keys=()):
        nc = self.nc
        sems = {}
        for s in ("pe", "act", "dve", "pool"):
            sems[s] = es.enter_context(nc.semaphore("c_" + s))
        for i in range(self.n_dma):
            sems[("d", i)] = es.enter_context(nc.semaphore("d_%d" % i))
        fin = list(self.pending["sp"])
        know = self.know["sp"]
        for k in final_wait_keys:
            w = self.last_writer.get(k)
            if w is None:
                continue
            key, val = (("d", w.slot), w.cnt) if w.is_dma else (w.stream, w.idx)
            if know.get(key, 0) < val:
                know[key] = val
                fin.append((key, val))
        block = es.enter_context(nc.Block())
        by = {s: [o for o in self.ops if o.stream == s] for s in self.STREAMS}

        def run(eng, stream, extra=()):
            for o in by[stream]:
                for key, val in o.waits:
                    eng.wait_ge(sems[key], val * (16 if isinstance(key, tuple) else 1))
                ins = o.fn(eng)
                if o.is_dma:
                    ins.then_inc(sems[("d", o.slot)], 16)
                else:
                    ins.then_inc(sems[stream], 1)
            for key, val in extra:
                eng.wait_ge(sems[key], val * (16 if isinstance(key, tuple) else 1))

        @block.tensor
        def _(e):
            run(e, "pe")

        @block.scalar
        def _(e):
            run(e, "act")

        @block.vector
        def _(e):
            run(e, "dve")

        @block.gpsimd
        def _(e):
            run(e, "pool")

        @block.sync
        def _(e):
            run(e, "sp", fin)


class Arena:
    def __init__(self, nc, es, words):
        self.t = es.enter_context(nc.sbuf_tensor("arena", [128, words], F32))
        self.off = 0
        self.words = words

    def mark(self):
        return self.off

    def release(self, m):
        self.off = m

    def alloc(self, shape, dtype=F32, parts=128):
        shape = [int(s) for s in shape]
        n = int(np.prod(shape))
        sz = 4 if dtype == F32 else 2
        words = (n * sz + 3) // 4
        words = (words + 7) // 8 * 8
        assert self.off + words <= self.words, "arena overflow %d + %d > %d" % (self.off, words, self.words)
        ap = self.t[0:parts, self.off:self.off + words]
        self.off += words
        if dtype != F32:
            ap = ap.bitcast(dtype)
        ap = ap[:, 0:n]
        if len(shape) > 1:
            names = " ".join("d%d" % i for i in range(len(shape)))
            kw = {"d%d" % i: s for i, s in enumerate(shape)}
            ap = ap.rearrange("p (%s) -> p %s" % (names, names), **kw)
        return ap


def host_consts():
    c = {}
    c["c_ident_bf"] = np.eye(128).astype(ml_dtypes.bfloat16)
    c["c_ident_f"] = np.eye(128).astype(np.float32)
    m = (np.arange(128)[None, :] >= np.arange(128)[:, None]).astype(np.float32)
    c["c_mask"] = m.astype(ml_dtypes.bfloat16)
    c["c_ones_bf"] = np.ones((128, 128), ml_dtypes.bfloat16)
    s127 = np.zeros((128, 128), np.float32)
    s127[127, :] = 1
    s63 = np.zeros((128, 128), np.float32)
    s63[63, :] = 1
    c["c_sel127"] = s127
    c["c_sel63"] = s63
    return c


class Builder:
    def __init__(self, S=4096, debug=False):
        self.S = S
        self.NT = S // 512
        self.Ns = S // 8
        self.NB = S // 128
        self.debug = debug
        self.nc = bass.Bass("TRN2", target_bir_lowering=False)
        self.es = ExitStack()
        self.uid = 0

    def din(self, name, shape, dt=F32):
        return self.nc.dram_tensor(name, list(shape), dt, kind="ExternalInput").ap()

    def dscr(self, name, shape, dt):
        kind = "ExternalOutput" if self.debug else "Internal"
        return self.nc.dram_tensor(name, list(shape), dt, kind=kind).ap()

    def op(self, stream, fn, reads=(), writes=()):
        return self.sc.add(stream, fn, reads, writes)

    def dma(self, stream, out, in_, reads=(), writes=(), slow=False):
        if slow:
            return self.sc.add(stream, lambda e: e.dma_start(out=out, in_=in_, allow_slow_non_contiguous=True), reads, writes, dma=True)
        return self.sc.add(stream, lambda e: e.dma_start(out=out, in_=in_), reads, writes, dma=True)

    def load_w(self, dst, src, key):
        cols = src.shape[-1]
        step = 2048
        c = 0
        while c < cols:
            n = min(step, cols - c)
            self.dma("pool", dst[:, c:c + n], src[:, c:c + n], writes=[key])
            c += n

    def new_phase(self, mark):
        self.sc.barrier()
        self.ar.release(mark)

    def build(self):
        nc, es = self.nc, self.es
        S, NT, Ns, NB = self.S, self.NT, self.Ns, self.NB
        with es:
            self.sc = Sched(nc)
            self.ar = Arena(nc, es, 53000)
            self.ps = [es.enter_context(nc.psum_tensor("ps%d" % i, [128, 512], F32)) for i in range(8)]
            I = {}
            I["x"] = self.din("x", [S, D])
            I["mem"] = self.din("mem", [MEM, D])
            for nm, shp in [("g_mix", [2, D]), ("w_in", [2, D, D]), ("w_out", [2, D, D]), ("g_mem", [D]),
                            ("w_mem_kv", [2, D, 512]), ("s5_a_re", [1, 48, 64]), ("s5_a_im", [1, 48, 64]),
                            ("s5_log_dt", [1, 48]), ("s5_b_re", [1, 48, 64, 16]), ("s5_b_im", [1, 48, 64, 16]),
                            ("s5_c_re", [1, 48, 16, 64]), ("s5_c_im", [1, 48, 16, 64]), ("s5_d", [1, 768]),
                            ("w_glu", [1, 768, 768]), ("g_kv", [D]), ("w_kv", [D, 1536]), ("w_fgate", [D, 12]),
                            ("b_fgate", [12]), ("g_ffn", [2, D]), ("w_ffn_up", [2, D, 2 * DFF]), ("conv_w", [2, 3, DFF]),
                            ("conv_b", [2, DFF]), ("w_ffn_down", [2, DFF, D]), ("g_final", [D])]:
                I[nm] = self.din(nm, shp)
            I["c_ident_bf"] = self.din("c_ident_bf", [128, 128], BF16)
            I["c_ident_f"] = self.din("c_ident_f", [128, 128], F32)
            I["c_mask"] = self.din("c_mask", [128, 128], BF16)
            I["c_ones_bf"] = self.din("c_ones_bf", [128, 128], BF16)
            I["c_sel127"] = self.din("c_sel127", [128, 128], F32)
            I["c_sel63"] = self.din("c_sel63", [128, 128], F32)
            self.I = I
            self.y = nc.dram_tensor("y", [S, D], F32, kind="ExternalOutput").ap()
            self.uT_d = self.dscr("uT_d", [8, 96, 8, Ns], BF16)
            self.gT_d = self.dscr("gT_d", [8, 96, S], BF16)
            self.memo_d = [self.dscr("memo_d%d" % l, [4, 64, S], BF16) for l in range(2)]
            self.hA = self.dscr("hA", [S, D], F32)
            self.hB = self.dscr("hB", [S, D], F32)
            self.kT_d = self.dscr("kT_d", [768, S], BF16)
            self.qT_d = self.dscr("qT_d", [768, S], BF16)
            self.v_d = self.dscr("v_d", [S, 768], BF16)
            self.oT_d = self.dscr("oT_d", [768, S], BF16)
            self.E_d = self.dscr("E_d", [8, 2, 128, 3 * Ns], F32)

            if self.debug:
                self.dbg_rc = self.dscr("dbg_rc", [NT, 512], F32)
                self.dbg_bc = self.dscr("dbg_bc", [NT, 65, 512], F32)
                self.dbg_qa1 = self.dscr("dbg_qa1", [1, S], BF16)
                self.dbg_ka1 = self.dscr("dbg_ka1", [1, S], BF16)
                self.dbg_v1 = self.dscr("dbg_v1", [128, NB, 2], BF16)
            self.setup_persistent()
            m0 = self.ar.mark()
            self.s5_prep()
            m1 = self.ar.mark()
            self.phase_front0()
            self.new_phase(m1)
            self.phase_s5()
            self.new_phase(m0)
            wup0 = self.ar.alloc([8, 2 * DFF], BF16)
            mA = self.ar.mark()
            self.sweep_glu_wout0(after_weights=lambda: self.load_wup(0, wup0))
            self.new_phase(mA)
            self.sweep_ffn(0, self.hA, self.hB, final=False, wup=wup0)
            self.new_phase(m0)
            self.cs = self.ar.alloc([S], F32)
            m2 = self.ar.mark()
            self.sweep_kv_front1()
            self.new_phase(m2)
            self.phase_fox()
            self.new_phase(m0)
            wup1 = self.ar.alloc([8, 2 * DFF], BF16)
            mB = self.ar.mark()
            self.sweep_wout1(after_weights=lambda: self.load_wup(1, wup1))
            self.new_phase(mB)
            self.sweep_ffn(1, self.hA, None, final=True, wup=wup1)
            self.sc.emit(es, final_wait_keys=[("dram", "y", t) for t in range(NT)])
        return nc

    def setup_persistent(self):
        ar, I = self.ar, self.I
        self.ident_bf = ar.alloc([128], BF16)
        self.ident_f = ar.alloc([128], F32)
        self.mask = ar.alloc([128], BF16)
        self.ones_bf = ar.alloc([128], BF16)
        self.sel127 = ar.alloc([128], F32)
        self.sel63 = ar.alloc([128], F32)
        for nm, dst in [("c_ident_bf", self.ident_bf), ("c_ident_f", self.ident_f), ("c_mask", self.mask),
                        ("c_ones_bf", self.ones_bf), ("c_sel127", self.sel127), ("c_sel63", self.sel63)]:
            self.dma("sp", dst, I[nm], writes=[nm])
        self.gcol = {}
        self.late = []
        for nm, src in [("gmix0", I["g_mix"][0]), ("gmem", I["g_mem"]), ("gmix1", I["g_mix"][1]), ("gffn0", I["g_ffn"][0]),
                        ("gffn1", I["g_ffn"][1]), ("gkv", I["g_kv"])]:
            t = ar.alloc([8], F32)
            f = lambda t=t, src=src, nm=nm: self.dma("sp", t, src.rearrange("(k p) -> p k", p=128), writes=[nm], slow=True)
            if nm in ("gmix0", "gmem"):
                f()
            else:
                self.late.append(f)
            self.gcol[nm] = t
        self.gF = ar.alloc([D], F32)
        self.late.append(lambda: self.dma("sp", self.gF, I["g_final"].partition_broadcast(128), writes=["gF"]))
        self.cw = []
        self.cb = []
        for l in range(2):
            t = ar.alloc([NJ, 3], F32)
            for kk in range(3):
                self.late.append(lambda t=t, l=l, kk=kk: self.dma("sp", t[:, :, kk], I["conv_w"][l, kk].rearrange("(j p) -> p j", p=128), reads=[("cw", l)], writes=[("cw", l)], slow=True))
            self.cw.append(t)
            t2 = ar.alloc([NJ], F32)
            self.late.append(lambda t2=t2, l=l: self.dma("sp", t2, I["conv_b"][l].rearrange("(j p) -> p j", p=128), writes=[("cb", l)], slow=True))
            self.cb.append(t2)
        self.nbf = ar.alloc([1], F32)
        self.late.append(lambda: self.dma("sp", self.nbf[0:12, :], I["b_fgate"].rearrange("(p o) -> p o", o=1), writes=["nbf"], slow=True))
        self.late.append(lambda: self.op("dve", lambda e: e.tensor_scalar(out=self.nbf[0:12, :], in0=self.nbf[0:12, :], scalar1=-1.0, scalar2=None, op0=ALU.mult),
                                         reads=["nbf"], writes=["nbf"]))
        self.zrow = ar.alloc([128], BF16)
        self.op("dve", lambda e: e.memset(self.zrow, 0.0), writes=["zrow"])
        self.KTm = ar.alloc([2, 2, MEM], BF16)
        self.Vm = ar.alloc([2, 2, 256], BF16)
        m = ar.mark()
        mt = ar.alloc([2, D], F32)
        mn = ar.alloc([2, D], BF16)
        mnT = ar.alloc([8, MEM], BF16)
        wmk = ar.alloc([2, 8, 512], BF16)
        ss = ar.alloc([2], F32)
        junk = ar.alloc([D], F32)
        for i in range(2):
            self.dma("sp", mt[:, i, :], I["mem"][128 * i:128 * i + 128, :], writes=[("mt", i)])
        for l in range(2):
            for k in range(8):
                self.load_w(wmk[:, l, k, :], I["w_mem_kv"][l, 128 * k:128 * k + 128, :], ("wmk", l, k))
        for i in range(2):
            self.rms_rows(mt[:, i, :], mn[:, i, :], ss[:, i:i + 1], junk, ("mt", i), ("mn", i), ("mss", i), "mjunk")
        for k in range(8):
            pb = self.ps[k % 2].bitcast(BF16)
            for i in range(2):
                self.op("pe", lambda e, k=k, i=i, pb=pb: e.transpose(out=pb[:, 128 * i:128 * i + 128], in_=mn[:, i, 128 * k:128 * k + 128], identity=self.ident_bf),
                        reads=[("mn", i), "c_ident_bf"], writes=[("ps", k % 2)])
            self.op("dve", lambda e, k=k, pb=pb: e.tensor_scalar(out=mnT[:, k, :], in0=pb[:, 0:256], scalar1=self.gcol["gmem"][:, k:k + 1], scalar2=None, op0=ALU.mult),
                    reads=[("ps", k % 2), "gmem"], writes=[("mnT", k)])
        for l in range(2):
            for c in range(2):
                b = 2 + (2 * l + c) % 2
                for k in range(8):
                    self.op("pe", lambda e, l=l, c=c, k=k, b=b: e.matmul(self.ps[b][:, 0:256], lhsT=wmk[:, l, k, 128 * c:128 * c + 128], rhs=mnT[:, k, :], start=(k == 0), stop=(k == 7)),
                            reads=[("wmk", l, k), ("mnT", k)], writes=[("ps", b)])
                self.op("act", lambda e, l=l, c=c, b=b: e.activation(out=self.KTm[:, l, c, :], in_=self.ps[b][:, 0:256], func=AF.Copy),
                        reads=[("ps", b)], writes=[("KTm", l)])
            for kb in range(2):
                b = 4 + kb
                for k in range(8):
                    self.op("pe", lambda e, l=l, kb=kb, k=k, b=b: e.matmul(self.ps[b][:, 0:256], lhsT=mnT[:, k, 128 * kb:128 * kb + 128], rhs=wmk[:, l, k, 256:512], start=(k == 0), stop=(k == 7)),
                            reads=[("wmk", l, k), ("mnT", k)], writes=[("ps", b)])
                self.op("act", lambda e, l=l, kb=kb, b=b: e.activation(out=self.Vm[:, l, kb, :], in_=self.ps[b][:, 0:256], func=AF.Copy),
                        reads=[("ps", b)], writes=[("Vm", l)])
        self.sc.barrier()
        ar.release(m)

    def rms_rows(self, src, dst_bf, ss, junk, ksrc, kdst, kss, kjunk):
        self.op("act", lambda e: e.activation(out=junk, in_=src, func=AF.Square, accum_out=ss), reads=[ksrc], writes=[kjunk, kss])
        self.op("act", lambda e: e.activation(out=ss, in_=ss, func=AF.Sqrt, scale=1.0 / D, bias=EPS), reads=[kss], writes=[kss])
        self.op("dve", lambda e: e.reciprocal(out=ss, in_=ss), reads=[kss], writes=[kss])
        self.op("act", lambda e: e.activation(out=dst_bf, in_=src, func=AF.Copy, scale=ss), reads=[ksrc, kss], writes=[kdst])

    def norm_T(self, ht, hkeys, outs, tag):
        w = self.nt_work
        for i in range(4):
            sl = i % 2
            self.rms_rows(ht[:, i, :], w["xn"][:, sl, :], w["ss"][:, i:i + 1], w["junk"], hkeys[i], ("xn", sl), ("nss", i), "njunk")
            bi = 6 + i % 2
            pb = self.ps[bi].bitcast(BF16)
            for k in range(8):
                self.op("pe", lambda e, k=k, sl=sl, pb=pb: e.transpose(out=pb[:, 128 * k:128 * k + 128], in_=w["xn"][:, sl, 128 * k:128 * k + 128], identity=self.ident_bf),
                        reads=[("xn", sl), "c_ident_bf"], writes=[("ps", bi)])
            for gi, (gname, dst, dkey) in enumerate(outs):
                self.op("dve", lambda e, i=i, pb=pb, gname=gname, dst=dst: e.tensor_tensor(
                    out=dst[:, :, 128 * i:128 * i + 128], in0=pb.rearrange("p (k t) -> p k t", k=8),
                    in1=self.gcol[gname].unsqueeze(2).to_broadcast([128, 8, 128]), op=ALU.mult),
                    reads=[("ps", bi), gname], writes=[(dkey, k) for k in range(8)])

    def alloc_norm_work(self):
        ar = self.ar
        self.nt_work = {"xn": ar.alloc([2, D], BF16), "ss": ar.alloc([4], F32), "junk": ar.alloc([D], BF16)}

    def mem_attention(self, l, qmT, qkey, mo, mokey, PTs, tag):
        def st(i):
            hh, kb = i // 2, i % 2
            c, base = hh // 2, 64 * (hh % 2)
            bs = 2 + i % 2
            self.op("pe", lambda e: e.matmul(self.ps[bs][:, :], lhsT=self.KTm[base:base + 64, l, c, 128 * kb:128 * kb + 128], rhs=qmT[base:base + 64, c, :], start=True, stop=True),
                    reads=[("KTm", l), qkey], writes=[("ps", bs)])

        def pv(i):
            hh, kb = i // 2, i % 2
            bs = 2 + i % 2
            bo, bd = (4, 5) if hh % 2 == 0 else (0, 1)
            pt = PTs[i % 2]
            self.op("act", lambda e: e.activation(out=pt, in_=self.ps[bs][:, :], func=AF.Exp), reads=[("ps", bs)], writes=[("PT", i % 2)])
            self.op("pe", lambda e: e.matmul(self.ps[bo][0:64, :], lhsT=self.Vm[:, l, kb, 64 * hh:64 * hh + 64], rhs=pt, start=(kb == 0), stop=(kb == 1)),
                    reads=[("Vm", l), ("PT", i % 2)], writes=[("ps", bo)])
            self.op("pe", lambda e: e.matmul(self.ps[bd][0:64, :], lhsT=self.ones_bf[:, 0:64], rhs=pt, start=(kb == 0), stop=(kb == 1)),
                    reads=["c_ones_bf", ("PT", i % 2)], writes=[("ps", bd)])

        def tail(hh):
            bo, bd = (4, 5) if hh % 2 == 0 else (0, 1)
            rec = self.ma_rec
            self.op("dve", lambda e: e.reciprocal(out=rec[0:64, :], in_=self.ps[bd][0:64, :]), reads=[("ps", bd)], writes=["marec"])
            self.op("dve", lambda e: e.tensor_tensor(out=mo[0:64, hh, :], in0=self.ps[bo][0:64, :], in1=rec[0:64, :], op=ALU.mult),
                    reads=[("ps", bo), "marec"], writes=[mokey])

        st(0)
        for i in range(8):
            if i + 1 < 8:
                st(i + 1)
            pv(i)
            if i % 2 == 0 and i >= 2:
                tail(i // 2 - 1)
        tail(3)

    def s5_prep(self):
        ar, I, sc = self.ar, self.I, self.sc
        nsteps = max(1, int(np.ceil(np.log2(self.Ns))))
        self.ks_steps = nsteps
        self.Win = ar.alloc([8, 8, 2, 128], BF16, parts=96)
        self.Kblk = ar.alloc([8, 8, 96], BF16, parts=96)
        self.Wout = ar.alloc([24, 8, 2, 32], BF16)
        self.r8 = ar.alloc([24], F32)
        self.EP = ar.alloc([2, nsteps, 24], F32)
        m = ar.mark()
        T = lambda shape=(24,): ar.alloc(list(shape), F32)
        are, aim, ldt = T(), T(), T()
        Bre, Bim = T((24, 16)), T((24, 16))
        Cre, Cim = T((24, 16)), T((24, 16))
        Yre, Yim = ar.alloc([8, 2, 64], F32, parts=48), ar.alloc([8, 2, 64], F32, parts=48)
        for gl in range(2):
            P = slice(64 * gl, 64 * gl + 64)
            self.dma("sp", are[P, :], I["s5_a_re"][0, gl::2, :].rearrange("g p -> p g"), writes=["are"], slow=True)
            self.dma("sp", aim[P, :], I["s5_a_im"][0, gl::2, :].rearrange("g p -> p g"), writes=["aim"], slow=True)
            self.dma("sp", ldt[P, :], I["s5_log_dt"][0, gl::2].partition_broadcast(64), writes=["ldt"], slow=True)
            self.dma("sp", Bre[P, :, :], I["s5_b_re"][0, gl::2].rearrange("g p i -> p g i"), writes=["Bre"], slow=True)
            self.dma("sp", Bim[P, :, :], I["s5_b_im"][0, gl::2].rearrange("g p i -> p g i"), writes=["Bim"], slow=True)
        for k in range(3):
            for (Y, nm, ky) in ((Yre, "s5_c_re", "Yre"), (Yim, "s5_c_im", "Yim")):
                src = I[nm][0].rearrange("(q k gl) i p -> k q i gl p", k=3, gl=2)[k]
                for q in range(8):
                    self.dma("sp", Y[16 * k:16 * k + 16, q, :, :], src[q], writes=[ky], slow=True)
        for (Y, Cx, ky, kc) in ((Yre, Cre, "Yre", "Cre"), (Yim, Cim, "Yim", "Cim")):
            for q in range(8):
                b = q % 2
                self.op("pe", lambda e, Y=Y, q=q, b=b: e.transpose(out=self.ps[b][:, 0:48], in_=Y[0:48, q, :, :].rearrange("p a b -> p (a b)"), identity=self.ident_f[0:48, 0:48]),
                        reads=[ky, "c_ident_f"], writes=[("ps", b)])
                self.op("act", lambda e, Cx=Cx, q=q, b=b: e.activation(out=Cx[:, 3 * q:3 * q + 3, :], in_=self.ps[b][:, 0:48].rearrange("p (k i) -> p k i", k=3), func=AF.Copy),
                        reads=[("ps", b)], writes=[kc])
        V = "dve"

        def tt(out, a, b, op, r, w):
            self.op(V, lambda e: e.tensor_tensor(out=out, in0=a, in1=b, op=op), reads=r, writes=w)

        def ts(out, a, s1, s2, op0, op1, r, w):
            if s2 is None:
                self.op(V, lambda e: e.tensor_scalar(out=out, in0=a, scalar1=s1, scalar2=None, op0=op0), reads=r, writes=w)
            else:
                self.op(V, lambda e: e.tensor_scalar(out=out, in0=a, scalar1=s1, scalar2=s2, op0=op0, op1=op1), reads=r, writes=w)

        dt, lre, mag, ph, sn, cs = T(), T(), T(), T(), T(), T()
        t1, t2, t3 = T(), T(), T()
        self.op("act", lambda e: e.activation(out=dt, in_=ldt, func=AF.Exp), reads=["ldt"], writes=["dt"])
        ts(lre, are, -1e-4, None, ALU.min, None, ["are"], ["lre"])
        tt(t1, lre, dt, ALU.mult, ["lre", "dt"], ["t1"])
        self.op("act", lambda e: e.activation(out=mag, in_=t1, func=AF.Exp), reads=["t1"], writes=["mag"])
        self.op("act", lambda e: e.activation(out=self.r8, in_=t1, func=AF.Exp, scale=8.0), reads=["t1"], writes=["r8"])
        tt(ph, aim, dt, ALU.mult, ["aim", "dt"], ["ph"])
        hpi = ar.alloc([1], F32)
        self.op(V, lambda e: e.memset(hpi, float(np.pi / 2)), writes=["hpi"])
        self.op("act", lambda e: e.activation(out=sn, in_=ph, func=AF.Sin, scale=1.0 / 16), reads=["ph"], writes=["sn"])
        self.op("act", lambda e: e.activation(out=cs, in_=ph, func=AF.Sin, scale=1.0 / 16, bias=hpi), reads=["ph", "hpi"], writes=["cs"])
        for it in range(4):
            tt(t1, cs, cs, ALU.mult, ["cs"], ["t1"])
            tt(t2, sn, sn, ALU.mult, ["sn"], ["t2"])
            tt(t3, cs, sn, ALU.mult, ["cs", "sn"], ["t3"])
            tt(cs, t1, t2, ALU.subtract, ["t1", "t2"], ["cs"])
            ts(sn, t3, 2.0, None, ALU.mult, None, ["t3"], ["sn"])
        PW = T((2, 9, 24))
        self.op(V, lambda e: e.memset(PW[:, 0, 0, :], 1.0), writes=["PW"])
        self.op(V, lambda e: e.memset(PW[:, 1, 0, :], 0.0), reads=["PW"], writes=["PW"])
        tt(PW[:, 0, 1, :], mag, cs, ALU.mult, ["mag", "cs", "PW"], ["PW"])
        tt(PW[:, 1, 1, :], mag, sn, ALU.mult, ["mag", "sn", "PW"], ["PW"])

        def cmul(or_, oi, ar_, ai_, br_, bi_, rk, wk):
            tt(t1, ar_, br_, ALU.mult, rk, ["t1"])
            tt(t2, ai_, bi_, ALU.mult, rk, ["t2"])
            tt(or_, t1, t2, ALU.subtract, ["t1", "t2"] + rk, wk)
            tt(t1, ar_, bi_, ALU.mult, rk, ["t1"])
            tt(t2, ai_, br_, ALU.mult, rk, ["t2"])
            tt(oi, t1, t2, ALU.add, ["t1", "t2"] + rk, wk)

        for k in range(2, 9):
            cmul(PW[:, 0, k, :], PW[:, 1, k, :], PW[:, 0, k - 1, :], PW[:, 1, k - 1, :], PW[:, 0, 1, :], PW[:, 1, 1, :], ["PW"], ["PW"])
        EP = self.EP
        rr = T()
        self.op(V, lambda e: e.reciprocal(out=rr, in_=self.r8), reads=["r8"], writes=["rr"])
        tt(EP[:, 0, 0, :], PW[:, 0, 8, :], rr, ALU.mult, ["PW", "rr", "EP"], ["EP"])
        tt(EP[:, 1, 0, :], PW[:, 1, 8, :], rr, ALU.mult, ["PW", "rr", "EP"], ["EP"])
        for k in range(1, nsteps):
            cmul(EP[:, 0, k, :], EP[:, 1, k, :], EP[:, 0, k - 1, :], EP[:, 1, k - 1, :], EP[:, 0, k - 1, :], EP[:, 1, k - 1, :], ["EP"], ["EP"])
        den, am1, zre, zim = T(), T(), T(), T()
        tt(t1, lre, lre, ALU.mult, ["lre"], ["t1"])
        tt(t2, aim, aim, ALU.mult, ["aim"], ["t2"])
        tt(den, t1, t2, ALU.add, ["t1", "t2"], ["den"])
        self.op(V, lambda e: e.reciprocal(out=den, in_=den), reads=["den"], writes=["den"])
        ts(am1, PW[:, 0, 1, :], -1.0, None, ALU.add, None, ["PW"], ["am1"])
        tt(t1, am1, lre, ALU.mult, ["am1", "lre"], ["t1"])
        tt(t2, PW[:, 1, 1, :], aim, ALU.mult, ["PW", "aim"], ["t2"])
        tt(t3, t1, t2, ALU.add, ["t1", "t2"], ["t3"])
        tt(zre, t3, den, ALU.mult, ["t3", "den"], ["zre"])
        tt(t1, PW[:, 1, 1, :], lre, ALU.mult, ["PW", "lre"], ["t1"])
        tt(t2, am1, aim, ALU.mult, ["am1", "aim"], ["t2"])
        tt(t3, t1, t2, ALU.subtract, ["t1", "t2"], ["t3"])
        tt(zim, t3, den, ALU.mult, ["t3", "den"], ["zim"])
        bbr, bbi, u1, u2 = T((24, 16)), T((24, 16)), T((24, 16)), T((24, 16))

        def bc(a):
            return a.unsqueeze(2).to_broadcast([128, 24, 16])

        def cmul3(or_, oi, pr, pi, br_, bi_, rk, wk):
            tt(u1, br_, bc(pr), ALU.mult, rk, ["u1"])
            tt(u2, bi_, bc(pi), ALU.mult, rk, ["u2"])
            tt(or_, u1, u2, ALU.subtract, ["u1", "u2"] + rk, wk)
            tt(u1, bi_, bc(pr), ALU.mult, rk, ["u1"])
            tt(u2, br_, bc(pi), ALU.mult, rk, ["u2"])
            tt(oi, u1, u2, ALU.add, ["u1", "u2"] + rk, wk)

        cmul3(bbr, bbi, zre, zim, Bre, Bim, ["zre", "zim", "Bre", "Bim"], ["bb"])
        W0 = T((24, 2, 32))
        self.op(V, lambda e: e.memset(W0, 0.0), writes=["W0"])
        self.op("pool", lambda e: e.memset(self.Wout, 0.0), writes=["Wout"])
        wr, wi = T((24, 16)), T((24, 16))
        for gl in range(2):
            P = slice(64 * gl, 64 * gl + 64)
            self.op(V, lambda e, P=P, gl=gl: e.tensor_copy(out=W0[P, :, 0, 16 * gl:16 * gl + 16], in_=Cre[P, :, :]), reads=["Cre", "W0"], writes=["W0"])
            self.op(V, lambda e, P=P, gl=gl: e.tensor_scalar(out=W0[P, :, 1, 16 * gl:16 * gl + 16], in0=Cim[P, :, :], scalar1=-1.0, scalar2=None, op0=ALU.mult), reads=["Cim", "W0"], writes=["W0"])
        for l in range(8):
            cmul3(wr, wi, PW[:, 0, l + 1, :], PW[:, 1, l + 1, :], Cre, Cim, ["PW", "Cre", "Cim"], ["wrwi"])
            for gl in range(2):
                P = slice(64 * gl, 64 * gl + 64)
                self.op(V, lambda e, P=P, gl=gl, l=l: e.tensor_copy(out=self.Wout[P, :, l, 0, 16 * gl:16 * gl + 16], in_=wr[P, :, :]), reads=["wrwi", "Wout"], writes=["Wout"])
                self.op(V, lambda e, P=P, gl=gl, l=l: e.tensor_scalar(out=self.Wout[P, :, l, 1, 16 * gl:16 * gl + 16], in0=wi[P, :, :], scalar1=-1.0, scalar2=None, op0=ALU.mult), reads=["wrwi", "Wout"], writes=["Wout"])
        Kst = ar.alloc([24, 8, 32], BF16, parts=32)
        xr, xi = T((24, 16)), T((24, 16))
        Xpad = T((4, 8, 2, 3, 32))
        for qh in range(2):
            self.op("pool", lambda e: e.memset(Xpad, 0.0), reads=["Xpad"], writes=["Xpad"])
            for j in range(8):
                cmul3(xr, xi, PW[:, 0, 7 - j, :], PW[:, 1, 7 - j, :], bbr, bbi, ["PW", "bb"], ["xrxi"])
                for gl in range(2):
                    P = slice(64 * gl, 64 * gl + 64)
                    for c, xx in ((0, xr), (1, xi)):
                        self.op(V, lambda e, P=P, gl=gl, j=j, c=c, xx=xx, qh=qh: e.tensor_copy(
                            out=Xpad[P, :, j, c, :, 16 * gl:16 * gl + 16],
                            in_=xx[P, 12 * qh:12 * qh + 12, :].rearrange("p (q k) i -> p q k i", k=3)),
                            reads=["xrxi", "Xpad"], writes=["Xpad"])
            n = 0
            for ql in range(4):
                q = 4 * qh + ql
                for j in range(8):
                    for c in range(2):
                        b = (n // 4) % 2
                        col = 128 * (n % 4)
                        self.op("pe", lambda e, ql=ql, j=j, c=c, b=b, col=col: e.transpose(
                            out=self.ps[b][0:96, col:col + 128], in_=Xpad[:, ql, j, c, :, :].rearrange("p k i -> p (k i)"), identity=self.ident_f),
                            reads=["Xpad", "c_ident_f"], writes=[("ps", b)])
                        if n % 4 == 3:
                            j0 = j - 1
                            self.op("act", lambda e, q=q, j0=j0, b=b: e.activation(
                                out=self.Win[:, q, j0:j0 + 2, :, :].rearrange("p a b s -> p (a b s)"), in_=self.ps[b][0:96, :], func=AF.Copy),
                                reads=[("ps", b)], writes=["Win"])
                        n += 1
            n = 0
            for ql in range(4):
                for k in range(3):
                    pair = 3 * (4 * qh + ql) + k
                    for tau in range(8):
                        b = 2 + (n // 16) % 2
                        col = 32 * (n % 16)
                        for c in range(2):
                            self.op("pe", lambda e, ql=ql, k=k, pair=pair, tau=tau, c=c, b=b, col=col: e.matmul(
                                self.ps[b][0:32, col:col + 32], lhsT=Xpad[:, ql, 7 - tau, c, k, :], rhs=W0[:, pair, c, :], start=(c == 0), stop=(c == 1)),
                                reads=["Xpad", "W0"], writes=[("ps", b)])
                        if n % 16 == 15:
                            self.op("act", lambda e, pair=pair, b=b: e.activation(
                                out=Kst[0:32, pair - 1:pair + 1, :, :].rearrange("p a t i -> p (a t i)"), in_=self.ps[b][0:32, :], func=AF.Copy),
                                reads=[("ps", b)], writes=["Kst"])
                        n += 1
        self.op("pool", lambda e: e.memset(self.Kblk, 0.0), writes=["Kblk"])
        for k in range(3):
            for q in range(8):
                self.dma("sp", self.Kblk[32 * k:32 * k + 32, q, :, 32 * k:32 * k + 32],
                         Kst[0:32, 3 * q + k, :, :], reads=["Kst", "Kblk"], writes=["Kblk"], slow=True)
        dT = ar.alloc([8], F32, parts=96)
        self.dma("sp", dT, I["s5_d"][0].rearrange("(q p) -> p q", p=96), writes=["dT"], slow=True)
        for q in range(8):
            self.op(V, lambda e, q=q: e.scalar_tensor_tensor(out=self.Kblk[:, q, 0, :], in0=self.ident_f[0:96, 0:96], scalar=dT[:, q:q + 1], in1=self.Kblk[:, q, 0, :], op0=ALU.mult, op1=ALU.add),
                    reads=["dT", "Kblk", "c_ident_f"], writes=["Kblk"])
        self.sc.barrier()
        ar.release(m)

    def phase_front0(self):
        ar, I, S, NT = self.ar, self.I, self.S, self.NT
        for f in self.late:
            f()
        self.late = []
        wsb = ar.alloc([8, D], BF16)
        for k in range(8):
            self.load_w(wsb[:, k, :], I["w_in"][0, 128 * k:128 * k + 128, :], ("w", k))
        xt = [ar.alloc([4, D], F32) for _ in range(2)]
        self.alloc_norm_work()
        xnT = ar.alloc([8, 512], BF16)
        ust = [ar.alloc([8, 8, 64], BF16, parts=96) for _ in range(2)]
        qmT = ar.alloc([2, 512], BF16)
        PTs = [ar.alloc([512], BF16) for _ in range(2)]
        self.ma_rec = ar.alloc([512], F32)
        mo = [ar.alloc([4, 512], BF16, parts=64) for _ in range(2)]
        Ns = self.Ns
        gEc = ar.alloc([3, Ns], F32)
        gEs = ar.alloc([3, Ns], F32)
        gtg = [ar.alloc([3, max(1, Ns // 2)], F32) for _ in range(4)]
        per_tile = 8 // NT if NT <= 8 else 1

        def load(t):
            for i in range(4):
                self.dma("sp", xt[t % 2][:, i, :], I["x"][512 * t + 128 * i:512 * t + 128 * i + 128, :], writes=[("xt", t % 2, i)])

        load(0)
        for t in range(NT):
            if t + 1 < NT:
                load(t + 1)
            s = t % 2
            self.norm_T(xt[s], [("xt", s, i) for i in range(4)], [("gmix0", xnT, "xnT")], "f0")
            for m in range(8):
                b = m % 2
                for k in range(8):
                    self.op("pe", lambda e, m=m, k=k, b=b: e.matmul(self.ps[b][0:96, :], lhsT=wsb[:, k, 96 * m:96 * m + 96], rhs=xnT[:, k, :], start=(k == 0), stop=(k == 7)),
                            reads=[("w", k), ("xnT", k)], writes=[("ps", b)])
                self.op("act", lambda e, m=m, b=b, s=s: e.activation(out=ust[s][:, m, :, :], in_=self.ps[b][0:96, :].rearrange("p (s l) -> p l s", l=8), func=AF.Copy),
                        reads=[("ps", b)], writes=[("ust", s)])
            for m in range(8):
                self.dma("sp", self.uT_d[m, :, :, 64 * t:64 * t + 64], ust[s][:, m, :, :], reads=[("ust", s)], writes=[("dram", "uT")])
            for c in range(2):
                b = c % 2
                for k in range(8):
                    self.op("pe", lambda e, c=c, k=k, b=b: e.matmul(self.ps[b][:, :], lhsT=wsb[:, k, 768 + 128 * c:768 + 128 * c + 128], rhs=xnT[:, k, :], start=(k == 0), stop=(k == 7)),
                            reads=[("w", k), ("xnT", k)], writes=[("ps", b)])
                self.op("act", lambda e, c=c, b=b: e.activation(out=qmT[:, c, :], in_=self.ps[b][:, :], func=AF.Copy, scale=0.125),
                        reads=[("ps", b)], writes=["qmT"])
            self.mem_attention(0, qmT, "qmT", mo[s], ("mo", s), PTs, "m0")
            self.dma("sp", self.memo_d[0][:, :, 512 * t:512 * t + 512].rearrange("h p t -> p h t"), mo[s], reads=[("mo", s)], writes=[("dram", "memo0")])
            for q in range(t * per_tile, min(8, (t + 1) * per_tile)):
                self.gen_tables(q, gEc, gEs, gtg)

    def gen_tables(self, q, ec, es_, tg):
        Ns, EP = self.Ns, self.EP
        kc, ks = "gEc", "gEs"
        P_ = "pool"
        self.op(P_, lambda e: e.memset(ec[:, :, 0:1], 1.0), writes=[kc])
        self.op(P_, lambda e: e.memset(es_[:, :, 0:1], 0.0), writes=[ks])
        for k in range(self.ks_steps):
            n = 1 << k
            if n >= Ns:
                break
            cb = EP[:, 0, k, 3 * q:3 * q + 3].unsqueeze(2).to_broadcast([128, 3, n])
            sb = EP[:, 1, k, 3 * q:3 * q + 3].unsqueeze(2).to_broadcast([128, 3, n])
            lo = slice(0, n)
            hi = slice(n, 2 * n)
            t1, t2, t3, t4 = [t[:, :, 0:n] for t in tg]
            self.op(P_, lambda e, t1=t1, cb=cb, lo=lo: e.tensor_tensor(out=t1, in0=ec[:, :, lo], in1=cb, op=ALU.mult), reads=[kc, "EP"], writes=["tg0"])
            self.op(P_, lambda e, t2=t2, sb=sb, lo=lo: e.tensor_tensor(out=t2, in0=es_[:, :, lo], in1=sb, op=ALU.mult), reads=[ks, "EP"], writes=["tg1"])
            self.op(P_, lambda e, t3=t3, sb=sb, lo=lo: e.tensor_tensor(out=t3, in0=ec[:, :, lo], in1=sb, op=ALU.mult), reads=[kc, "EP"], writes=["tg2"])
            self.op(P_, lambda e, t4=t4, cb=cb, lo=lo: e.tensor_tensor(out=t4, in0=es_[:, :, lo], in1=cb, op=ALU.mult), reads=[ks, "EP"], writes=["tg3"])
            self.op(P_, lambda e, t1=t1, t2=t2, hi=hi: e.tensor_tensor(out=ec[:, :, hi], in0=t1, in1=t2, op=ALU.subtract), reads=["tg0", "tg1", kc], writes=[kc])
            self.op(P_, lambda e, t3=t3, t4=t4, hi=hi: e.tensor_tensor(out=es_[:, :, hi], in0=t3, in1=t4, op=ALU.add), reads=["tg2", "tg3", ks], writes=[ks])
        self.dma("sp", self.E_d[q, 0], ec.rearrange("p a b -> p (a b)"), reads=[kc], writes=[("dram", "E", q)])
        self.dma("sp", self.E_d[q, 1], es_.rearrange("p a b -> p (a b)"), reads=[ks], writes=[("dram", "E", q)])

    def phase_s5(self):
        ar, S, Ns = self.ar, self.S, self.Ns
        nsteps = self.ks_steps
        PAD = 1 << (nsteps - 1)
        ut = [ar.alloc([8, Ns], BF16, parts=96) for _ in range(2)]
        Hsh = ar.alloc([3, 2, Ns], BF16)
        gq = [ar.alloc([S], BF16, parts=96) for _ in range(2)]
        x2 = [ar.alloc([Ns], F32, parts=96) for _ in range(2)]
        sg = [ar.alloc([Ns], F32, parts=96) for _ in range(2)]
        Ec = [ar.alloc([3, Ns], F32) for _ in range(2)]
        Es = [ar.alloc([3, Ns], F32) for _ in range(2)]
        NW = 8
        wk = [[ar.alloc([Ns], F32) for _ in range(NW)] for _ in range(2)]
        self.op("pool", lambda e: e.memset(Hsh[:, :, :, 0:1], 0.0), writes=["Hsh"])

        def gen_tables(q):
            sl = q % 2
            self.dma("sp", Ec[sl].rearrange("p a b -> p (a b)"), self.E_d[q, 0], reads=[("dram", "E", q)], writes=[("Ec", sl)])
            self.dma("sp", Es[sl].rearrange("p a b -> p (a b)"), self.E_d[q, 1], reads=[("dram", "E", q)], writes=[("Es", sl)])

        def load(q):
            self.dma("sp", ut[q % 2], self.uT_d[q], reads=[("dram", "uT")], writes=[("ut", q % 2)])

        load(0)
        gen_tables(0)
        npair = 0
        for q in range(8):
            if q + 1 < 8:
                load(q + 1)
                gen_tables(q + 1)
            u = ut[q % 2]
            ukey = ("ut", q % 2)
            sl = q % 2
            for k in range(3):
                pair = 3 * q + k
                R = slice(32 * k, 32 * k + 32)
                W = wk[npair % 2]
                wkey = lambda i, npair=npair: ("wk", npair % 2, i)
                npair += 1
                for c in range(2):
                    b = c
                    for j in range(8):
                        self.op("pe", lambda e, R=R, q=q, j=j, c=c, b=b, u=u: e.matmul(self.ps[b][:, 0:Ns], lhsT=self.Win[R, q, j, c, :], rhs=u[R, j, :], start=(j == 0), stop=(j == 7)),
                                reads=["Win", ukey], writes=[("ps", b)])
                ec, es_ = Ec[sl][:, k, :], Es[sl][:, k, :]
                kc, ks = ("Ec", sl), ("Es", sl)
                Sre, Sim = self.ps[0][:, 0:Ns], self.ps[1][:, 0:Ns]
                D_ = "dve"
                self.op(D_, lambda e, W=W, ec=ec, Sre=Sre: e.tensor_tensor(out=W[0], in0=Sre, in1=ec, op=ALU.mult), reads=[("ps", 0), kc], writes=[wkey(0)])
                self.op(D_, lambda e, W=W, es_=es_, Sim=Sim: e.tensor_tensor(out=W[1], in0=Sim, in1=es_, op=ALU.mult), reads=[("ps", 1), ks], writes=[wkey(1)])
                self.op(D_, lambda e, W=W, ec=ec, Sim=Sim: e.tensor_tensor(out=W[2], in0=Sim, in1=ec, op=ALU.mult), reads=[("ps", 1), kc], writes=[wkey(2)])
                self.op(D_, lambda e, W=W, es_=es_, Sre=Sre: e.tensor_tensor(out=W[3], in0=Sre, in1=es_, op=ALU.mult), reads=[("ps", 0), ks], writes=[wkey(3)])
                self.op(D_, lambda e, W=W: e.tensor_tensor(out=W[0], in0=W[0], in1=W[1], op=ALU.add), reads=[wkey(0), wkey(1)], writes=[wkey(0)])
                self.op(D_, lambda e, W=W: e.tensor_tensor(out=W[2], in0=W[2], in1=W[3], op=ALU.subtract), reads=[wkey(2), wkey(3)], writes=[wkey(2)])
                r8b = self.r8[:, pair:pair + 1].to_broadcast([128, Ns])
                self.op(D_, lambda e, W=W, r8b=r8b: e.tensor_tensor_scan(out=W[4], data0=r8b, data1=W[0], initial=0.0, op0=ALU.mult, op1=ALU.add), reads=[wkey(0), "r8"], writes=[wkey(4)])
                self.op(D_, lambda e, W=W, r8b=r8b: e.tensor_tensor_scan(out=W[5], data0=r8b, data1=W[2], initial=0.0, op0=ALU.mult, op1=ALU.add), reads=[wkey(2), "r8"], writes=[wkey(5)])
                n1 = Ns - 1
                self.op(D_, lambda e, W=W, ec=ec: e.tensor_tensor(out=W[0], in0=W[4], in1=ec, op=ALU.mult), reads=[wkey(4), kc], writes=[wkey(0)])
                self.op(D_, lambda e, W=W, es_=es_: e.tensor_tensor(out=W[1], in0=W[5], in1=es_, op=ALU.mult), reads=[wkey(5), ks], writes=[wkey(1)])
                self.op(D_, lambda e, W=W, ec=ec: e.tensor_tensor(out=W[2], in0=W[5], in1=ec, op=ALU.mult), reads=[wkey(5), kc], writes=[wkey(2)])
                self.op(D_, lambda e, W=W, es_=es_: e.tensor_tensor(out=W[3], in0=W[4], in1=es_, op=ALU.mult), reads=[wkey(4), ks], writes=[wkey(3)])
                self.op(D_, lambda e, W=W, k=k, n1=n1: e.tensor_tensor(out=Hsh[:, k, 0, 1:Ns], in0=W[0][:, 0:n1], in1=W[1][:, 0:n1], op=ALU.subtract), reads=[wkey(0), wkey(1), "Hsh"], writes=["Hsh"])
                self.op(D_, lambda e, W=W, k=k, n1=n1: e.tensor_tensor(out=Hsh[:, k, 1, 1:Ns], in0=W[2][:, 0:n1], in1=W[3][:, 0:n1], op=ALU.add), reads=[wkey(2), wkey(3), "Hsh"], writes=["Hsh"])
            g = gq[q % 2]
            gkey = ("gq", q % 2)
            for l in range(8):
                b = 2 + l % 2
                first = True
                for tau in range(l + 1):
                    self.op("pe", lambda e, q=q, tau=tau, l=l, b=b, u=u, first=first: e.matmul(self.ps[b][0:96, 0:Ns], lhsT=self.Kblk[:, q, tau, :], rhs=u[:, l - tau, :], start=first, stop=(tau == l)),
                            reads=["Kblk", ukey], writes=[("ps", b)])
                    first = False
                for k in range(3):
                    for c in range(2):
                        last = (c == 1)
                        self.op("pe", lambda e, q=q, k=k, c=c, l=l, b=b, last=last: e.matmul(self.ps[b][32 * k:32 * k + 32, 0:Ns], lhsT=self.Wout[:, 3 * q + k, l, c, :], rhs=Hsh[:, k, c, :], start=False, stop=last, skip_group_check=True),
                                reads=["Wout", "Hsh"], writes=[("ps", b)])
                yp = self.ps[b][0:96, 0:Ns]
                a2, a3 = x2[l % 2], sg[l % 2]
                k2, k3 = ("x2", l % 2), ("sg", l % 2)
                self.op("act", lambda e, yp=yp, a2=a2: e.activation(out=a2, in_=yp, func=AF.Square), reads=[("ps", b)], writes=[k2])
                self.op("dve", lambda e, a2=a2: e.tensor_scalar(out=a2, in0=a2, scalar1=0.044715, scalar2=1.0, op0=ALU.mult, op1=ALU.add), reads=[k2], writes=[k2])
                self.op("dve", lambda e, a2=a2, yp=yp: e.tensor_tensor(out=a2, in0=a2, in1=yp, op=ALU.mult), reads=[k2, ("ps", b)], writes=[k2])
                self.op("act", lambda e, a2=a2, a3=a3: e.activation(out=a3, in_=a2, func=AF.Sigmoid, scale=1.5957691216057308), reads=[k2], writes=[k3])
                self.op("dve", lambda e, a3=a3, yp=yp, g=g, l=l: e.tensor_tensor(out=g.rearrange("p (s l) -> p l s", l=8)[:, l, :], in0=a3, in1=yp, op=ALU.mult),
                        reads=[k3, ("ps", b)], writes=[gkey])
            self.dma("sp", self.gT_d[q], g, reads=[gkey], writes=[("dram", "gT")])

    def load_wup(self, l, wup):
        for k in range(8):
            self.load_w(wup[:, k, :], self.I["w_ffn_up"][l, 128 * k:128 * k + 128, :], ("wup", k))

    def sweep_glu_wout0(self, after_weights=None):
        ar, I, S, NT = self.ar, self.I, self.S, self.NT
        wg = ar.alloc([8, 768], BF16, parts=96)
        woT = ar.alloc([8, D], BF16, parts=96)
        woM = ar.alloc([4, D], BF16, parts=64)
        self.load_w(wg.rearrange("p q n -> p (q n)") if False else wg[:, 0, :], I["w_glu"][0, 0:96, :], ("wg", 0))
        for q in range(1, 8):
            self.load_w(wg[:, q, :], I["w_glu"][0, 96 * q:96 * q + 96, :], ("wg", q))
        for q in range(8):
            self.load_w(woT[:, q, :], I["w_out"][0, 96 * q:96 * q + 96, :], ("woT", q))
        for h in range(4):
            self.load_w(woM[:, h, :], I["w_out"][0, 768 + 64 * h:768 + 64 * h + 64, :], ("woM", h))
        if after_weights is not None:
            after_weights()
        gt = [ar.alloc([8, 512], BF16, parts=96) for _ in range(2)]
        mo = [ar.alloc([4, 512], BF16, parts=64) for _ in range(2)]
        xt = [ar.alloc([4, D], F32) for _ in range(2)]
        tokT = ar.alloc([8, 512], BF16, parts=96)
        sig = [ar.alloc([512], F32, parts=96) for _ in range(2)]

        def load(t):
            s = t % 2
            self.dma("sp", gt[s], self.gT_d[:, :, 512 * t:512 * t + 512].rearrange("q p t -> p q t"), reads=[("dram", "gT")], writes=[("gt", s)])
            self.dma("sp", mo[s], self.memo_d[0][:, :, 512 * t:512 * t + 512].rearrange("h p t -> p h t"), reads=[("dram", "memo0")], writes=[("mo", s)])
            for i in range(4):
                self.dma("sp", xt[s][:, i, :], I["x"][512 * t + 128 * i:512 * t + 128 * i + 128, :], reads=[("dram", "hA", t)], writes=[("xt", s, i)])

        load(0)
        for t in range(NT):
            if t + 1 < NT:
                load(t + 1)
            s = t % 2
            for m in range(8):
                b = m % 2
                for q in range(8):
                    self.op("pe", lambda e, m=m, q=q, b=b, s=s: e.matmul(self.ps[b][0:96, :], lhsT=wg[:, q, 96 * m:96 * m + 96], rhs=gt[s][:, q, :], start=(q == 0), stop=(q == 7)),
                            reads=[("wg", q), ("gt", s)], writes=[("ps", b)])
                sgb = sig[m % 2]
                self.op("act", lambda e, b=b, sgb=sgb: e.activation(out=sgb, in_=self.ps[b][0:96, :], func=AF.Sigmoid), reads=[("ps", b)], writes=[("sig", m % 2)])
                self.op("dve", lambda e, m=m, s=s, sgb=sgb: e.tensor_tensor(out=tokT[:, m, :], in0=gt[s][:, m, :], in1=sgb, op=ALU.mult),
                        reads=[("sig", m % 2), ("gt", s)], writes=[("tokT", m)])
            self.wout_residual(xt[s], [("xt", s, i) for i in range(4)],
                               [(tokT[:, q, :], woT[:, q, :], [("tokT", q), ("woT", q)]) for q in range(8)] +
                               [(mo[s][0:64, h, :], woM[0:64, h, :], [("mo", s), ("woM", h)]) for h in range(4)])
            for i in range(4):
                self.dma("sp", self.hA[512 * t + 128 * i:512 * t + 128 * i + 128, :], xt[s][:, i, :], reads=[("xt", s, i)], writes=[("dram", "hA", t)])

    def wout_residual(self, ht, hkeys, terms):
        n = len(terms)
        for sub in range(4):
            for half in range(2):
                b = 2 + (2 * sub + half) % 2
                for ti, (aT, w, keys) in enumerate(terms):
                    self.op("pe", lambda e, aT=aT, w=w, sub=sub, half=half, b=b, ti=ti: e.matmul(self.ps[b][:, :], lhsT=aT[:, 128 * sub:128 * sub + 128], rhs=w[:, 512 * half:512 * half + 512], start=(ti == 0), stop=(ti == n - 1)),
                            reads=keys, writes=[("ps", b)])
                self.op("dve", lambda e, sub=sub, half=half, b=b: e.tensor_tensor(out=ht[:, sub, 512 * half:512 * half + 512], in0=self.ps[b][:, :], in1=ht[:, sub, 512 * half:512 * half + 512], op=ALU.add),
                        reads=[("ps", b), hkeys[sub]], writes=[hkeys[sub]])

    def sweep_ffn(self, l, hin, hout, final, wup=None):
        ar, I, S, NT = self.ar, self.I, self.S, self.NT
        if wup is None:
            wup = ar.alloc([8, 2 * DFF], BF16)
            self.load_wup(l, wup)
        wdn = ar.alloc([NJ, D], BF16)
        for j in range(NJ):
            self.load_w(wdn[:, j, :], I["w_ffn_down"][l, 128 * j:128 * j + 128, :], ("wdn", j))
        ht = ar.alloc([4, D], F32)
        self.alloc_norm_work()
        xnT = ar.alloc([8, 512], BF16)
        act = ar.alloc([NJ, 512], BF16)
        Gb = [ar.alloc([514], F32) for _ in range(2)]
        T1 = [ar.alloc([512], F32) for _ in range(2)]
        carry = ar.alloc([NJ, 2], F32)
        self.op("pool", lambda e: e.memset(carry, 0.0), writes=["carry"])
        cw, cb = self.cw[l], self.cb[l]
        gname = "gffn%d" % l
        if final:
            yss = ar.alloc([4], F32)
        for t in range(NT):
            for i in range(4):
                self.dma("sp", ht[:, i, :], hin[512 * t + 128 * i:512 * t + 128 * i + 128, :], reads=[("dram", "h_in", l, t)] + ([("dram", "hA", t)] if True else []), writes=[("ht", i)])
            self.norm_T(ht, [("ht", i) for i in range(4)], [(gname, xnT, "xnT")], "ffn")
            def chain_a(j):
                bg = (j % 3) * 2 + 1
                G, Tt = Gb[j % 2], T1[j % 2]
                kG, kT = ("Gb", j % 2), ("T1", j % 2)
                self.op("pool", lambda e: e.tensor_copy(out=G[:, 0:2], in_=carry[:, j, :]), reads=["carry", kG], writes=[kG])
                self.op("act", lambda e: e.activation(out=G[:, 2:514], in_=self.ps[bg][:, :], func=AF.Copy), reads=[("ps", bg), kG], writes=[kG])
                self.op("pool", lambda e: e.tensor_copy(out=carry[:, j, :], in_=G[:, 512:514]), reads=[kG, "carry"], writes=["carry"])
                self.op("act", lambda e: e.activation(out=Tt, in_=G[:, 2:514], func=AF.Identity, scale=cw[:, j, 2:3], bias=cb[:, j:j + 1]),
                        reads=[kG, ("cw", l), ("cb", l)], writes=[kT])
                self.op("dve", lambda e: e.scalar_tensor_tensor(out=Tt, in0=G[:, 1:513], scalar=cw[:, j, 1:2], in1=Tt, op0=ALU.mult, op1=ALU.add),
                        reads=[kG, kT, ("cw", l)], writes=[kT])
                self.op("dve", lambda e: e.scalar_tensor_tensor(out=Tt, in0=G[:, 0:512], scalar=cw[:, j, 0:1], in1=Tt, op0=ALU.mult, op1=ALU.add),
                        reads=[kG, kT, ("cw", l)], writes=[kT])

            def chain_b(j):
                ba = (j % 3) * 2
                Tt = T1[j % 2]
                kT = ("T1", j % 2)
                self.op("act", lambda e: e.activation(out=Tt, in_=Tt, func=AF.Silu), reads=[kT], writes=[kT])
                self.op("dve", lambda e: e.tensor_tensor(out=act[:, j, :], in0=Tt, in1=self.ps[ba][:, :], op=ALU.mult),
                        reads=[kT, ("ps", ba)], writes=[("act", j)])

            for j in range(NJ):
                ba, bg = (j % 3) * 2, (j % 3) * 2 + 1
                for (bb, off) in ((ba, 0), (bg, DFF)):
                    for k in range(8):
                        self.op("pe", lambda e, j=j, k=k, bb=bb, off=off: e.matmul(self.ps[bb][:, :], lhsT=wup[:, k, off + 128 * j:off + 128 * j + 128], rhs=xnT[:, k, :], start=(k == 0), stop=(k == 7)),
                                reads=[("wup", k), ("xnT", k)], writes=[("ps", bb)])
                chain_a(j)
                if not FFN_PIPE:
                    chain_b(j)
                elif j >= 1:
                    chain_b(j - 1)
            if FFN_PIPE:
                chain_b(NJ - 1)
            for sub in range(4):
                for half in range(2):
                    b = 6 + (2 * sub + half) % 2
                    for j in range(NJ):
                        self.op("pe", lambda e, j=j, sub=sub, half=half, b=b: e.matmul(self.ps[b][:, :], lhsT=act[:, j, 128 * sub:128 * sub + 128], rhs=wdn[:, j, 512 * half:512 * half + 512], start=(j == 0), stop=(j == NJ - 1)),
                                reads=[("act", j), ("wdn", j)], writes=[("ps", b)])
                    self.op("dve", lambda e, sub=sub, half=half, b=b: e.tensor_tensor(out=ht[:, sub, 512 * half:512 * half + 512], in0=self.ps[b][:, :], in1=ht[:, sub, 512 * half:512 * half + 512], op=ALU.add),
                            reads=[("ps", b), ("ht", sub)], writes=[("ht", sub)])
            if not final:
                for i in range(4):
                    self.dma("sp", hout[512 * t + 128 * i:512 * t + 128 * i + 128, :], ht[:, i, :], reads=[("ht", i)], writes=[("dram", "hB", t)])
            else:
                w = self.nt_work
                for i in range(4):
                    ss = yss[:, i:i + 1]
                    self.op("act", lambda e, i=i, ss=ss: e.activation(out=w["junk"], in_=ht[:, i, :], func=AF.Square, accum_out=ss), reads=[("ht", i)], writes=["njunk", ("yss", i)])
                    self.op("act", lambda e, ss=ss: e.activation(out=ss, in_=ss, func=AF.Sqrt, scale=1.0 / D, bias=EPS), reads=[("yss", i)], writes=[("yss", i)])
                    self.op("dve", lambda e, ss=ss: e.reciprocal(out=ss, in_=ss), reads=[("yss", i)], writes=[("yss", i)])
                    self.op("act", lambda e, i=i, ss=ss: e.activation(out=ht[:, i, :], in_=ht[:, i, :], func=AF.Copy, scale=ss), reads=[("ht", i), ("yss", i)], writes=[("ht", i)])
                    self.op("dve", lambda e, i=i: e.tensor_tensor(out=ht[:, i, :], in0=ht[:, i, :], in1=self.gF, op=ALU.mult), reads=[("ht", i), "gF"], writes=[("ht", i)])
                    self.dma("sp", self.y[512 * t + 128 * i:512 * t + 128 * i + 128, :], ht[:, i, :], reads=[("ht", i)], writes=[("dram", "y", t)])

    def sweep_kv_front1(self):
        ar, I, S, NT = self.ar, self.I, self.S, self.NT
        wkv = ar.alloc([8, 1536], BF16)
        wfg = ar.alloc([8, 12], BF16)
        win = ar.alloc([8, D], BF16)
        for k in range(8):
            self.load_w(wkv[:, k, :], I["w_kv"][128 * k:128 * k + 128, :], ("wkv", k))
            self.load_w(win[:, k, :], I["w_in"][1, 128 * k:128 * k + 128, :], ("win", k))
        self.dma("pool", wfg, I["w_fgate"].rearrange("(k p) h -> p k h", p=128), writes=["wfg"])
        ht = [ar.alloc([4, D], F32) for _ in range(2)]
        self.alloc_norm_work()
        hsT = ar.alloc([8, 512], BF16)
        hnT = ar.alloc([8, 512], BF16)
        kst = [ar.alloc([6, 512], BF16) for _ in range(2)]
        qst = [ar.alloc([6, 512], BF16) for _ in range(2)]
        vst = [ar.alloc([4, 768], BF16) for _ in range(2)]
        qmT = ar.alloc([2, 512], BF16)
        PTs = [ar.alloc([512], BF16) for _ in range(2)]
        self.ma_rec = ar.alloc([512], F32)
        mo = [ar.alloc([4, 512], BF16, parts=64) for _ in range(2)]
        ex = ar.alloc([512], F32)

        def load(t):
            for i in range(4):
                self.dma("sp", ht[t % 2][:, i, :], self.hB[512 * t + 128 * i:512 * t + 128 * i + 128, :], reads=[("dram", "hB", t)], writes=[("ht", t % 2, i)])

        load(0)
        for t in range(NT):
            if t + 1 < NT:
                load(t + 1)
            s = t % 2
            self.norm_T(ht[s], [("ht", s, i) for i in range(4)], [("gkv", hsT, "hsT"), ("gmix1", hnT, "hnT")], "kv")
            for m in range(6):
                b = m % 2
                for k in range(8):
                    self.op("pe", lambda e, m=m, k=k, b=b: e.matmul(self.ps[b][:, :], lhsT=wkv[:, k, 128 * m:128 * m + 128], rhs=hsT[:, k, :], start=(k == 0), stop=(k == 7)),
                            reads=[("wkv", k), ("hsT", k)], writes=[("ps", b)])
                self.op("act", lambda e, m=m, b=b, s=s: e.activation(out=kst[s][:, m, :], in_=self.ps[b][:, :], func=AF.Copy), reads=[("ps", b)], writes=[("kst", s)])
            self.dma("sp", self.kT_d[:, 512 * t:512 * t + 512].rearrange("(m p) t -> p m t", p=128), kst[s], reads=[("kst", s)], writes=[("dram", "kT")])
            for sub in range(4):
                for half in range(2):
                    b = 2 + (2 * sub + half) % 2
                    for k in range(8):
                        self.op("pe", lambda e, sub=sub, half=half, k=k, b=b: e.matmul(self.ps[b][:, 0:384], lhsT=hsT[:, k, 128 * sub:128 * sub + 128], rhs=wkv[:, k, 768 + 384 * half:768 + 384 * half + 384], start=(k == 0), stop=(k == 7)),
                                reads=[("wkv", k), ("hsT", k)], writes=[("ps", b)])
                    self.op("dve", lambda e, sub=sub, half=half, b=b, s=s: e.tensor_copy(out=vst[s][:, sub, 384 * half:384 * half + 384], in_=self.ps[b][:, 0:384]), reads=[("ps", b)], writes=[("vst", s)])
            self.dma("sp", self.v_d[512 * t:512 * t + 512, :].rearrange("(i p) d -> p i d", p=128), vst[s], reads=[("vst", s)], writes=[("dram", "v")])
            b = 4
            for k in range(8):
                self.op("pe", lambda e, k=k, b=b: e.matmul(self.ps[b][0:12, :], lhsT=wfg[:, k, :], rhs=hsT[:, k, :], start=(k == 0), stop=(k == 7)),
                        reads=["wfg", ("hsT", k)], writes=[("ps", b)])
            self.op("act", lambda e, b=b: e.activation(out=ex[0:12, :], in_=self.ps[b][0:12, :], func=AF.Exp, scale=-1.0, bias=self.nbf[0:12, :]), reads=[("ps", b), "nbf"], writes=["ex"])
            self.op("act", lambda e, t=t: e.activation(out=self.cs[0:12, 512 * t:512 * t + 512], in_=ex[0:12, :], func=AF.Ln, bias=1.0), reads=["ex"], writes=["cs"])
            for m in range(6):
                b = m % 2
                for k in range(8):
                    self.op("pe", lambda e, m=m, k=k, b=b: e.matmul(self.ps[b][:, :], lhsT=win[:, k, 128 * m:128 * m + 128], rhs=hnT[:, k, :], start=(k == 0), stop=(k == 7)),
                            reads=[("win", k), ("hnT", k)], writes=[("ps", b)])
                self.op("act", lambda e, m=m, b=b, s=s: e.activation(out=qst[s][:, m, :], in_=self.ps[b][:, :], func=AF.Copy, scale=0.125), reads=[("ps", b)], writes=[("qst", s)])
            self.dma("sp", self.qT_d[:, 512 * t:512 * t + 512].rearrange("(m p) t -> p m t", p=128), qst[s], reads=[("qst", s)], writes=[("dram", "qT")])
            for c in range(2):
                b = c % 2
                for k in range(8):
                    self.op("pe", lambda e, c=c, k=k, b=b: e.matmul(self.ps[b][:, :], lhsT=win[:, k, 768 + 128 * c:768 + 128 * c + 128], rhs=hnT[:, k, :], start=(k == 0), stop=(k == 7)),
                            reads=[("win", k), ("hnT", k)], writes=[("ps", b)])
                self.op("act", lambda e, c=c, b=b: e.activation(out=qmT[:, c, :], in_=self.ps[b][:, :], func=AF.Copy, scale=0.125), reads=[("ps", b)], writes=["qmT"])
            self.mem_attention(1, qmT, "qmT", mo[s], ("mo", s), PTs, "m1")
            self.dma("sp", self.memo_d[1][:, :, 512 * t:512 * t + 512].rearrange("h p t -> p h t"), mo[s], reads=[("mo", s)], writes=[("dram", "memo1")])

    def phase_fox(self):
        ar, S, NT, NB = self.ar, self.S, self.NT, self.NB
        cs = self.cs
        ones12 = ar.alloc([S], F32)
        self.op("pool", lambda e: e.memset(ones12[0:12, :], 1.0), writes=["ones12"])
        cum = ar.alloc([S], F32)
        self.op("dve", lambda e: e.tensor_tensor_scan(out=cum[0:12, :], data0=ones12[0:12, :], data1=cs[0:12, :], initial=0.0, op0=ALU.mult, op1=ALU.add),
                reads=["cs", "ones12"], writes=["cum"])
        csKP = ar.alloc([NB, 12], F32)
        CL = ar.alloc([NB, 12], F32)
        CM = ar.alloc([NB, 12], F32)
        DL = ar.alloc([NB, 12], F32)
        for j in range(NB):
            self.op("pe", lambda e, j=j: e.transpose(out=self.ps[0][:, 12 * j:12 * j + 12], in_=cum[0:12, 128 * j:128 * j + 128], identity=self.ident_f[0:12, 0:12]),
                    reads=["cum", "c_ident_f"], writes=[("ps", 0)])
        self.op("act", lambda e: e.activation(out=csKP.rearrange("p j h -> p (j h)"), in_=self.ps[0][:, 0:12 * NB], func=AF.Copy), reads=[("ps", 0)], writes=["csKP"])
        for (sel, dst, kd, b) in ((self.sel127, CL, "CL", 1), (self.sel63, CM, "CM", 2)):
            self.op("pe", lambda e, sel=sel, b=b: e.matmul(self.ps[b][:, 0:12 * NB], lhsT=sel, rhs=csKP.rearrange("p j h -> p (j h)"), start=True, stop=True),
                    reads=["csKP", "c_sel127", "c_sel63"], writes=[("ps", b)])
            self.op("act", lambda e, dst=dst, b=b: e.activation(out=dst.rearrange("p j h -> p (j h)"), in_=self.ps[b][:, 0:12 * NB], func=AF.Copy), reads=[("ps", b)], writes=[kd])
        for Ig in range(NT):
            self.op("dve", lambda e, Ig=Ig: e.tensor_tensor(out=DL[:, 4 * Ig:4 * Ig + 4, :], in0=CL[:, 4 * Ig + 1:4 * Ig + 2, :].to_broadcast([128, 4, 12]), in1=CM[:, 4 * Ig:4 * Ig + 4, :], op=ALU.subtract),
                    reads=["CL", "CM"], writes=["DL"])
        qa = [ar.alloc([S], BF16) for _ in range(2)]
        ka = [ar.alloc([S], BF16) for _ in range(2)]
        Vh = [ar.alloc([NB, 128], BF16) for _ in range(2)]
        oh = [ar.alloc([S], BF16) for _ in range(2)]
        NPT = 4
        PT = [ar.alloc([512], BF16) for _ in range(NPT)]
        BIAS = [ar.alloc([NB], F32) for _ in range(2)]
        rcs = [ar.alloc([512], F32, parts=1) for _ in range(2)]
        bcs = [ar.alloc([512], F32) for _ in range(2)]
        rch = [ar.alloc([512], BF16, parts=1) for _ in range(2)]
        rcl = [ar.alloc([512], BF16, parts=1) for _ in range(2)]
        onesb = ar.alloc([128], BF16, parts=1)
        self.op("pool", lambda e: e.memset(onesb, 1.0), writes=["onesb"])
        for s in range(2):
            self.op("pool", lambda e, s=s: e.memset(ka[s][64:65, :], 1.0), writes=[("ka1", s)])
            self.op("pool", lambda e, s=s: e.memset(Vh[s][:, :, 0:64], 0.0), writes=[("Vh1", s)])
            self.op("pool", lambda e, s=s: e.memset(Vh[s][:, :, 0:1], 1.0), reads=[("Vh1", s)], writes=[("Vh1", s)])

        def load(h):
            s = h % 2
            self.dma("sp", qa[s][0:64, :], self.qT_d[64 * h:64 * h + 64, :], reads=[("dram", "qT")], writes=[("qa", s)])
            self.dma("sp", ka[s][0:64, :], self.kT_d[64 * h:64 * h + 64, :], reads=[("dram", "kT")], writes=[("ka", s)])
            self.dma("sp", Vh[s][:, :, 64:128], self.v_d[:, 64 * h:64 * h + 64].rearrange("(j p) d -> p j d", p=128), reads=[("dram", "v")], writes=[("Vh", s)])
            self.op("dve", lambda e, s=s, h=h: e.tensor_copy(out=qa[s][64:65, :].rearrange("p (j t) -> p j t", t=128), in_=DL[64:65, :, h:h + 1].to_broadcast([1, NB, 128])),
                    reads=["DL", ("qa1", s)], writes=[("qa1", s)])

        its = []
        for h in range(12):
            for Ig in range(NT):
                for j in range(4 * Ig + 4):
                    its.append((h, Ig, j, 4 * Ig + 4))
        NSB = 3

        def group_pre(h, Ig):
            s = h % 2
            g = h * NT + Ig
            bi = BIAS[g % 2]
            nj = 4 * Ig + 4
            self.op("dve", lambda e, bi=bi, nj=nj, Ig=Ig, h=h: e.tensor_scalar(out=bi[:, 0:nj], in0=csKP[:, 0:nj, h], scalar1=CL[:, 4 * Ig + 1, h:h + 1], scalar2=None, op0=ALU.subtract),
                    reads=["csKP", "CL"], writes=[("BIAS", g % 2)])

        def emit_st(idx):
            h, Ig, j, nj = its[idx]
            s = h % 2
            if j == 0:
                group_pre(h, Ig)
            c0 = max(0, 128 * (j - 4 * Ig))
            bs = idx % NSB
            self.op("pe", lambda e, s=s, j=j, Ig=Ig, c0=c0, bs=bs: e.matmul(self.ps[bs][:, c0:512], lhsT=ka[s][0:65, 128 * j:128 * j + 128], rhs=qa[s][0:65, 512 * Ig + c0:512 * Ig + 512], start=True, stop=True),
                    reads=[("ka", s), ("ka1", s), ("qa", s), ("qa1", s)], writes=[("ps", bs)])

        def emit_exp_pv(idx):
            h, Ig, j, nj = its[idx]
            s = h % 2
            g = h * NT + Ig
            a = j - 4 * Ig
            c0 = max(0, 128 * a)
            bs = idx % NSB
            pt = PT[idx % NPT]
            kpt = ("PTf", idx % NPT)
            bi = BIAS[g % 2]
            po = self.ps[4 + g % 2]
            ko = ("ps", 4 + g % 2)
            self.op("act", lambda e, pt=pt, bs=bs, c0=c0, bi=bi, j=j: e.activation(out=pt[:, c0:512], in_=self.ps[bs][:, c0:512], func=AF.Exp, bias=bi[:, j:j + 1]),
                    reads=[("ps", bs), ("BIAS", g % 2)], writes=[kpt])
            if a >= 0:
                self.op("pool", lambda e, pt=pt, c0=c0: e.tensor_tensor(out=pt[:, c0:c0 + 128], in0=pt[:, c0:c0 + 128], in1=self.mask, op=ALU.mult),
                        reads=[kpt, "c_mask"], writes=[kpt])
            self.op("pe", lambda e, s=s, j=j, pt=pt, c0=c0, po=po, nj=nj: e.matmul(po[:, c0:512], lhsT=Vh[s][:, j, :], rhs=pt[:, c0:512], start=(j == 0), stop=(j == nj - 1), skip_group_check=True),
                    reads=[("Vh", s), ("Vh1", s), kpt], writes=[ko])

        def tail_a(g):
            po = self.ps[4 + g % 2]
            ko = ("ps", 4 + g % 2)
            rc, rh, rl = rcs[g % 2], rch[g % 2], rcl[g % 2]
            self.op("dve", lambda e: e.reciprocal(out=rc[0:1, :], in_=po[0:1, :]), reads=[ko], writes=[("rcs", g % 2)])
            self.op("dve", lambda e: e.tensor_copy(out=rh[0:1, :], in_=rc[0:1, :]), reads=[("rcs", g % 2)], writes=[("rch", g % 2)])
            self.op("dve", lambda e: e.tensor_tensor(out=rl[0:1, :], in0=rc[0:1, :], in1=rh[0:1, :], op=ALU.subtract), reads=[("rcs", g % 2), ("rch", g % 2)], writes=[("rcl", g % 2)])

        def tail(g):
            h, Ig = g // NT, g % NT
            s = h % 2
            po = self.ps[4 + g % 2]
            ko = ("ps", 4 + g % 2)
            rc, bc = rcs[g % 2], bcs[g % 2]
            rh, rl = rch[g % 2], rcl[g % 2]
            self.op("pe", lambda e: e.matmul(self.ps[6][:, :], lhsT=onesb[0:1, 0:128], rhs=rh[0:1, :], start=True, stop=False),
                    reads=["onesb", ("rch", g % 2)], writes=[("ps", 6)])
            self.op("pe", lambda e: e.matmul(self.ps[6][:, :], lhsT=onesb[0:1, 0:128], rhs=rl[0:1, :], start=False, stop=True),
                    reads=["onesb", ("rcl", g % 2)], writes=[("ps", 6)])
            self.op("act", lambda e, bc=bc: e.activation(out=bc[64:128, :], in_=self.ps[6][64:128, :], func=AF.Copy), reads=[("ps", 6)], writes=[("bcs", g % 2)])
            self.op("dve", lambda e, bc=bc, po=po, s=s, Ig=Ig: e.tensor_tensor(out=oh[s][64:128, 512 * Ig:512 * Ig + 512], in0=po[64:128, :], in1=bc[64:128, :], op=ALU.mult),
                    reads=[ko, ("bcs", g % 2)], writes=[("oh", s)])
            if Ig == NT - 1:
                self.dma("sp", self.oT_d[64 * h:64 * h + 64, :], oh[s][64:128, :], reads=[("oh", s)], writes=[("dram", "oT")])
            if self.debug and h == 2:
                self.dma("sp", self.dbg_rc[Ig:Ig + 1, :], rc[0:1, :], reads=[("rcs", g % 2)], writes=[("dram", "dbg")])
                if Ig == NT - 1:
                    self.dma("sp", self.dbg_qa1, qa[s][64:65, :], reads=[("qa1", s)], writes=[("dram", "dbg3")])
                    self.dma("sp", self.dbg_ka1, ka[s][64:65, :], reads=[("ka1", s)], writes=[("dram", "dbg4")])
                    self.dma("sp", self.dbg_v1, Vh[s][:, :, 0:2], reads=[("Vh1", s), ("Vh", s)], writes=[("dram", "dbg5")])

        load(0)
        LOOK = 2
        p_st = 0
        n_it = len(its)
        pending_tail = None
        for idx in range(n_it):
            while p_st < n_it and p_st <= idx + LOOK:
                emit_st(p_st)
                p_st += 1
            h, Ig, j, nj = its[idx]
            g = h * NT + Ig
            if j == 0 and Ig == 0 and h + 1 < 12:
                load(h + 1)
            emit_exp_pv(idx)
            if pending_tail is not None and j == min(6, nj - 1):
                tail(pending_tail)
                pending_tail = None
            if j == nj - 1:
                if pending_tail is not None:
                    tail(pending_tail)
                tail_a(g)
                pending_tail = g
        if pending_tail is not None:
            tail(pending_tail)

    def sweep_wout1(self, after_weights=None):
        ar, I, S, NT = self.ar, self.I, self.S, self.NT
        woT = ar.alloc([6, D], BF16)
        woM = ar.alloc([4, D], BF16, parts=64)
        for q in range(6):
            self.load_w(woT[:, q, :], I["w_out"][1, 128 * q:128 * q + 128, :], ("woT", q))
        for h in range(4):
            self.load_w(woM[:, h, :], I["w_out"][1, 768 + 64 * h:768 + 64 * h + 64, :], ("woM", h))
        if after_weights is not None:
            after_weights()
        ot = [ar.alloc([6, 512], BF16) for _ in range(2)]
        mo = [ar.alloc([4, 512], BF16, parts=64) for _ in range(2)]
        ht = [ar.alloc([4, D], F32) for _ in range(2)]

        def load(t):
            s = t % 2
            self.dma("sp", ot[s], self.oT_d[:, 512 * t:512 * t + 512].rearrange("(q p) t -> p q t", p=128), reads=[("dram", "oT")], writes=[("ot", s)])
            self.dma("sp", mo[s], self.memo_d[1][:, :, 512 * t:512 * t + 512].rearrange("h p t -> p h t"), reads=[("dram", "memo1")], writes=[("mo", s)])
            for i in range(4):
                self.dma("sp", ht[s][:, i, :], self.hB[512 * t + 128 * i:512 * t + 128 * i + 128, :], reads=[("dram", "hB", t), ("dram", "hA", t)], writes=[("ht", s, i)])

        load(0)
        for t in range(NT):
            if t + 1 < NT:
                load(t + 1)
            s = t % 2
            self.wout_residual(ht[s], [("ht", s, i) for i in range(4)],
                               [(ot[s][:, q, :], woT[:, q, :], [("ot", s), ("woT", q)]) for q in range(6)] +
                               [(mo[s][0:64, h, :], woM[0:64, h, :], [("mo", s), ("woM", h)]) for h in range(4)])
            for i in range(4):
                self.dma("sp", self.hA[512 * t + 128 * i:512 * t + 128 * i + 128, :], ht[s][:, i, :], reads=[("ht", s, i)], writes=[("dram", "hA", t)])


_CACHE = {}


def _get_nc(S, debug=False):
    key = (S, debug)
    if key not in _CACHE:
        _CACHE[key] = Builder(S, debug).build()
    return _CACHE[key]


def kernel(**inputs):
    x = np.asarray(inputs["x"], np.float32)
    mem = np.asarray(inputs["mem"], np.float32)
    B, S, _ = x.shape
    nc = _get_nc(S)
    consts = host_consts()
    shared = {k: np.ascontiguousarray(np.asarray(v, np.float32)) for k, v in inputs.items() if k not in ("x", "mem")}
    in_maps = []
    for b in range(B):
        m = dict(shared)
        m.update(consts)
        m["x"] = np.ascontiguousarray(x[b])
        m["mem"] = np.ascontiguousarray(mem[b])
        in_maps.append(m)
    res = run_bass_kernel_spmd(nc, in_maps, core_ids=list(range(B)))
    return np.stack([np.asarray(r["y"], np.float32) for r in res.results], axis=0)
```

```python
import numpy as np
import ml_dtypes
from contextlib import ExitStack
import concourse.bass as bass
import concourse.mybir as mybir
from concourse.bass_utils import run_bass_kernel_spmd

F32 = mybir.dt.float32
BF16 = mybir.dt.bfloat16
AF = mybir.ActivationFunctionType
ALU = mybir.AluOpType

D = 1024
DFF = 2816
NJ = 22
MEM = 256
EPS = 1e-6
FFN_PIPE = True


class _Op:
    __slots__ = ("stream", "fn", "is_dma", "idx", "slot", "cnt", "waits", "clock")


class Sched:
    STREAMS = ("pe", "act", "dve", "pool", "sp")

    def __init__(self, nc, n_dma_sems=40, n_sw=12):
        self.nc = nc
        self.ops = []
        self.n_dma = n_dma_sems
        self.n_sw = n_sw
        self.rr_sw = 0
        self.rr_hw = 0
        self.dma_cnt = [0] * n_dma_sems
        self.last_writer = {}
        self.readers = {}
        self.know = {s: {} for s in self.STREAMS}
        self.count = {s: 0 for s in self.STREAMS}
        self.pending = {s: [] for s in self.STREAMS}

    def barrier(self):
        tgt = {}
        for s in ("pe", "act", "dve", "pool"):
            if self.count[s] > 0:
                tgt[s] = self.count[s]
        for i in range(self.n_dma):
            if self.dma_cnt[i] > 0:
                tgt[("d", i)] = self.dma_cnt[i]
        for s in self.STREAMS:
            know = self.know[s]
            for k, v in tgt.items():
                if k == s == "pe":
                    continue
                if know.get(k, 0) < v:
                    know[k] = v
                    self.pending[s].append((k, v))
        self.last_writer = {k: w for k, w in self.last_writer.items() if isinstance(k, tuple) and k and k[0] == "dram"}
        self.readers = {k: r for k, r in self.readers.items() if k in self.last_writer}

    def add(self, stream, fn, reads=(), writes=(), dma=False):
        op = _Op()
        op.stream = stream
        op.fn = fn
        op.is_dma = dma
        reads = list(reads)
        writes = list(writes)
        deps = []
        for k in reads + writes:
            w = self.last_writer.get(k)
            if w is not None:
                deps.append(w)
        for k in writes:
            deps.extend(self.readers.get(k, ()))
        know = self.know[stream]
        waits = self.pending[stream]
        self.pending[stream] = []

        def need(key, val, clock):
            if know.get(key, 0) >= val:
                return
            waits.append((key, val))
            if clock is not None:
                for kk, vv in clock.items():
                    if know.get(kk, 0) < vv:
                        know[kk] = vv
            know[key] = max(know.get(key, 0), val)

        if dma:
            if stream == "pool":
                slot = self.rr_sw
                self.rr_sw = (self.rr_sw + 1) % self.n_sw
            else:
                slot = self.n_sw + self.rr_hw
                self.rr_hw = (self.rr_hw + 1) % (self.n_dma - self.n_sw)
            prev = self.dma_cnt[slot]
            if prev > 0:
                need(("d", slot), prev, None)
            self.dma_cnt[slot] = prev + 1
            op.slot = slot
            op.cnt = prev + 1
        for p in deps:
            if p.is_dma:
                need(("d", p.slot), p.cnt, p.clock)
            else:
                if p.stream == "pe" and stream == "pe" and not dma:
                    continue
                need(p.stream, p.idx, p.clock)
        op.waits = waits
        op.clock = dict(know)
        if dma:
            op.idx = None
            op.clock[("d", op.slot)] = op.cnt
        else:
            self.count[stream] += 1
            op.idx = self.count[stream]
            op.clock[stream] = op.idx
        for k in reads:
            self.readers.setdefault(k, []).append(op)
        for k in writes:
            self.last_writer[k] = op
            self.readers[k] = []
        self.ops.append(op)
        return op

    def emit(self, es, final_wait_keys=()):
        nc = self.nc
        sems = {}
        for s in ("pe", "act", "dve", "pool"):
            sems[s] = es.enter_context(nc.semaphore("c_" + s))
        for i in range(self.n_dma):
            sems[("d", i)] = es.enter_context(nc.semaphore("d_%d" % i))
        fin = list(self.pending["sp"])
        know = self.know["sp"]
        for k in final_wait_keys:
            w = self.last_writer.get(k)
            if w is None:
                continue
            key, val = (("d", w.slot), w.cnt) if w.is_dma else (w.stream, w.idx)
            if know.get(key, 0) < val:
                know[key] = val
                fin.append((key, val))
        block = es.enter_context(nc.Block())
        by = {s: [o for o in self.ops if o.stream == s] for s in self.STREAMS}

        def run(eng, stream, extra=()):
            for o in by[stream]:
                for key, val in o.waits:
                    eng.wait_ge(sems[key], val * (16 if isinstance(key, tuple) else 1))
                ins = o.fn(eng)
                if o.is_dma:
                    ins.then_inc(sems[("d", o.slot)], 16)
                else:
                    ins.then_inc(sems[stream], 1)
            for key, val in extra:
                eng.wait_ge(sems[key], val * (16 if isinstance(key, tuple) else 1))

        @block.tensor
        def _(e):
            run(e, "pe")

        @block.scalar
        def _(e):
            run(e, "act")

        @block.vector
        def _(e):
            run(e, "dve")

        @block.gpsimd
        def _(e):
            run(e, "pool")

        @block.sync
        def _(e):
            run(e, "sp", fin)


class Arena:
    def __init__(self, nc, es, words):
        self.t = es.enter_context(nc.sbuf_tensor("arena", [128, words], F32))
        self.off = 0
        self.words = words

    def mark(self):
        return self.off

    def release(self, m):
        self.off = m

    def alloc(self, shape, dtype=F32, parts=128):
        shape = [int(s) for s in shape]
        n = int(np.prod(shape))
        sz = 4 if dtype == F32 else 2
        words = (n * sz + 3) // 4
        words = (words + 7) // 8 * 8
        assert self.off + words <= self.words, "arena overflow %d + %d > %d" % (self.off, words, self.words)
        ap = self.t[0:parts, self.off:self.off + words]
        self.off += words
        if dtype != F32:
            ap = ap.bitcast(dtype)
        ap = ap[:, 0:n]
        if len(shape) > 1:
            names = " ".join("d%d" % i for i in range(len(shape)))
            kw = {"d%d" % i: s for i, s in enumerate(shape)}
            ap = ap.rearrange("p (%s) -> p %s" % (names, names), **kw)
        return ap


def host_consts():
    c = {}
    c["c_ident_bf"] = np.eye(128).astype(ml_dtypes.bfloat16)
    c["c_ident_f"] = np.eye(128).astype(np.float32)
    m = (np.arange(128)[None, :] >= np.arange(128)[:, None]).astype(np.float32)
    c["c_mask"] = m.astype(ml_dtypes.bfloat16)
    c["c_ones_bf"] = np.ones((128, 128), ml_dtypes.bfloat16)
    s127 = np.zeros((128, 128), np.float32)
    s127[127, :] = 1
    s63 = np.zeros((128, 128), np.float32)
    s63[63, :] = 1
    c["c_sel127"] = s127
    c["c_sel63"] = s63
    return c


class Builder:
    def __init__(self, S=4096, debug=False):
        self.S = S
        self.NT = S // 512
        self.Ns = S // 8
        self.NB = S // 128
        self.debug = debug
        self.nc = bass.Bass("TRN2", target_bir_lowering=False)
        self.es = ExitStack()
        self.uid = 0

    def din(self, name, shape, dt=F32):
        return self.nc.dram_tensor(name, list(shape), dt, kind="ExternalInput").ap()

    def dscr(self, name, shape, dt):
        kind = "ExternalOutput" if self.debug else "Internal"
        return self.nc.dram_tensor(name, list(shape), dt, kind=kind).ap()

    def op(self, stream, fn, reads=(), writes=()):
        return self.sc.add(stream, fn, reads, writes)

    def dma(self, stream, out, in_, reads=(), writes=(), slow=False):
        if slow:
            return self.sc.add(stream, lambda e: e.dma_start(out=out, in_=in_, allow_slow_non_contiguous=True), reads, writes, dma=True)
        return self.sc.add(stream, lambda e: e.dma_start(out=out, in_=in_), reads, writes, dma=True)

    def load_w(self, dst, src, key):
        cols = src.shape[-1]
        step = 2048
        c = 0
        while c < cols:
            n = min(step, cols - c)
            self.dma("pool", dst[:, c:c + n], src[:, c:c + n], writes=[key])
            c += n

    def new_phase(self, mark):
        self.sc.barrier()
        self.ar.release(mark)

    def build(self):
        nc, es = self.nc, self.es
        S, NT, Ns, NB = self.S, self.NT, self.Ns, self.NB
        with es:
            self.sc = Sched(nc)
            self.ar = Arena(nc, es, 53000)
            self.ps = [es.enter_context(nc.psum_tensor("ps%d" % i, [128, 512], F32)) for i in range(8)]
            I = {}
            I["x"] = self.din("x", [S, D])
            I["mem"] = self.din("mem", [MEM, D])
            for nm, shp in [("g_mix", [2, D]), ("w_in", [2, D, D]), ("w_out", [2, D, D]), ("g_mem", [D]),
                            ("w_mem_kv", [2, D, 512]), ("s5_a_re", [1, 48, 64]), ("s5_a_im", [1, 48, 64]),
                            ("s5_log_dt", [1, 48]), ("s5_b_re", [1, 48, 64, 16]), ("s5_b_im", [1, 48, 64, 16]),
                            ("s5_c_re", [1, 48, 16, 64]), ("s5_c_im", [1, 48, 16, 64]), ("s5_d", [1, 768]),
                            ("w_glu", [1, 768, 768]), ("g_kv", [D]), ("w_kv", [D, 1536]), ("w_fgate", [D, 12]),
                            ("b_fgate", [12]), ("g_ffn", [2, D]), ("w_ffn_up", [2, D, 2 * DFF]), ("conv_w", [2, 3, DFF]),
                            ("conv_b", [2, DFF]), ("w_ffn_down", [2, DFF, D]), ("g_final", [D])]:
                I[nm] = self.din(nm, shp)
            I["c_ident_bf"] = self.din("c_ident_bf", [128, 128], BF16)
            I["c_ident_f"] = self.din("c_ident_f", [128, 128], F32)
            I["c_mask"] = self.din("c_mask", [128, 128], BF16)
            I["c_ones_bf"] = self.din("c_ones_bf", [128, 128], BF16)
            I["c_sel127"] = self.din("c_sel127", [128, 128], F32)
            I["c_sel63"] = self.din("c_sel63", [128, 128], F32)
            self.I = I
            self.y = nc.dram_tensor("y", [S, D], F32, kind="ExternalOutput").ap()
            self.uT_d = self.dscr("uT_d", [8, 96, 8, Ns], BF16)
            self.gT_d = self.dscr("gT_d", [8, 96, S], BF16)
            self.memo_d = [self.dscr("memo_d%d" % l, [4, 64, S], BF16) for l in range(2)]
            self.hA = self.dscr("hA", [S, D], F32)
            self.hB = self.dscr("hB", [S, D], F32)
            self.kT_d = self.dscr("kT_d", [768, S], BF16)
            self.qT_d = self.dscr("qT_d", [768, S], BF16)
            self.v_d = self.dscr("v_d", [S, 768], BF16)
            self.oT_d = self.dscr("oT_d", [768, S], BF16)
            self.E_d = self.dscr("E_d", [8, 2, 128, 3 * Ns], F32)

            if self.debug:
                self.dbg_rc = self.dscr("dbg_rc", [NT, 512], F32)
                self.dbg_bc = self.dscr("dbg_bc", [NT, 65, 512], F32)
                self.dbg_qa1 = self.dscr("dbg_qa1", [1, S], BF16)
                self.dbg_ka1 = self.dscr("dbg_ka1", [1, S], BF16)
                self.dbg_v1 = self.dscr("dbg_v1", [128, NB, 2], BF16)
            self.setup_persistent()
            m0 = self.ar.mark()
            self.s5_prep()
            m1 = self.ar.mark()
            self.phase_front0()
            self.new_phase(m1)
            self.phase_s5()
            self.new_phase(m0)
            wup0 = self.ar.alloc([8, 2 * DFF], BF16)
            mA = self.ar.mark()
            self.sweep_glu_wout0(after_weights=lambda: self.load_wup(0, wup0))
            self.new_phase(mA)
            self.sweep_ffn(0, self.hA, self.hB, final=False, wup=wup0)
            self.new_phase(m0)
            self.cs = self.ar.alloc([S], F32)
            m2 = self.ar.mark()
            self.sweep_kv_front1()
            self.new_phase(m2)
            self.phase_fox()
            self.new_phase(m0)
            wup1 = self.ar.alloc([8, 2 * DFF], BF16)
            mB = self.ar.mark()
            self.sweep_wout1(after_weights=lambda: self.load_wup(1, wup1))
            self.new_phase(mB)
            self.sweep_ffn(1, self.hA, None, final=True, wup=wup1)
            self.sc.emit(es, final_wait_keys=[("dram", "y", t) for t in range(NT)])
        return nc

    def setup_persistent(self):
        ar, I = self.ar, self.I
        self.ident_bf = ar.alloc([128], BF16)
        self.ident_f = ar.alloc([128], F32)
        self.mask = ar.alloc([128], BF16)
        self.ones_bf = ar.alloc([128], BF16)
        self.sel127 = ar.alloc([128], F32)
        self.sel63 = ar.alloc([128], F32)
        for nm, dst in [("c_ident_bf", self.ident_bf), ("c_ident_f", self.ident_f), ("c_mask", self.mask),
                        ("c_ones_bf", self.ones_bf), ("c_sel127", self.sel127), ("c_sel63", self.sel63)]:
            self.dma("sp", dst, I[nm], writes=[nm])
        self.gcol = {}
        self.late = []
        for nm, src in [("gmix0", I["g_mix"][0]), ("gmem", I["g_mem"]), ("gmix1", I["g_mix"][1]), ("gffn0", I["g_ffn"][0]),
                        ("gffn1", I["g_ffn"][1]), ("gkv", I["g_kv"])]:
            t = ar.alloc([8], F32)
            f = lambda t=t, src=src, nm=nm: self.dma("sp", t, src.rearrange("(k p) -> p k", p=128), writes=[nm], slow=True)
            if nm in ("gmix0", "gmem"):
                f()
            else:
                self.late.append(f)
            self.gcol[nm] = t
        self.gF = ar.alloc([D], F32)
        self.late.append(lambda: self.dma("sp", self.gF, I["g_final"].partition_broadcast(128), writes=["gF"]))
        self.cw = []
        self.cb = []
        for l in range(2):
            t = ar.alloc([NJ, 3], F32)
            for kk in range(3):
                self.late.append(lambda t=t, l=l, kk=kk: self.dma("sp", t[:, :, kk], I["conv_w"][l, kk].rearrange("(j p) -> p j", p=128), reads=[("cw", l)], writes=[("cw", l)], slow=True))
            self.cw.append(t)
            t2 = ar.alloc([NJ], F32)
            self.late.append(lambda t2=t2, l=l: self.dma("sp", t2, I["conv_b"][l].rearrange("(j p) -> p j", p=128), writes=[("cb", l)], slow=True))
            self.cb.append(t2)
        self.nbf = ar.alloc([1], F32)
        self.late.append(lambda: self.dma("sp", self.nbf[0:12, :], I["b_fgate"].rearrange("(p o) -> p o", o=1), writes=["nbf"], slow=True))
        self.late.append(lambda: self.op("dve", lambda e: e.tensor_scalar(out=self.nbf[0:12, :], in0=self.nbf[0:12, :], scalar1=-1.0, scalar2=None, op0=ALU.mult),
                                         reads=["nbf"], writes=["nbf"]))
        self.zrow = ar.alloc([128], BF16)
        self.op("dve", lambda e: e.memset(self.zrow, 0.0), writes=["zrow"])
        self.KTm = ar.alloc([2, 2, MEM], BF16)
        self.Vm = ar.alloc([2, 2, 256], BF16)
        m = ar.mark()
        mt = ar.alloc([2, D], F32)
        mn = ar.alloc([2, D], BF16)
        mnT = ar.alloc([8, MEM], BF16)
        wmk = ar.alloc([2, 8, 512], BF16)
        ss = ar.alloc([2], F32)
        junk = ar.alloc([D], F32)
        for i in range(2):
            self.dma("sp", mt[:, i, :], I["mem"][128 * i:128 * i + 128, :], writes=[("mt", i)])
        for l in range(2):
            for k in range(8):
                self.load_w(wmk[:, l, k, :], I["w_mem_kv"][l, 128 * k:128 * k + 128, :], ("wmk", l, k))
        for i in range(2):
            self.rms_rows(mt[:, i, :], mn[:, i, :], ss[:, i:i + 1], junk, ("mt", i), ("mn", i), ("mss", i), "mjunk")
        for k in range(8):
            pb = self.ps[k % 2].bitcast(BF16)
            for i in range(2):
                self.op("pe", lambda e, k=k, i=i, pb=pb: e.transpose(out=pb[:, 128 * i:128 * i + 128], in_=mn[:, i, 128 * k:128 * k + 128], identity=self.ident_bf),
                        reads=[("mn", i), "c_ident_bf"], writes=[("ps", k % 2)])
            self.op("dve", lambda e, k=k, pb=pb: e.tensor_scalar(out=mnT[:, k, :], in0=pb[:, 0:256], scalar1=self.gcol["gmem"][:, k:k + 1], scalar2=None, op0=ALU.mult),
                    reads=[("ps", k % 2), "gmem"], writes=[("mnT", k)])
        for l in range(2):
            for c in range(2):
                b = 2 + (2 * l + c) % 2
                for k in range(8):
                    self.op("pe", lambda e, l=l, c=c, k=k, b=b: e.matmul(self.ps[b][:, 0:256], lhsT=wmk[:, l, k, 128 * c:128 * c + 128], rhs=mnT[:, k, :], start=(k == 0), stop=(k == 7)),
                            reads=[("wmk", l, k), ("mnT", k)], writes=[("ps", b)])
                self.op("act", lambda e, l=l, c=c, b=b: e.activation(out=self.KTm[:, l, c, :], in_=self.ps[b][:, 0:256], func=AF.Copy),
                        reads=[("ps", b)], writes=[("KTm", l)])
            for kb in range(2):
                b = 4 + kb
                for k in range(8):
                    self.op("pe", lambda e, l=l, kb=kb, k=k, b=b: e.matmul(self.ps[b][:, 0:256], lhsT=mnT[:, k, 128 * kb:128 * kb + 128], rhs=wmk[:, l, k, 256:512], start=(k == 0), stop=(k == 7)),
                            reads=[("wmk", l, k), ("mnT", k)], writes=[("ps", b)])
                self.op("act", lambda e, l=l, kb=kb, b=b: e.activation(out=self.Vm[:, l, kb, :], in_=self.ps[b][:, 0:256], func=AF.Copy),
                        reads=[("ps", b)], writes=[("Vm", l)])
        self.sc.barrier()
        ar.release(m)

    def rms_rows(self, src, dst_bf, ss, junk, ksrc, kdst, kss, kjunk):
        self.op("act", lambda e: e.activation(out=junk, in_=src, func=AF.Square, accum_out=ss), reads=[ksrc], writes=[kjunk, kss])
        self.op("act", lambda e: e.activation(out=ss, in_=ss, func=AF.Sqrt, scale=1.0 / D, bias=EPS), reads=[kss], writes=[kss])
        self.op("dve", lambda e: e.reciprocal(out=ss, in_=ss), reads=[kss], writes=[kss])
        self.op("act", lambda e: e.activation(out=dst_bf, in_=src, func=AF.Copy, scale=ss), reads=[ksrc, kss], writes=[kdst])

    def norm_T(self, ht, hkeys, outs, tag):
        w = self.nt_work
        for i in range(4):
            sl = i % 2
            self.rms_rows(ht[:, i, :], w["xn"][:, sl, :], w["ss"][:, i:i + 1], w["junk"], hkeys[i], ("xn", sl), ("nss", i), "njunk")
            bi = 6 + i % 2
            pb = self.ps[bi].bitcast(BF16)
            for k in range(8):
                self.op("pe", lambda e, k=k, sl=sl, pb=pb: e.transpose(out=pb[:, 128 * k:128 * k + 128], in_=w["xn"][:, sl, 128 * k:128 * k + 128], identity=self.ident_bf),
                        reads=[("xn", sl), "c_ident_bf"], writes=[("ps", bi)])
            for gi, (gname, dst, dkey) in enumerate(outs):
                self.op("dve", lambda e, i=i, pb=pb, gname=gname, dst=dst: e.tensor_tensor(
                    out=dst[:, :, 128 * i:128 * i + 128], in0=pb.rearrange("p (k t) -> p k t", k=8),
                    in1=self.gcol[gname].unsqueeze(2).to_broadcast([128, 8, 128]), op=ALU.mult),
                    reads=[("ps", bi), gname], writes=[(dkey, k) for k in range(8)])

    def alloc_norm_work(self):
        ar = self.ar
        self.nt_work = {"xn": ar.alloc([2, D], BF16), "ss": ar.alloc([4], F32), "junk": ar.alloc([D], BF16)}

    def mem_attention(self, l, qmT, qkey, mo, mokey, PTs, tag):
        def st(i):
            hh, kb = i // 2, i % 2
            c, base = hh // 2, 64 * (hh % 2)
            bs = 2 + i % 2
            self.op("pe", lambda e: e.matmul(self.ps[bs][:, :], lhsT=self.KTm[base:base + 64, l, c, 128 * kb:128 * kb + 128], rhs=qmT[base:base + 64, c, :], start=True, stop=True),
                    reads=[("KTm", l), qkey], writes=[("ps", bs)])

        def pv(i):
            hh, kb = i // 2, i % 2
            bs = 2 + i % 2
            bo, bd = (4, 5) if hh % 2 == 0 else (0, 1)
            pt = PTs[i % 2]
            self.op("act", lambda e: e.activation(out=pt, in_=self.ps[bs][:, :], func=AF.Exp), reads=[("ps", bs)], writes=[("PT", i % 2)])
            self.op("pe", lambda e: e.matmul(self.ps[bo][0:64, :], lhsT=self.Vm[:, l, kb, 64 * hh:64 * hh + 64], rhs=pt, start=(kb == 0), stop=(kb == 1)),
                    reads=[("Vm", l), ("PT", i % 2)], writes=[("ps", bo)])
            self.op("pe", lambda e: e.matmul(self.ps[bd][0:64, :], lhsT=self.ones_bf[:, 0:64], rhs=pt, start=(kb == 0), stop=(kb == 1)),
                    reads=["c_ones_bf", ("PT", i % 2)], writes=[("ps", bd)])

        def tail(hh):
            bo, bd = (4, 5) if hh % 2 == 0 else (0, 1)
            rec = self.ma_rec
            self.op("dve", lambda e: e.reciprocal(out=rec[0:64, :], in_=self.ps[bd][0:64, :]), reads=[("ps", bd)], writes=["marec"])
            self.op("dve", lambda e: e.tensor_tensor(out=mo[0:64, hh, :], in0=self.ps[bo][0:64, :], in1=rec[0:64, :], op=ALU.mult),
                    reads=[("ps", bo), "marec"], writes=[mokey])

        st(0)
        for i in range(8):
            if i + 1 < 8:
                st(i + 1)
            pv(i)
            if i % 2 == 0 and i >= 2:
                tail(i // 2 - 1)
        tail(3)

    def s5_prep(self):
        ar, I, sc = self.ar, self.I, self.sc
        nsteps = max(1, int(np.ceil(np.log2(self.Ns))))
        self.ks_steps = nsteps
        self.Win = ar.alloc([8, 8, 2, 128], BF16, parts=96)
        self.Kblk = ar.alloc([8, 8, 96], BF16, parts=96)
        self.Wout = ar.alloc([24, 8, 2, 32], BF16)
        self.r8 = ar.alloc([24], F32)
        self.EP = ar.alloc([2, nsteps, 24], F32)
        m = ar.mark()
        T = lambda shape=(24,): ar.alloc(list(shape), F32)
        are, aim, ldt = T(), T(), T()
        Bre, Bim = T((24, 16)), T((24, 16))
        Cre, Cim = T((24, 16)), T((24, 16))
        Yre, Yim = ar.alloc([8, 2, 64], F32, parts=48), ar.alloc([8, 2, 64], F32, parts=48)
        for gl in range(2):
            P = slice(64 * gl, 64 * gl + 64)
            self.dma("sp", are[P, :], I["s5_a_re"][0, gl::2, :].rearrange("g p -> p g"), writes=["are"], slow=True)
            self.dma("sp", aim[P, :], I["s5_a_im"][0, gl::2, :].rearrange("g p -> p g"), writes=["aim"], slow=True)
            self.dma("sp", ldt[P, :], I["s5_log_dt"][0, gl::2].partition_broadcast(64), writes=["ldt"], slow=True)
            self.dma("sp", Bre[P, :, :], I["s5_b_re"][0, gl::2].rearrange("g p i -> p g i"), writes=["Bre"], slow=True)
            self.dma("sp", Bim[P, :, :], I["s5_b_im"][0, gl::2].rearrange("g p i -> p g i"), writes=["Bim"], slow=True)
        for k in range(3):
            for (Y, nm, ky) in ((Yre, "s5_c_re", "Yre"), (Yim, "s5_c_im", "Yim")):
                src = I[nm][0].rearrange("(q k gl) i p -> k q i gl p", k=3, gl=2)[k]
                for q in range(8):
                    self.dma("sp", Y[16 * k:16 * k + 16, q, :, :], src[q], writes=[ky], slow=True)
        for (Y, Cx, ky, kc) in ((Yre, Cre, "Yre", "Cre"), (Yim, Cim, "Yim", "Cim")):
            for q in range(8):
                b = q % 2
                self.op("pe", lambda e, Y=Y, q=q, b=b: e.transpose(out=self.ps[b][:, 0:48], in_=Y[0:48, q, :, :].rearrange("p a b -> p (a b)"), identity=self.ident_f[0:48, 0:48]),
                        reads=[ky, "c_ident_f"], writes=[("ps", b)])
                self.op("act", lambda e, Cx=Cx, q=q, b=b: e.activation(out=Cx[:, 3 * q:3 * q + 3, :], in_=self.ps[b][:, 0:48].rearrange("p (k i) -> p k i", k=3), func=AF.Copy),
                        reads=[("ps", b)], writes=[kc])
        V = "dve"

        def tt(out, a, b, op, r, w):
            self.op(V, lambda e: e.tensor_tensor(out=out, in0=a, in1=b, op=op), reads=r, writes=w)

        def ts(out, a, s1, s2, op0, op1, r, w):
            if s2 is None:
                self.op(V, lambda e: e.tensor_scalar(out=out, in0=a, scalar1=s1, scalar2=None, op0=op0), reads=r, writes=w)
            else:
                self.op(V, lambda e: e.tensor_scalar(out=out, in0=a, scalar1=s1, scalar2=s2, op0=op0, op1=op1), reads=r, writes=w)

        dt, lre, mag, ph, sn, cs = T(), T(), T(), T(), T(), T()
        t1, t2, t3 = T(), T(), T()
        self.op("act", lambda e: e.activation(out=dt, in_=ldt, func=AF.Exp), reads=["ldt"], writes=["dt"])
        ts(lre, are, -1e-4, None, ALU.min, None, ["are"], ["lre"])
        tt(t1, lre, dt, ALU.mult, ["lre", "dt"], ["t1"])
        self.op("act", lambda e: e.activation(out=mag, in_=t1, func=AF.Exp), reads=["t1"], writes=["mag"])
        self.op("act", lambda e: e.activation(out=self.r8, in_=t1, func=AF.Exp, scale=8.0), reads=["t1"], writes=["r8"])
        tt(ph, aim, dt, ALU.mult, ["aim", "dt"], ["ph"])
        hpi = ar.alloc([1], F32)
        self.op(V, lambda e: e.memset(hpi, float(np.pi / 2)), writes=["hpi"])
        self.op("act", lambda e: e.activation(out=sn, in_=ph, func=AF.Sin, scale=1.0 / 16), reads=["ph"], writes=["sn"])
        self.op("act", lambda e: e.activation(out=cs, in_=ph, func=AF.Sin, scale=1.0 / 16, bias=hpi), reads=["ph", "hpi"], writes=["cs"])
        for it in range(4):
            tt(t1, cs, cs, ALU.mult, ["cs"], ["t1"])
            tt(t2, sn, sn, ALU.mult, ["sn"], ["t2"])
            tt(t3, cs, sn, ALU.mult, ["cs", "sn"], ["t3"])
            tt(cs, t1, t2, ALU.subtract, ["t1", "t2"], ["cs"])
            ts(sn, t3, 2.0, None, ALU.mult, None, ["t3"], ["sn"])
        PW = T((2, 9, 24))
        self.op(V, lambda e: e.memset(PW[:, 0, 0, :], 1.0), writes=["PW"])
        self.op(V, lambda e: e.memset(PW[:, 1, 0, :], 0.0), reads=["PW"], writes=["PW"])
        tt(PW[:, 0, 1, :], mag, cs, ALU.mult, ["mag", "cs", "PW"], ["PW"])
        tt(PW[:, 1, 1, :], mag, sn, ALU.mult, ["mag", "sn", "PW"], ["PW"])

        def cmul(or_, oi, ar_, ai_, br_, bi_, rk, wk):
            tt(t1, ar_, br_, ALU.mult, rk, ["t1"])
            tt(t2, ai_, bi_, ALU.mult, rk, ["t2"])
            tt(or_, t1, t2, ALU.subtract, ["t1", "t2"] + rk, wk)
            tt(t1, ar_, bi_, ALU.mult, rk, ["t1"])
            tt(t2, ai_, br_, ALU.mult, rk, ["t2"])
            tt(oi, t1, t2, ALU.add, ["t1", "t2"] + rk, wk)

        for k in range(2, 9):
            cmul(PW[:, 0, k, :], PW[:, 1, k, :], PW[:, 0, k - 1, :], PW[:, 1, k - 1, :], PW[:, 0, 1, :], PW[:, 1, 1, :], ["PW"], ["PW"])
        EP = self.EP
        rr = T()
        self.op(V, lambda e: e.reciprocal(out=rr, in_=self.r8), reads=["r8"], writes=["rr"])
        tt(EP[:, 0, 0, :], PW[:, 0, 8, :], rr, ALU.mult, ["PW", "rr", "EP"], ["EP"])
        tt(EP[:, 1, 0, :], PW[:, 1, 8, :], rr, ALU.mult, ["PW", "rr", "EP"], ["EP"])
        for k in range(1, nsteps):
            cmul(EP[:, 0, k, :], EP[:, 1, k, :], EP[:, 0, k - 1, :], EP[:, 1, k - 1, :], EP[:, 0, k - 1, :], EP[:, 1, k - 1, :], ["EP"], ["EP"])
        den, am1, zre, zim = T(), T(), T(), T()
        tt(t1, lre, lre, ALU.mult, ["lre"], ["t1"])
        tt(t2, aim, aim, ALU.mult, ["aim"], ["t2"])
        tt(den, t1, t2, ALU.add, ["t1", "t2"], ["den"])
        self.op(V, lambda e: e.reciprocal(out=den, in_=den), reads=["den"], writes=["den"])
        ts(am1, PW[:, 0, 1, :], -1.0, None, ALU.add, None, ["PW"], ["am1"])
        tt(t1, am1, lre, ALU.mult, ["am1", "lre"], ["t1"])
        tt(t2, PW[:, 1, 1, :], aim, ALU.mult, ["PW", "aim"], ["t2"])
        tt(t3, t1, t2, ALU.add, ["t1", "t2"], ["t3"])
        tt(zre, t3, den, ALU.mult, ["t3", "den"], ["zre"])
        tt(t1, PW[:, 1, 1, :], lre, ALU.mult, ["PW", "lre"], ["t1"])
        tt(t2, am1, aim, ALU.mult, ["am1", "aim"], ["t2"])
        tt(t3, t1, t2, ALU.subtract, ["t1", "t2"], ["t3"])
        tt(zim, t3, den, ALU.mult, ["t3", "den"], ["zim"])
        bbr, bbi, u1, u2 = T((24, 16)), T((24, 16)), T((24, 16)), T((24, 16))

        def bc(a):
            return a.unsqueeze(2).to_broadcast([128, 24, 16])

        def cmul3(or_, oi, pr, pi, br_, bi_, rk, wk):
            tt(u1, br_, bc(pr), ALU.mult, rk, ["u1"])
            tt(u2, bi_, bc(pi), ALU.mult, rk, ["u2"])
            tt(or_, u1, u2, ALU.subtract, ["u1", "u2"] + rk, wk)
            tt(u1, bi_, bc(pr), ALU.mult, rk, ["u1"])
            tt(u2, br_, bc(pi), ALU.mult, rk, ["u2"])
            tt(oi, u1, u2, ALU.add, ["u1", "u2"] + rk, wk)

        cmul3(bbr, bbi, zre, zim, Bre, Bim, ["zre", "zim", "Bre", "Bim"], ["bb"])
        W0 = T((24, 2, 32))
        self.op(V, lambda e: e.memset(W0, 0.0), writes=["W0"])
        self.op("pool", lambda e: e.memset(self.Wout, 0.0), writes=["Wout"])
        wr, wi = T((24, 16)), T((24, 16))
        for gl in range(2):
            P = slice(64 * gl, 64 * gl + 64)
            self.op(V, lambda e, P=P, gl=gl: e.tensor_copy(out=W0[P, :, 0, 16 * gl:16 * gl + 16], in_=Cre[P, :, :]), reads=["Cre", "W0"], writes=["W0"])
            self.op(V, lambda e, P=P, gl=gl: e.tensor_scalar(out=W0[P, :, 1, 16 * gl:16 * gl + 16], in0=Cim[P, :, :], scalar1=-1.0, scalar2=None, op0=ALU.mult), reads=["Cim", "W0"], writes=["W0"])
        for l in range(8):
            cmul3(wr, wi, PW[:, 0, l + 1, :], PW[:, 1, l + 1, :], Cre, Cim, ["PW", "Cre", "Cim"], ["wrwi"])
            for gl in range(2):
                P = slice(64 * gl, 64 * gl + 64)
                self.op(V, lambda e, P=P, gl=gl, l=l: e.tensor_copy(out=self.Wout[P, :, l, 0, 16 * gl:16 * gl + 16], in_=wr[P, :, :]), reads=["wrwi", "Wout"], writes=["Wout"])
                self.op(V, lambda e, P=P, gl=gl, l=l: e.tensor_scalar(out=self.Wout[P, :, l, 1, 16 * gl:16 * gl + 16], in0=wi[P, :, :], scalar1=-1.0, scalar2=None, op0=ALU.mult), reads=["wrwi", "Wout"], writes=["Wout"])
        Kst = ar.alloc([24, 8, 32], BF16, parts=32)
        xr, xi = T((24, 16)), T((24, 16))
        Xpad = T((4, 8, 2, 3, 32))
        for qh in range(2):
            self.op("pool", lambda e: e.memset(Xpad, 0.0), reads=["Xpad"], writes=["Xpad"])
            for j in range(8):
                cmul3(xr, xi, PW[:, 0, 7 - j, :], PW[:, 1, 7 - j, :], bbr, bbi, ["PW", "bb"], ["xrxi"])
                for gl in range(2):
                    P = slice(64 * gl, 64 * gl + 64)
                    for c, xx in ((0, xr), (1, xi)):
                        self.op(V, lambda e, P=P, gl=gl, j=j, c=c, xx=xx, qh=qh: e.tensor_copy(
                            out=Xpad[P, :, j, c, :, 16 * gl:16 * gl + 16],
                            in_=xx[P, 12 * qh:12 * qh + 12, :].rearrange("p (q k) i -> p q k i", k=3)),
                            reads=["xrxi", "Xpad"], writes=["Xpad"])
            n = 0
            for ql in range(4):
                q = 4 * qh + ql
                for j in range(8):
                    for c in range(2):
                        b = (n // 4) % 2
                        col = 128 * (n % 4)
                        self.op("pe", lambda e, ql=ql, j=j, c=c, b=b, col=col: e.transpose(
                            out=self.ps[b][0:96, col:col + 128], in_=Xpad[:, ql, j, c, :, :].rearrange("p k i -> p (k i)"), identity=self.ident_f),
                            reads=["Xpad", "c_ident_f"], writes=[("ps", b)])
                        if n % 4 == 3:
                            j0 = j - 1
                            self.op("act", lambda e, q=q, j0=j0, b=b: e.activation(
                                out=self.Win[:, q, j0:j0 + 2, :, :].rearrange("p a b s -> p (a b s)"), in_=self.ps[b][0:96, :], func=AF.Copy),
                                reads=[("ps", b)], writes=["Win"])
                        n += 1
            n = 0
            for ql in range(4):
                for k in range(3):
                    pair = 3 * (4 * qh + ql) + k
                    for tau in range(8):
                        b = 2 + (n // 16) % 2
                        col = 32 * (n % 16)
                        for c in range(2):
                            self.op("pe", lambda e, ql=ql, k=k, pair=pair, tau=tau, c=c, b=b, col=col: e.matmul(
                                self.ps[b][0:32, col:col + 32], lhsT=Xpad[:, ql, 7 - tau, c, k, :], rhs=W0[:, pair, c, :], start=(c == 0), stop=(c == 1)),
                                reads=["Xpad", "W0"], writes=[("ps", b)])
                        if n % 16 == 15:
                            self.op("act", lambda e, pair=pair, b=b: e.activation(
                                out=Kst[0:32, pair - 1:pair + 1, :, :].rearrange("p a t i -> p (a t i)"), in_=self.ps[b][0:32, :], func=AF.Copy),
                                reads=[("ps", b)], writes=["Kst"])
                        n += 1
        self.op("pool", lambda e: e.memset(self.Kblk, 0.0), writes=["Kblk"])
        for k in range(3):
            for q in range(8):
                self.dma("sp", self.Kblk[32 * k:32 * k + 32, q, :, 32 * k:32 * k + 32],
                         Kst[0:32, 3 * q + k, :, :], reads=["Kst", "Kblk"], writes=["Kblk"], slow=True)
        dT = ar.alloc([8], F32, parts=96)
        self.dma("sp", dT, I["s5_d"][0].rearrange("(q p) -> p q", p=96), writes=["dT"], slow=True)
        for q in range(8):
            self.op(V, lambda e, q=q: e.scalar_tensor_tensor(out=self.Kblk[:, q, 0, :], in0=self.ident_f[0:96, 0:96], scalar=dT[:, q:q + 1], in1=self.Kblk[:, q, 0, :], op0=ALU.mult, op1=ALU.add),
                    reads=["dT", "Kblk", "c_ident_f"], writes=["Kblk"])
        self.sc.barrier()
        ar.release(m)

    def phase_front0(self):
        ar, I, S, NT = self.ar, self.I, self.S, self.NT
        for f in self.late:
            f()
        self.late = []
        wsb = ar.alloc([8, D], BF16)
        for k in range(8):
            self.load_w(wsb[:, k, :], I["w_in"][0, 128 * k:128 * k + 128, :], ("w", k))
        xt = [ar.alloc([4, D], F32) for _ in range(2)]
        self.alloc_norm_work()
        xnT = ar.alloc([8, 512], BF16)
        ust = [ar.alloc([8, 8, 64], BF16, parts=96) for _ in range(2)]
        qmT = ar.alloc([2, 512], BF16)
        PTs = [ar.alloc([512], BF16) for _ in range(2)]
        self.ma_rec = ar.alloc([512], F32)
        mo = [ar.alloc([4, 512], BF16, parts=64) for _ in range(2)]
        Ns = self.Ns
        gEc = ar.alloc([3, Ns], F32)
        gEs = ar.alloc([3, Ns], F32)
        gtg = [ar.alloc([3, max(1, Ns // 2)], F32) for _ in range(4)]
        per_tile = 8 // NT if NT <= 8 else 1

        def load(t):
            for i in range(4):
                self.dma("sp", xt[t % 2][:, i, :], I["x"][512 * t + 128 * i:512 * t + 128 * i + 128, :], writes=[("xt", t % 2, i)])

        load(0)
        for t in range(NT):
            if t + 1 < NT:
                load(t + 1)
            s = t % 2
            self.norm_T(xt[s], [("xt", s, i) for i in range(4)], [("gmix0", xnT, "xnT")], "f0")
            for m in range(8):
                b = m % 2
                for k in range(8):
                    self.op("pe", lambda e, m=m, k=k, b=b: e.matmul(self.ps[b][0:96, :], lhsT=wsb[:, k, 96 * m:96 * m + 96], rhs=xnT[:, k, :], start=(k == 0), stop=(k == 7)),
                            reads=[("w", k), ("xnT", k)], writes=[("ps", b)])
                self.op("act", lambda e, m=m, b=b, s=s: e.activation(out=ust[s][:, m, :, :], in_=self.ps[b][0:96, :].rearrange("p (s l) -> p l s", l=8), func=AF.Copy),
                        reads=[("ps", b)], writes=[("ust", s)])
            for m in range(8):
                self.dma("sp", self.uT_d[m, :, :, 64 * t:64 * t + 64], ust[s][:, m, :, :], reads=[("ust", s)], writes=[("dram", "uT")])
            for c in range(2):
                b = c % 2
                for k in range(8):
                    self.op("pe", lambda e, c=c, k=k, b=b: e.matmul(self.ps[b][:, :], lhsT=wsb[:, k, 768 + 128 * c:768 + 128 * c + 128], rhs=xnT[:, k, :], start=(k == 0), stop=(k == 7)),
                            reads=[("w", k), ("xnT", k)], writes=[("ps", b)])
                self.op("act", lambda e, c=c, b=b: e.activation(out=qmT[:, c, :], in_=self.ps[b][:, :], func=AF.Copy, scale=0.125),
                        reads=[("ps", b)], writes=["qmT"])
            self.mem_attention(0, qmT, "qmT", mo[s], ("mo", s), PTs, "m0")
            self.dma("sp", self.memo_d[0][:, :, 512 * t:512 * t + 512].rearrange("h p t -> p h t"), mo[s], reads=[("mo", s)], writes=[("dram", "memo0")])
            for q in range(t * per_tile, min(8, (t + 1) * per_tile)):
                self.gen_tables(q, gEc, gEs, gtg)

    def gen_tables(self, q, ec, es_, tg):
        Ns, EP = self.Ns, self.EP
        kc, ks = "gEc", "gEs"
        P_ = "pool"
        self.op(P_, lambda e: e.memset(ec[:, :, 0:1], 1.0), writes=[kc])
        self.op(P_, lambda e: e.memset(es_[:, :, 0:1], 0.0), writes=[ks])
        for k in range(self.ks_steps):
            n = 1 << k
            if n >= Ns:
                break
            cb = EP[:, 0, k, 3 * q:3 * q + 3].unsqueeze(2).to_broadcast([128, 3, n])
            sb = EP[:, 1, k, 3 * q:3 * q + 3].unsqueeze(2).to_broadcast([128, 3, n])
            lo = slice(0, n)
            hi = slice(n, 2 * n)
            t1, t2, t3, t4 = [t[:, :, 0:n] for t in tg]
            self.op(P_, lambda e, t1=t1, cb=cb, lo=lo: e.tensor_tensor(out=t1, in0=ec[:, :, lo], in1=cb, op=ALU.mult), reads=[kc, "EP"], writes=["tg0"])
            self.op(P_, lambda e, t2=t2, sb=sb, lo=lo: e.tensor_tensor(out=t2, in0=es_[:, :, lo], in1=sb, op=ALU.mult), reads=[ks, "EP"], writes=["tg1"])
            self.op(P_, lambda e, t3=t3, sb=sb, lo=lo: e.tensor_tensor(out=t3, in0=ec[:, :, lo], in1=sb, op=ALU.mult), reads=[kc, "EP"], writes=["tg2"])
            self.op(P_, lambda e, t4=t4, cb=cb, lo=lo: e.tensor_tensor(out=t4, in0=es_[:, :, lo], in1=cb, op=ALU.mult), reads=[ks, "EP"], writes=["tg3"])
            self.op(P_, lambda e, t1=t1, t2=t2, hi=hi: e.tensor_tensor(out=ec[:, :, hi], in0=t1, in1=t2, op=ALU.subtract), reads=["tg0", "tg1", kc], writes=[kc])
            self.op(P_, lambda e, t3=t3, t4=t4, hi=hi: e.tensor_tensor(out=es_[:, :, hi], in0=t3, in1=t4, op=ALU.add), reads=["tg2", "tg3", ks], writes=[ks])
        self.dma("sp", self.E_d[q, 0], ec.rearrange("p a b -> p (a b)"), reads=[kc], writes=[("dram", "E", q)])
        self.dma("sp", self.E_d[q, 1], es_.rearrange("p a b -> p (a b)"), reads=[ks], writes=[("dram", "E", q)])

    def phase_s5(self):
        ar, S, Ns = self.ar, self.S, self.Ns
        nsteps = self.ks_steps
        PAD = 1 << (nsteps - 1)
        ut = [ar.alloc([8, Ns], BF16, parts=96) for _ in range(2)]
        Hsh = ar.alloc([3, 2, Ns], BF16)
        gq = [ar.alloc([S], BF16, parts=96) for _ in range(2)]
        x2 = [ar.alloc([Ns], F32, parts=96) for _ in range(2)]
        sg = [ar.alloc([Ns], F32, parts=96) for _ in range(2)]
        Ec = [ar.alloc([3, Ns], F32) for _ in range(2)]
        Es = [ar.alloc([3, Ns], F32) for _ in range(2)]
        NW = 8
        wk = [[ar.alloc([Ns], F32) for _ in range(NW)] for _ in range(2)]
        self.op("pool", lambda e: e.memset(Hsh[:, :, :, 0:1], 0.0), writes=["Hsh"])

        def gen_tables(q):
            sl = q % 2
            self.dma("sp", Ec[sl].rearrange("p a b -> p (a b)"), self.E_d[q, 0], reads=[("dram", "E", q)], writes=[("Ec", sl)])
            self.dma("sp", Es[sl].rearrange("p a b -> p (a b)"), self.E_d[q, 1], reads=[("dram", "E", q)], writes=[("Es", sl)])

        def load(q):
            self.dma("sp", ut[q % 2], self.uT_d[q], reads=[("dram", "uT")], writes=[("ut", q % 2)])

        load(0)
        gen_tables(0)
        npair = 0
        for q in range(8):
            if q + 1 < 8:
                load(q + 1)
                gen_tables(q + 1)
            u = ut[q % 2]
            ukey = ("ut", q % 2)
            sl = q % 2
            for k in range(3):
                pair = 3 * q + k
                R = slice(32 * k, 32 * k + 32)
                W = wk[npair % 2]
                wkey = lambda i, npair=npair: ("wk", npair % 2, i)
                npair += 1
                for c in range(2):
                    b = c
                    for j in range(8):
                        self.op("pe", lambda e, R=R, q=q, j=j, c=c, b=b, u=u: e.matmul(self.ps[b][:, 0:Ns], lhsT=self.Win[R, q, j, c, :], rhs=u[R, j, :], start=(j == 0), stop=(j == 7)),
                                reads=["Win", ukey], writes=[("ps", b)])
                ec, es_ = Ec[sl][:, k, :], Es[sl][:, k, :]
                kc, ks = ("Ec", sl), ("Es", sl)
                Sre, Sim = self.ps[0][:, 0:Ns], self.ps[1][:, 0:Ns]
                D_ = "dve"
                self.op(D_, lambda e, W=W, ec=ec, Sre=Sre: e.tensor_tensor(out=W[0], in0=Sre, in1=ec, op=ALU.mult), reads=[("ps", 0), kc], writes=[wkey(0)])
                self.op(D_, lambda e, W=W, es_=es_, Sim=Sim: e.tensor_tensor(out=W[1], in0=Sim, in1=es_, op=ALU.mult), reads=[("ps", 1), ks], writes=[wkey(1)])
                self.op(D_, lambda e, W=W, ec=ec, Sim=Sim: e.tensor_tensor(out=W[2], in0=Sim, in1=ec, op=ALU.mult), reads=[("ps", 1), kc], writes=[wkey(2)])
                self.op(D_, lambda e, W=W, es_=es_, Sre=Sre: e.tensor_tensor(out=W[3], in0=Sre, in1=es_, op=ALU.mult), reads=[("ps", 0), ks], writes=[wkey(3)])
                self.op(D_, lambda e, W=W: e.tensor_tensor(out=W[0], in0=W[0], in1=W[1], op=ALU.add), reads=[wkey(0), wkey(1)], writes=[wkey(0)])
                self.op(D_, lambda e, W=W: e.tensor_tensor(out=W[2], in0=W[2], in1=W[3], op=ALU.subtract), reads=[wkey(2), wkey(3)], writes=[wkey(2)])
                r8b = self.r8[:, pair:pair + 1].to_broadcast([128, Ns])
                self.op(D_, lambda e, W=W, r8b=r8b: e.tensor_tensor_scan(out=W[4], data0=r8b, data1=W[0], initial=0.0, op0=ALU.mult, op1=ALU.add), reads=[wkey(0), "r8"], writes=[wkey(4)])
                self.op(D_, lambda e, W=W, r8b=r8b: e.tensor_tensor_scan(out=W[5], data0=r8b, data1=W[2], initial=0.0, op0=ALU.mult, op1=ALU.add), reads=[wkey(2), "r8"], writes=[wkey(5)])
                n1 = Ns - 1
                self.op(D_, lambda e, W=W, ec=ec: e.tensor_tensor(out=W[0], in0=W[4], in1=ec, op=ALU.mult), reads=[wkey(4), kc], writes=[wkey(0)])
                self.op(D_, lambda e, W=W, es_=es_: e.tensor_tensor(out=W[1], in0=W[5], in1=es_, op=ALU.mult), reads=[wkey(5), ks], writes=[wkey(1)])
                self.op(D_, lambda e, W=W, ec=ec: e.tensor_tensor(out=W[2], in0=W[5], in1=ec, op=ALU.mult), reads=[wkey(5), kc], writes=[wkey(2)])
                self.op(D_, lambda e, W=W, es_=es_: e.tensor_tensor(out=W[3], in0=W[4], in1=es_, op=ALU.mult), reads=[wkey(4), ks], writes=[wkey(3)])
                self.op(D_, lambda e, W=W, k=k, n1=n1: e.tensor_tensor(out=Hsh[:, k, 0, 1:Ns], in0=W[0][:, 0:n1], in1=W[1][:, 0:n1], op=ALU.subtract), reads=[wkey(0), wkey(1), "Hsh"], writes=["Hsh"])
                self.op(D_, lambda e, W=W, k=k, n1=n1: e.tensor_tensor(out=Hsh[:, k, 1, 1:Ns], in0=W[2][:, 0:n1], in1=W[3][:, 0:n1], op=ALU.add), reads=[wkey(2), wkey(3), "Hsh"], writes=["Hsh"])
            g = gq[q % 2]
            gkey = ("gq", q % 2)
            for l in range(8):
                b = 2 + l % 2
                first = True
                for tau in range(l + 1):
                    self.op("pe", lambda e, q=q, tau=tau, l=l, b=b, u=u, first=first: e.matmul(self.ps[b][0:96, 0:Ns], lhsT=self.Kblk[:, q, tau, :], rhs=u[:, l - tau, :], start=first, stop=(tau == l)),
                            reads=["Kblk", ukey], writes=[("ps", b)])
                    first = False
                for k in range(3):
                    for c in range(2):
                        last = (c == 1)
                        self.op("pe", lambda e, q=q, k=k, c=c, l=l, b=b, last=last: e.matmul(self.ps[b][32 * k:32 * k + 32, 0:Ns], lhsT=self.Wout[:, 3 * q + k, l, c, :], rhs=Hsh[:, k, c, :], start=False, stop=last, skip_group_check=True),
                                reads=["Wout", "Hsh"], writes=[("ps", b)])
                yp = self.ps[b][0:96, 0:Ns]
                a2, a3 = x2[l % 2], sg[l % 2]
                k2, k3 = ("x2", l % 2), ("sg", l % 2)
                self.op("act", lambda e, yp=yp, a2=a2: e.activation(out=a2, in_=yp, func=AF.Square), reads=[("ps", b)], writes=[k2])
                self.op("dve", lambda e, a2=a2: e.tensor_scalar(out=a2, in0=a2, scalar1=0.044715, scalar2=1.0, op0=ALU.mult, op1=ALU.add), reads=[k2], writes=[k2])
                self.op("dve", lambda e, a2=a2, yp=yp: e.tensor_tensor(out=a2, in0=a2, in1=yp, op=ALU.mult), reads=[k2, ("ps", b)], writes=[k2])
                self.op("act", lambda e, a2=a2, a3=a3: e.activation(out=a3, in_=a2, func=AF.Sigmoid, scale=1.5957691216057308), reads=[k2], writes=[k3])
                self.op("dve", lambda e, a3=a3, yp=yp, g=g, l=l: e.tensor_tensor(out=g.rearrange("p (s l) -> p l s", l=8)[:, l, :], in0=a3, in1=yp, op=ALU.mult),
                        reads=[k3, ("ps", b)], writes=[gkey])
            self.dma("sp", self.gT_d[q], g, reads=[gkey], writes=[("dram", "gT")])

    def load_wup(self, l, wup):
        for k in range(8):
            self.load_w(wup[:, k, :], self.I["w_ffn_up"][l, 128 * k:128 * k + 128, :], ("wup", k))

    def sweep_glu_wout0(self, after_weights=None):
        ar, I, S, NT = self.ar, self.I, self.S, self.NT
        wg = ar.alloc([8, 768], BF16, parts=96)
        woT = ar.alloc([8, D], BF16, parts=96)
        woM = ar.alloc([4, D], BF16, parts=64)
        self.load_w(wg.rearrange("p q n -> p (q n)") if False else wg[:, 0, :], I["w_glu"][0, 0:96, :], ("wg", 0))
        for q in range(1, 8):
            self.load_w(wg[:, q, :], I["w_glu"][0, 96 * q:96 * q + 96, :], ("wg", q))
        for q in range(8):
            self.load_w(woT[:, q, :], I["w_out"][0, 96 * q:96 * q + 96, :], ("woT", q))
        for h in range(4):
            self.load_w(woM[:, h, :], I["w_out"][0, 768 + 64 * h:768 + 64 * h + 64, :], ("woM", h))
        if after_weights is not None:
            after_weights()
        gt = [ar.alloc([8, 512], BF16, parts=96) for _ in range(2)]
        mo = [ar.alloc([4, 512], BF16, parts=64) for _ in range(2)]
        xt = [ar.alloc([4, D], F32) for _ in range(2)]
        tokT = ar.alloc([8, 512], BF16, parts=96)
        sig = [ar.alloc([512], F32, parts=96) for _ in range(2)]

        def load(t):
            s = t % 2
            self.dma("sp", gt[s], self.gT_d[:, :, 512 * t:512 * t + 512].rearrange("q p t -> p q t"), reads=[("dram", "gT")], writes=[("gt", s)])
            self.dma("sp", mo[s], self.memo_d[0][:, :, 512 * t:512 * t + 512].rearrange("h p t -> p h t"), reads=[("dram", "memo0")], writes=[("mo", s)])
            for i in range(4):
                self.dma("sp", xt[s][:, i, :], I["x"][512 * t + 128 * i:512 * t + 128 * i + 128, :], reads=[("dram", "hA", t)], writes=[("xt", s, i)])

        load(0)
        for t in range(NT):
            if t + 1 < NT:
                load(t + 1)
            s = t % 2
            for m in range(8):
                b = m % 2
                for q in range(8):
                    self.op("pe", lambda e, m=m, q=q, b=b, s=s: e.matmul(self.ps[b][0:96, :], lhsT=wg[:, q, 96 * m:96 * m + 96], rhs=gt[s][:, q, :], start=(q == 0), stop=(q == 7)),
                            reads=[("wg", q), ("gt", s)], writes=[("ps", b)])
                sgb = sig[m % 2]
                self.op("act", lambda e, b=b, sgb=sgb: e.activation(out=sgb, in_=self.ps[b][0:96, :], func=AF.Sigmoid), reads=[("ps", b)], writes=[("sig", m % 2)])
                self.op("dve", lambda e, m=m, s=s, sgb=sgb: e.tensor_tensor(out=tokT[:, m, :], in0=gt[s][:, m, :], in1=sgb, op=ALU.mult),
                        reads=[("sig", m % 2), ("gt", s)], writes=[("tokT", m)])
            self.wout_residual(xt[s], [("xt", s, i) for i in range(4)],
                               [(tokT[:, q, :], woT[:, q, :], [("tokT", q), ("woT", q)]) for q in range(8)] +
                               [(mo[s][0:64, h, :], woM[0:64, h, :], [("mo", s), ("woM", h)]) for h in range(4)])
            for i in range(4):
                self.dma("sp", self.hA[512 * t + 128 * i:512 * t + 128 * i + 128, :], xt[s][:, i, :], reads=[("xt", s, i)], writes=[("dram", "hA", t)])

    def wout_residual(self, ht, hkeys, terms):
        n = len(terms)
        for sub in range(4):
            for half in range(2):
                b = 2 + (2 * sub + half) % 2
                for ti, (aT, w, keys) in enumerate(terms):
                    self.op("pe", lambda e, aT=aT, w=w, sub=sub, half=half, b=b, ti=ti: e.matmul(self.ps[b][:, :], lhsT=aT[:, 128 * sub:128 * sub + 128], rhs=w[:, 512 * half:512 * half + 512], start=(ti == 0), stop=(ti == n - 1)),
                            reads=keys, writes=[("ps", b)])
                self.op("dve", lambda e, sub=sub, half=half, b=b: e.tensor_tensor(out=ht[:, sub, 512 * half:512 * half + 512], in0=self.ps[b][:, :], in1=ht[:, sub, 512 * half:512 * half + 512], op=ALU.add),
                        reads=[("ps", b), hkeys[sub]], writes=[hkeys[sub]])

    def sweep_ffn(self, l, hin, hout, final, wup=None):
        ar, I, S, NT = self.ar, self.I, self.S, self.NT
        if wup is None:
            wup = ar.alloc([8, 2 * DFF], BF16)
            self.load_wup(l, wup)
        wdn = ar.alloc([NJ, D], BF16)
        for j in range(NJ):
            self.load_w(wdn[:, j, :], I["w_ffn_down"][l, 128 * j:128 * j + 128, :], ("wdn", j))
        ht = ar.alloc([4, D], F32)
        self.alloc_norm_work()
        xnT = ar.alloc([8, 512], BF16)
        act = ar.alloc([NJ, 512], BF16)
        Gb = [ar.alloc([514], F32) for _ in range(2)]
        T1 = [ar.alloc([512], F32) for _ in range(2)]
        carry = ar.alloc([NJ, 2], F32)
        self.op("pool", lambda e: e.memset(carry, 0.0), writes=["carry"])
        cw, cb = self.cw[l], self.cb[l]
        gname = "gffn%d" % l
        if final:
            yss = ar.alloc([4], F32)
        for t in range(NT):
            for i in range(4):
                self.dma("sp", ht[:, i, :], hin[512 * t + 128 * i:512 * t + 128 * i + 128, :], reads=[("dram", "h_in", l, t)] + ([("dram", "hA", t)] if True else []), writes=[("ht", i)])
            self.norm_T(ht, [("ht", i) for i in range(4)], [(gname, xnT, "xnT")], "ffn")
            def chain_a(j):
                bg = (j % 3) * 2 + 1
                G, Tt = Gb[j % 2], T1[j % 2]
                kG, kT = ("Gb", j % 2), ("T1", j % 2)
                self.op("pool", lambda e: e.tensor_copy(out=G[:, 0:2], in_=carry[:, j, :]), reads=["carry", kG], writes=[kG])
                self.op("act", lambda e: e.activation(out=G[:, 2:514], in_=self.ps[bg][:, :], func=AF.Copy), reads=[("ps", bg), kG], writes=[kG])
                self.op("pool", lambda e: e.tensor_copy(out=carry[:, j, :], in_=G[:, 512:514]), reads=[kG, "carry"], writes=["carry"])
                self.op("act", lambda e: e.activation(out=Tt, in_=G[:, 2:514], func=AF.Identity, scale=cw[:, j, 2:3], bias=cb[:, j:j + 1]),
                        reads=[kG, ("cw", l), ("cb", l)], writes=[kT])
                self.op("dve", lambda e: e.scalar_tensor_tensor(out=Tt, in0=G[:, 1:513], scalar=cw[:, j, 1:2], in1=Tt, op0=ALU.mult, op1=ALU.add),
                        reads=[kG, kT, ("cw", l)], writes=[kT])
                self.op("dve", lambda e: e.scalar_tensor_tensor(out=Tt, in0=G[:, 0:512], scalar=cw[:, j, 0:1], in1=Tt, op0=ALU.mult, op1=ALU.add),
                        reads=[kG, kT, ("cw", l)], writes=[kT])

            def chain_b(j):
                ba = (j % 3) * 2
                Tt = T1[j % 2]
                kT = ("T1", j % 2)
                self.op("act", lambda e: e.activation(out=Tt, in_=Tt, func=AF.Silu), reads=[kT], writes=[kT])
                self.op("dve", lambda e: e.tensor_tensor(out=act[:, j, :], in0=Tt, in1=self.ps[ba][:, :], op=ALU.mult),
                        reads=[kT, ("ps", ba)], writes=[("act", j)])

            for j in range(NJ):
                ba, bg = (j % 3) * 2, (j % 3) * 2 + 1
                for (bb, off) in ((ba, 0), (bg, DFF)):
                    for k in range(8):
                        self.op("pe", lambda e, j=j, k=k, bb=bb, off=off: e.matmul(self.ps[bb][:, :], lhsT=wup[:, k, off + 128 * j:off + 128 * j + 128], rhs=xnT[:, k, :], start=(k == 0), stop=(k == 7)),
                                reads=[("wup", k), ("xnT", k)], writes=[("ps", bb)])
                chain_a(j)
                if not FFN_PIPE:
                    chain_b(j)
                elif j >= 1:
                    chain_b(j - 1)
            if FFN_PIPE:
                chain_b(NJ - 1)
            for sub in range(4):
                for half in range(2):
                    b = 6 + (2 * sub + half) % 2
                    for j in range(NJ):
                        self.op("pe", lambda e, j=j, sub=sub, half=half, b=b: e.matmul(self.ps[b][:, :], lhsT=act[:, j, 128 * sub:128 * sub + 128], rhs=wdn[:, j, 512 * half:512 * half + 512], start=(j == 0), stop=(j == NJ - 1)),
                                reads=[("act", j), ("wdn", j)], writes=[("ps", b)])
                    self.op("dve", lambda e, sub=sub, half=half, b=b: e.tensor_tensor(out=ht[:, sub, 512 * half:512 * half + 512], in0=self.ps[b][:, :], in1=ht[:, sub, 512 * half:512 * half + 512], op=ALU.add),
                            reads=[("ps", b), ("ht", sub)], writes=[("ht", sub)])
            if not final:
                for i in range(4):
                    self.dma("sp", hout[512 * t + 128 * i:512 * t + 128 * i + 128, :], ht[:, i, :], reads=[("ht", i)], writes=[("dram", "hB", t)])
            else:
                w = self.nt_work
                for i in range(4):
                    ss = yss[:, i:i + 1]
                    self.op("act", lambda e, i=i, ss=ss: e.activation(out=w["junk"], in_=ht[:, i, :], func=AF.Square, accum_out=ss), reads=[("ht", i)], writes=["njunk", ("yss", i)])
                    self.op("act", lambda e, ss=ss: e.activation(out=ss, in_=ss, func=AF.Sqrt, scale=1.0 / D, bias=EPS), reads=[("yss", i)], writes=[("yss", i)])
                    self.op("dve", lambda e, ss=ss: e.reciprocal(out=ss, in_=ss), reads=[("yss", i)], writes=[("yss", i)])
                    self.op("act", lambda e, i=i, ss=ss: e.activation(out=ht[:, i, :], in_=ht[:, i, :], func=AF.Copy, scale=ss), reads=[("ht", i), ("yss", i)], writes=[("ht", i)])
                    self.op("dve", lambda e, i=i: e.tensor_tensor(out=ht[:, i, :], in0=ht[:, i, :], in1=self.gF, op=ALU.mult), reads=[("ht", i), "gF"], writes=[("ht", i)])
                    self.dma("sp", self.y[512 * t + 128 * i:512 * t + 128 * i + 128, :], ht[:, i, :], reads=[("ht", i)], writes=[("dram", "y", t)])

    def sweep_kv_front1(self):
        ar, I, S, NT = self.ar, self.I, self.S, self.NT
        wkv = ar.alloc([8, 1536], BF16)
        wfg = ar.alloc([8, 12], BF16)
        win = ar.alloc([8, D], BF16)
        for k in range(8):
            self.load_w(wkv[:, k, :], I["w_kv"][128 * k:128 * k + 128, :], ("wkv", k))
            self.load_w(win[:, k, :], I["w_in"][1, 128 * k:128 * k + 128, :], ("win", k))
        self.dma("pool", wfg, I["w_fgate"].rearrange("(k p) h -> p k h", p=128), writes=["wfg"])
        ht = [ar.alloc([4, D], F32) for _ in range(2)]
        self.alloc_norm_work()
        hsT = ar.alloc([8, 512], BF16)
        hnT = ar.alloc([8, 512], BF16)
        kst = [ar.alloc([6, 512], BF16) for _ in range(2)]
        qst = [ar.alloc([6, 512], BF16) for _ in range(2)]
        vst = [ar.alloc([4, 768], BF16) for _ in range(2)]
        qmT = ar.alloc([2, 512], BF16)
        PTs = [ar.alloc([512], BF16) for _ in range(2)]
        self.ma_rec = ar.alloc([512], F32)
        mo = [ar.alloc([4, 512], BF16, parts=64) for _ in range(2)]
        ex = ar.alloc([512], F32)

        def load(t):
            for i in range(4):
                self.dma("sp", ht[t % 2][:, i, :], self.hB[512 * t + 128 * i:512 * t + 128 * i + 128, :], reads=[("dram", "hB", t)], writes=[("ht", t % 2, i)])

        load(0)
        for t in range(NT):
            if t + 1 < NT:
                load(t + 1)
            s = t % 2
            self.norm_T(ht[s], [("ht", s, i) for i in range(4)], [("gkv", hsT, "hsT"), ("gmix1", hnT, "hnT")], "kv")
            for m in range(6):
                b = m % 2
                for k in range(8):
                    self.op("pe", lambda e, m=m, k=k, b=b: e.matmul(self.ps[b][:, :], lhsT=wkv[:, k, 128 * m:128 * m + 128], rhs=hsT[:, k, :], start=(k == 0), stop=(k == 7)),
                            reads=[("wkv", k), ("hsT", k)], writes=[("ps", b)])
                self.op("act", lambda e, m=m, b=b, s=s: e.activation(out=kst[s][:, m, :], in_=self.ps[b][:, :], func=AF.Copy), reads=[("ps", b)], writes=[("kst", s)])
            self.dma("sp", self.kT_d[:, 512 * t:512 * t + 512].rearrange("(m p) t -> p m t", p=128), kst[s], reads=[("kst", s)], writes=[("dram", "kT")])
            for sub in range(4):
                for half in range(2):
                    b = 2 + (2 * sub + half) % 2
                    for k in range(8):
                        self.op("pe", lambda e, sub=sub, half=half, k=k, b=b: e.matmul(self.ps[b][:, 0:384], lhsT=hsT[:, k, 128 * sub:128 * sub + 128], rhs=wkv[:, k, 768 + 384 * half:768 + 384 * half + 384], start=(k == 0), stop=(k == 7)),
                                reads=[("wkv", k), ("hsT", k)], writes=[("ps", b)])
                    self.op("dve", lambda e, sub=sub, half=half, b=b, s=s: e.tensor_copy(out=vst[s][:, sub, 384 * half:384 * half + 384], in_=self.ps[b][:, 0:384]), reads=[("ps", b)], writes=[("vst", s)])
            self.dma("sp", self.v_d[512 * t:512 * t + 512, :].rearrange("(i p) d -> p i d", p=128), vst[s], reads=[("vst", s)], writes=[("dram", "v")])
            b = 4
            for k in range(8):
                self.op("pe", lambda e, k=k, b=b: e.matmul(self.ps[b][0:12, :], lhsT=wfg[:, k, :], rhs=hsT[:, k, :], start=(k == 0), stop=(k == 7)),
                        reads=["wfg", ("hsT", k)], writes=[("ps", b)])
            self.op("act", lambda e, b=b: e.activation(out=ex[0:12, :], in_=self.ps[b][0:12, :], func=AF.Exp, scale=-1.0, bias=self.nbf[0:12, :]), reads=[("ps", b), "nbf"], writes=["ex"])
            self.op("act", lambda e, t=t: e.activation(out=self.cs[0:12, 512 * t:512 * t + 512], in_=ex[0:12, :], func=AF.Ln, bias=1.0), reads=["ex"], writes=["cs"])
            for m in range(6):
                b = m % 2
                for k in range(8):
                    self.op("pe", lambda e, m=m, k=k, b=b: e.matmul(self.ps[b][:, :], lhsT=win[:, k, 128 * m:128 * m + 128], rhs=hnT[:, k, :], start=(k == 0), stop=(k == 7)),
                            reads=[("win", k), ("hnT", k)], writes=[("ps", b)])
                self.op("act", lambda e, m=m, b=b, s=s: e.activation(out=qst[s][:, m, :], in_=self.ps[b][:, :], func=AF.Copy, scale=0.125), reads=[("ps", b)], writes=[("qst", s)])
            self.dma("sp", self.qT_d[:, 512 * t:512 * t + 512].rearrange("(m p) t -> p m t", p=128), qst[s], reads=[("qst", s)], writes=[("dram", "qT")])
            for c in range(2):
                b = c % 2
                for k in range(8):
                    self.op("pe", lambda e, c=c, k=k, b=b: e.matmul(self.ps[b][:, :], lhsT=win[:, k, 768 + 128 * c:768 + 128 * c + 128], rhs=hnT[:, k, :], start=(k == 0), stop=(k == 7)),
                            reads=[("win", k), ("hnT", k)], writes=[("ps", b)])
                self.op("act", lambda e, c=c, b=b: e.activation(out=qmT[:, c, :], in_=self.ps[b][:, :], func=AF.Copy, scale=0.125), reads=[("ps", b)], writes=["qmT"])
            self.mem_attention(1, qmT, "qmT", mo[s], ("mo", s), PTs, "m1")
            self.dma("sp", self.memo_d[1][:, :, 512 * t:512 * t + 512].rearrange("h p t -> p h t"), mo[s], reads=[("mo", s)], writes=[("dram", "memo1")])

    def phase_fox(self):
        ar, S, NT, NB = self.ar, self.S, self.NT, self.NB
        cs = self.cs
        ones12 = ar.alloc([S], F32)
        self.op("pool", lambda e: e.memset(ones12[0:12, :], 1.0), writes=["ones12"])
        cum = ar.alloc([S], F32)
        self.op("dve", lambda e: e.tensor_tensor_scan(out=cum[0:12, :], data0=ones12[0:12, :], data1=cs[0:12, :], initial=0.0, op0=ALU.mult, op1=ALU.add),
                reads=["cs", "ones12"], writes=["cum"])
        csKP = ar.alloc([NB, 12], F32)
        CL = ar.alloc([NB, 12], F32)
        CM = ar.alloc([NB, 12], F32)
        DL = ar.alloc([NB, 12], F32)
        for j in range(NB):
            self.op("pe", lambda e, j=j: e.transpose(out=self.ps[0][:, 12 * j:12 * j + 12], in_=cum[0:12, 128 * j:128 * j + 128], identity=self.ident_f[0:12, 0:12]),
                    reads=["cum", "c_ident_f"], writes=[("ps", 0)])
        self.op("act", lambda e: e.activation(out=csKP.rearrange("p j h -> p (j h)"), in_=self.ps[0][:, 0:12 * NB], func=AF.Copy), reads=[("ps", 0)], writes=["csKP"])
        for (sel, dst, kd, b) in ((self.sel127, CL, "CL", 1), (self.sel63, CM, "CM", 2)):
            self.op("pe", lambda e, sel=sel, b=b: e.matmul(self.ps[b][:, 0:12 * NB], lhsT=sel, rhs=csKP.rearrange("p j h -> p (j h)"), start=True, stop=True),
                    reads=["csKP", "c_sel127", "c_sel63"], writes=[("ps", b)])
            self.op("act", lambda e, dst=dst, b=b: e.activation(out=dst.rearrange("p j h -> p (j h)"), in_=self.ps[b][:, 0:12 * NB], func=AF.Copy), reads=[("ps", b)], writes=[kd])
        for Ig in range(NT):
            self.op("dve", lambda e, Ig=Ig: e.tensor_tensor(out=DL[:, 4 * Ig:4 * Ig + 4, :], in0=CL[:, 4 * Ig + 1:4 * Ig + 2, :].to_broadcast([128, 4, 12]), in1=CM[:, 4 * Ig:4 * Ig + 4, :], op=ALU.subtract),
                    reads=["CL", "CM"], writes=["DL"])
        qa = [ar.alloc([S], BF16) for _ in range(2)]
        ka = [ar.alloc([S], BF16) for _ in range(2)]
        Vh = [ar.alloc([NB, 128], BF16) for _ in range(2)]
        oh = [ar.alloc([S], BF16) for _ in range(2)]
        NPT = 5
        PT = [ar.alloc([512], BF16) for _ in range(NPT)]
        BIAS = [ar.alloc([NB], F32) for _ in range(2)]
        rcs = [ar.alloc([512], F32, parts=1) for _ in range(2)]
        bcs = [ar.alloc([512], F32) for _ in range(2)]
        rch = [ar.alloc([512], BF16, parts=1) for _ in range(2)]
        rcl = [ar.alloc([512], BF16, parts=1) for _ in range(2)]
        onesb = ar.alloc([128], BF16, parts=1)
        self.op("pool", lambda e: e.memset(onesb, 1.0), writes=["onesb"])
        for s in range(2):
            self.op("pool", lambda e, s=s: e.memset(ka[s][64:65, :], 1.0), writes=[("ka1", s)])
            self.op("pool", lambda e, s=s: e.memset(Vh[s][:, :, 0:64], 0.0), writes=[("Vh1", s)])
            self.op("pool", lambda e, s=s: e.memset(Vh[s][:, :, 0:1], 1.0), reads=[("Vh1", s)], writes=[("Vh1", s)])

        def load(h):
            s = h % 2
            self.dma("sp", qa[s][0:64, :], self.qT_d[64 * h:64 * h + 64, :], reads=[("dram", "qT")], writes=[("qa", s)])
            self.dma("sp", ka[s][0:64, :], self.kT_d[64 * h:64 * h + 64, :], reads=[("dram", "kT")], writes=[("ka", s)])
            self.dma("sp", Vh[s][:, :, 64:128], self.v_d[:, 64 * h:64 * h + 64].rearrange("(j p) d -> p j d", p=128), reads=[("dram", "v")], writes=[("Vh", s)])
            self.op("dve", lambda e, s=s, h=h: e.tensor_copy(out=qa[s][64:65, :].rearrange("p (j t) -> p j t", t=128), in_=DL[64:65, :, h:h + 1].to_broadcast([1, NB, 128])),
                    reads=["DL", ("qa1", s)], writes=[("qa1", s)])

        its = []
        for h in range(12):
            for Ig in range(NT):
                for j in range(4 * Ig + 4):
                    its.append((h, Ig, j, 4 * Ig + 4))
        NSB = 4

        def group_pre(h, Ig):
            s = h % 2
            g = h * NT + Ig
            bi = BIAS[g % 2]
            nj = 4 * Ig + 4
            self.op("dve", lambda e, bi=bi, nj=nj, Ig=Ig, h=h: e.tensor_scalar(out=bi[:, 0:nj], in0=csKP[:, 0:nj, h], scalar1=CL[:, 4 * Ig + 1, h:h + 1], scalar2=None, op0=ALU.subtract),
                    reads=["csKP", "CL"], writes=[("BIAS", g % 2)])

        def emit_st(idx):
            h, Ig, j, nj = its[idx]
            s = h % 2
            if j == 0:
                group_pre(h, Ig)
            c0 = max(0, 128 * (j - 4 * Ig))
            bs = idx % NSB
            self.op("pe", lambda e, s=s, j=j, Ig=Ig, c0=c0, bs=bs: e.matmul(self.ps[bs][:, c0:512], lhsT=ka[s][0:65, 128 * j:128 * j + 128], rhs=qa[s][0:65, 512 * Ig + c0:512 * Ig + 512], start=True, stop=True),
                    reads=[("ka", s), ("ka1", s), ("qa", s), ("qa1", s)], writes=[("ps", bs)])

        def emit_exp_pv(idx):
            h, Ig, j, nj = its[idx]
            s = h % 2
            g = h * NT + Ig
            a = j - 4 * Ig
            c0 = max(0, 128 * a)
            bs = idx % NSB
            pt = PT[idx % NPT]
            kpt = ("PTf", idx % NPT)
            bi = BIAS[g % 2]
            po = self.ps[4 + g % 2]
            ko = ("ps", 4 + g % 2)
            self.op("act", lambda e, pt=pt, bs=bs, c0=c0, bi=bi, j=j: e.activation(out=pt[:, c0:512], in_=self.ps[bs][:, c0:512], func=AF.Exp, bias=bi[:, j:j + 1]),
                    reads=[("ps", bs), ("BIAS", g % 2)], writes=[kpt])
            if a >= 0:
                self.op("pool", lambda e, pt=pt, c0=c0: e.tensor_tensor(out=pt[:, c0:c0 + 128], in0=pt[:, c0:c0 + 128], in1=self.mask, op=ALU.mult),
                        reads=[kpt, "c_mask"], writes=[kpt])
            self.op("pe", lambda e, s=s, j=j, pt=pt, c0=c0, po=po, nj=nj: e.matmul(po[:, c0:512], lhsT=Vh[s][:, j, :], rhs=pt[:, c0:512], start=(j == 0), stop=(j == nj - 1), skip_group_check=True),
                    reads=[("Vh", s), ("Vh1", s), kpt], writes=[ko])

        def tail_a(g):
            po = self.ps[4 + g % 2]
            ko = ("ps", 4 + g % 2)
            rc, rh, rl = rcs[g % 2], rch[g % 2], rcl[g % 2]
            self.op("dve", lambda e: e.reciprocal(out=rc[0:1, :], in_=po[0:1, :]), reads=[ko], writes=[("rcs", g % 2)])
            self.op("dve", lambda e: e.tensor_copy(out=rh[0:1, :], in_=rc[0:1, :]), reads=[("rcs", g % 2)], writes=[("rch", g % 2)])
            self.op("dve", lambda e: e.tensor_tensor(out=rl[0:1, :], in0=rc[0:1, :], in1=rh[0:1, :], op=ALU.subtract), reads=[("rcs", g % 2), ("rch", g % 2)], writes=[("rcl", g % 2)])

        def tail(g):
            h, Ig = g // NT, g % NT
            s = h % 2
            po = self.ps[4 + g % 2]
            ko = ("ps", 4 + g % 2)
            rc, bc = rcs[g % 2], bcs[g % 2]
            rh, rl = rch[g % 2], rcl[g % 2]
            self.op("pe", lambda e: e.matmul(self.ps[6][:, :], lhsT=onesb[0:1, 0:128], rhs=rh[0:1, :], start=True, stop=False),
                    reads=["onesb", ("rch", g % 2)], writes=[("ps", 6)])
            self.op("pe", lambda e: e.matmul(self.ps[6][:, :], lhsT=onesb[0:1, 0:128], rhs=rl[0:1, :], start=False, stop=True),
                    reads=["onesb", ("rcl", g % 2)], writes=[("ps", 6)])
            self.op("act", lambda e, bc=bc: e.activation(out=bc[64:128, :], in_=self.ps[6][64:128, :], func=AF.Copy), reads=[("ps", 6)], writes=[("bcs", g % 2)])
            self.op("dve", lambda e, bc=bc, po=po, s=s, Ig=Ig: e.tensor_tensor(out=oh[s][64:128, 512 * Ig:512 * Ig + 512], in0=po[64:128, :], in1=bc[64:128, :], op=ALU.mult),
                    reads=[ko, ("bcs", g % 2)], writes=[("oh", s)])
            if Ig == NT - 1:
                self.dma("sp", self.oT_d[64 * h:64 * h + 64, :], oh[s][64:128, :], reads=[("oh", s)], writes=[("dram", "oT")])
            if self.debug and h == 2:
                self.dma("sp", self.dbg_rc[Ig:Ig + 1, :], rc[0:1, :], reads=[("rcs", g % 2)], writes=[("dram", "dbg")])
                if Ig == NT - 1:
                    self.dma("sp", self.dbg_qa1, qa[s][64:65, :], reads=[("qa1", s)], writes=[("dram", "dbg3")])
                    self.dma("sp", self.dbg_ka1, ka[s][64:65, :], reads=[("ka1", s)], writes=[("dram", "dbg4")])
                    self.dma("sp", self.dbg_v1, Vh[s][:, :, 0:2], reads=[("Vh1", s), ("Vh", s)], writes=[("dram", "dbg5")])

        load(0)
        LOOK = 3
        p_st = 0
        n_it = len(its)
        pending_tail = None
        for idx in range(n_it):
            while p_st < n_it and p_st <= idx + LOOK:
                emit_st(p_st)
                p_st += 1
            h, Ig, j, nj = its[idx]
            g = h * NT + Ig
            if j == 0 and Ig == 0 and h + 1 < 12:
                load(h + 1)
            emit_exp_pv(idx)
            if pending_tail is not None and j == min(6, nj - 1):
                tail(pending_tail)
                pending_tail = None
            if j == nj - 1:
                if pending_tail is not None:
                    tail(pending_tail)
                tail_a(g)
                pending_tail = g
        if pending_tail is not None:
            tail(pending_tail)

    def sweep_wout1(self, after_weights=None):
        ar, I, S, NT = self.ar, self.I, self.S, self.NT
        woT = ar.alloc([6, D], BF16)
        woM = ar.alloc([4, D], BF16, parts=64)
        for q in range(6):
            self.load_w(woT[:, q, :], I["w_out"][1, 128 * q:128 * q + 128, :], ("woT", q))
        for h in range(4):
            self.load_w(woM[:, h, :], I["w_out"][1, 768 + 64 * h:768 + 64 * h + 64, :], ("woM", h))
        if after_weights is not None:
            after_weights()
        ot = [ar.alloc([6, 512], BF16) for _ in range(2)]
        mo = [ar.alloc([4, 512], BF16, parts=64) for _ in range(2)]
        ht = [ar.alloc([4, D], F32) for _ in range(2)]

        def load(t):
            s = t % 2
            self.dma("sp", ot[s], self.oT_d[:, 512 * t:512 * t + 512].rearrange("(q p) t -> p q t", p=128), reads=[("dram", "oT")], writes=[("ot", s)])
            self.dma("sp", mo[s], self.memo_d[1][:, :, 512 * t:512 * t + 512].rearrange("h p t -> p h t"), reads=[("dram", "memo1")], writes=[("mo", s)])
            for i in range(4):
                self.dma("sp", ht[s][:, i, :], self.hB[512 * t + 128 * i:512 * t + 128 * i + 128, :], reads=[("dram", "hB", t), ("dram", "hA", t)], writes=[("ht", s, i)])

        load(0)
        for t in range(NT):
            if t + 1 < NT:
                load(t + 1)
            s = t % 2
            self.wout_residual(ht[s], [("ht", s, i) for i in range(4)],
                               [(ot[s][:, q, :], woT[:, q, :], [("ot", s), ("woT", q)]) for q in range(6)] +
                               [(mo[s][0:64, h, :], woM[0:64, h, :], [("mo", s), ("woM", h)]) for h in range(4)])
            for i in range(4):
                self.dma("sp", self.hA[512 * t + 128 * i:512 * t + 128 * i + 128, :], ht[s][:, i, :], reads=[("ht", s, i)], writes=[("dram", "hA", t)])


_CACHE = {}


def _get_nc(S, debug=False):
    key = (S, debug)
    if key not in _CACHE:
        _CACHE[key] = Builder(S, debug).build()
    return _CACHE[key]


def kernel(**inputs):
    x = np.asarray(inputs["x"], np.float32)
    mem = np.asarray(inputs["mem"], np.float32)
    B, S, _ = x.shape
    nc = _get_nc(S)
    consts = host_consts()
    shared = {k: np.ascontiguousarray(np.asarray(v, np.float32)) for k, v in inputs.items() if k not in ("x", "mem")}
    in_maps = []
    for b in range(B):
        m = dict(shared)
        m.update(consts)
        m["x"] = np.ascontiguousarray(x[b])
        m["mem"] = np.ascontiguousarray(mem[b])
        in_maps.append(m)
    res = run_bass_kernel_spmd(nc, in_maps, core_ids=list(range(B)))
    return np.stack([np.asarray(r["y"], np.float32) for r in res.results], axis=0)
```
